# Optimizing a Trainium2 kernel written in Bass

```python
import math
import jax
import jax.numpy as jnp
from jax import lax
import numpy as np

D_MODEL = 1024
BATCH = 16
SEQ = 2048
DEPTH = 2

N_MIXERS = 4
GROUP_W = D_MODEL // N_MIXERS
HEAD_V = 64
N_HEADS = GROUP_W // HEAD_V
D_MIX = N_MIXERS * GROUP_W
GDN_CONV = 4
RWKV_DECAY_RANK = 64
RWKV_A_RANK = 64
RWKV_GN_EPS = 64e-5
SC_CONV = 3
GLA_HEAD_K = HEAD_V // 2
GLA_K = N_HEADS * GLA_HEAD_K
GLA_RANK = 16
GLA_TAU = 16.0
CHUNK = 64
EPS = 1e-6

GDN_COLS = 4 * GROUP_W + 2 * N_HEADS
RWKV_COLS = 4 * GROUP_W + RWKV_DECAY_RANK + RWKV_A_RANK
SC_COLS = 4 * GROUP_W
GLA_COLS = 2 * GLA_K + 2 * GROUP_W + GLA_RANK
D_IN = GDN_COLS + RWKV_COLS + SC_COLS + GLA_COLS

kernel_name = 'hybrid_parallel_heads_gdn_rwkv7_shortconv_gla'


def _split(x, sizes):
    idx = [int(i) for i in np.cumsum(sizes)[:-1]]
    return jnp.split(x, idx, axis=-1)


def _rmsnorm(x, w):
    xf = x.astype(jnp.float32)
    y = xf * lax.rsqrt(jnp.mean(xf * xf, axis=-1, keepdims=True) + EPS)
    return (y * w).astype(x.dtype)


def _l2norm(x):
    return x * lax.rsqrt(jnp.sum(x * x, axis=-1, keepdims=True) + EPS)


def _causal_dwconv(x, w):
    kw = w.shape[0]
    t = x.shape[1]
    xp = jnp.pad(x, ((0, 0), (kw - 1, 0), (0, 0)))
    return sum(xp[:, i:i + t] * w[i] for i in range(kw))


def _token_shift(x):
    return jnp.pad(x, ((0, 0), (1, 0), (0, 0)))[:, :-1]


def _heads(x, d):
    b, t, c = x.shape
    return x.reshape(b, t, c // d, d).transpose(0, 2, 1, 3)


def _merge(x):
    b, h, t, d = x.shape
    return x.transpose(0, 2, 1, 3).reshape(b, t, h * d)


def _chunks(x):
    b, h, t = x.shape[:3]
    return x.reshape((b, h, t // CHUNK, CHUNK) + x.shape[3:])


def _gated_deltanet(p, conv_w, a_log, dt_bias, norm_w):
    dtype = p.dtype
    p = p.astype(jnp.float32)
    qkv, z, a_raw, b_raw = _split(p, [3 * GROUP_W, GROUP_W, N_HEADS, N_HEADS])
    qkv = jax.nn.silu(_causal_dwconv(qkv, conv_w))
    q, k, v = jnp.split(qkv, 3, axis=-1)
    q = _chunks(_l2norm(_heads(q, HEAD_V)) * HEAD_V ** -0.5)
    k = _chunks(_l2norm(_heads(k, HEAD_V)))
    v = _chunks(_heads(v, HEAD_V))
    g = -jnp.exp(a_log) * jax.nn.softplus(a_raw + dt_bias)
    g = jnp.cumsum(_chunks(g.transpose(0, 2, 1)), axis=-1)
    beta = _chunks(jax.nn.sigmoid(b_raw).transpose(0, 2, 1))
    incl = jnp.tril(jnp.ones((CHUNK, CHUNK), bool))
    strict = jnp.tril(jnp.ones((CHUNK, CHUNK), bool), -1)
    diff = g[..., :, None] - g[..., None, :]
    decay = jnp.where(incl, jnp.exp(jnp.where(incl, diff, 0.0)), 0.0)
    kb = k * beta[..., None]
    a_mat = jnp.where(strict, jnp.einsum('bhnid,bhnjd->bhnij', kb, k) * decay, 0.0)
    t_mat = a_mat + jnp.eye(CHUNK, dtype=a_mat.dtype)
    u = lax.linalg.triangular_solve(t_mat, v * beta[..., None], left_side=True, lower=True, unit_diagonal=True)
    w = lax.linalg.triangular_solve(t_mat, kb * jnp.exp(g)[..., None], left_side=True, lower=True, unit_diagonal=True)
    attn = jnp.einsum('bhnid,bhnjd->bhnij', q, k) * decay
    g_last = g[..., -1]
    k_end = k * jnp.exp(g_last[..., None] - g)[..., None]
    q_g = q * jnp.exp(g)[..., None]

    def step(s, inp):
        q_c, k_c, u_c, w_c, attn_c, gl_c = inp
        v_new = u_c - jnp.einsum('bhck,bhkv->bhcv', w_c, s)
        o = jnp.einsum('bhck,bhkv->bhcv', q_c, s) + jnp.einsum('bhij,bhjv->bhiv', attn_c, v_new)
        s = s * jnp.exp(gl_c)[..., None, None] + jnp.einsum('bhck,bhcv->bhkv', k_c, v_new)
        return s, o

    xs = tuple(jnp.moveaxis(t, 2, 0) for t in (q_g, k_end, u, w, attn, g_last))
    s0 = jnp.zeros(q.shape[:2] + (HEAD_V, HEAD_V), jnp.float32)
    _, o = lax.scan(step, s0, xs)
    o = jnp.moveaxis(o, 0, 2)
    o = o.reshape(o.shape[0], o.shape[1], -1, HEAD_V)
    out = _merge(_rmsnorm(o, norm_w)) * jax.nn.silu(z)
    return out.astype(dtype)


def _rwkv7(p, mu, w0, w_up, a0, a_up, k_k, k_a, r_k, ln_w, ln_b):
    dtype = p.dtype
    p = p.astype(jnp.float32)
    p = p + mu * (_token_shift(p) - p)
    r, k, v, z, w_down, a_down = _split(p, [GROUP_W] * 4 + [RWKV_DECAY_RANK, RWKV_A_RANK])
    decay = jnp.exp(-math.exp(-0.5) * jax.nn.sigmoid(w0 + jnp.tanh(w_down) @ w_up))
    a = jax.nn.sigmoid(a0 + a_down @ a_up)
    b_, t_ = p.shape[:2]

    def bthd(x):
        return x.reshape(b_, t_, N_HEADS, HEAD_V)

    kk = _l2norm(bthd(k * k_k))
    k = k * (1.0 + (a - 1.0) * k_a)

    def step(s, inp):
        r_t, w_t, k_t, kk_t, a_t, v_t = inp
        sa = jnp.einsum('bhvk,bhk->bhv', s, -kk_t)
        s = (s * w_t[:, :, None, :] + sa[..., None] * (kk_t * a_t)[:, :, None, :]
             + v_t[..., None] * k_t[:, :, None, :])
        return s, jnp.einsum('bhvk,bhk->bhv', s, r_t)

    xs = tuple(jnp.moveaxis(t, 1, 0) for t in (bthd(r), bthd(decay), bthd(k), kk, bthd(a), bthd(v)))
    s0 = jnp.zeros((b_, N_HEADS, HEAD_V, HEAD_V), jnp.float32)
    _, y = lax.scan(step, s0, xs)
    y = jnp.moveaxis(y, 0, 1)
    mean = jnp.mean(y, axis=-1, keepdims=True)
    var = jnp.mean(jnp.square(y - mean), axis=-1, keepdims=True)
    yn = ((y - mean) * lax.rsqrt(var + RWKV_GN_EPS)).reshape(b_, t_, GROUP_W) * ln_w + ln_b
    bonus = (jnp.sum(bthd(r * k * r_k), axis=-1, keepdims=True) * bthd(v)).reshape(b_, t_, GROUP_W)
    out = (yn + bonus) * jax.nn.silu(z)
    return out.astype(dtype)


def _short_conv(p, conv_w):
    bg, cg, xv, z = _split(p, [GROUP_W] * 4)
    return bg * _causal_dwconv(cg * xv, conv_w) * jax.nn.silu(z)


def _gla(p, a_up, a_bias, norm_w):
    dtype = p.dtype
    p = p.astype(jnp.float32)
    q, k, v, z, a_down = _split(p, [GLA_K, GLA_K, GROUP_W, GROUP_W, GLA_RANK])
    log_a = jax.nn.log_sigmoid(a_down @ a_up + a_bias) / GLA_TAU
    q = _chunks(_heads(q, GLA_HEAD_K)) * GLA_HEAD_K ** -0.5
    k = _chunks(_heads(k, GLA_HEAD_K))
    v = _chunks(_heads(v, HEAD_V))
    bcum = jnp.cumsum(_chunks(_heads(log_a, GLA_HEAD_K)), axis=-2)
    b_last = bcum[..., -1:, :]
    q_e = q * jnp.exp(bcum)
    k_e = k * jnp.exp(-bcum)
    k_end = k * jnp.exp(b_last - bcum)
    incl = jnp.tril(jnp.ones((CHUNK, CHUNK), bool))
    attn = jnp.where(incl, jnp.einsum('bhnik,bhnjk->bhnij', q_e, k_e), 0.0)
    intra = jnp.einsum('bhnij,bhnjv->bhniv', attn, v)

    def step(s, inp):
        qe_c, ke_c, v_c, bl_c, intra_c = inp
        o = jnp.einsum('bhck,bhkv->bhcv', qe_c, s) + intra_c
        s = s * jnp.exp(bl_c)[..., 0, :, None] + jnp.einsum('bhck,bhcv->bhkv', ke_c, v_c)
        return s, o

    xs = tuple(jnp.moveaxis(t, 2, 0) for t in (q_e, k_end, v, b_last, intra))
    s0 = jnp.zeros(q.shape[:2] + (GLA_HEAD_K, HEAD_V), jnp.float32)
    _, o = lax.scan(step, s0, xs)
    o = jnp.moveaxis(o, 0, 2)
    o = o.reshape(o.shape[0], o.shape[1], -1, HEAD_V)
    out = _merge(_rmsnorm(o, norm_w)) * jax.nn.silu(z)
    return out.astype(dtype)


def setup_inputs(seed: int = 0) -> dict:
    key = jax.random.key(seed)
    ks = jax.random.split(key, 24)
    f32 = jnp.float32
    L = DEPTH

    def nrm(k, shape, s):
        return s * jax.random.normal(k, shape, f32)

    dt = jnp.exp(jax.random.uniform(ks[5], (L, N_HEADS), f32, math.log(1e-3), math.log(1e-1)))
    return {
        'x': nrm(ks[0], (BATCH, SEQ, D_MODEL), 1.0),
        'pre_norm_w': 1.0 + nrm(ks[1], (L, D_MODEL), 0.02),
        'w_in': nrm(ks[2], (L, D_MODEL, D_IN), D_MODEL ** -0.5),
        'gdn_conv_w': nrm(ks[3], (L, GDN_CONV, 3 * GROUP_W), GDN_CONV ** -0.5),
        'gdn_a_log': jnp.log(jax.random.uniform(ks[4], (L, N_HEADS), f32, 1.0, 16.0)),
        'gdn_dt_bias': dt + jnp.log(-jnp.expm1(-dt)),
        'gdn_norm_w': 1.0 + nrm(ks[6], (L, HEAD_V), 0.02),
        'rwkv_mu': jax.random.uniform(ks[7], (L, RWKV_COLS), f32),
        'rwkv_w0': jax.random.uniform(ks[8], (L, GROUP_W), f32, -2.0, 2.0),
        'rwkv_w_up': nrm(ks[9], (L, RWKV_DECAY_RANK, GROUP_W), 0.5 * RWKV_DECAY_RANK ** -0.5),
        'rwkv_a0': nrm(ks[10], (L, GROUP_W), 0.1),
        'rwkv_a_up': nrm(ks[11], (L, RWKV_A_RANK, GROUP_W), 0.5 * RWKV_A_RANK ** -0.5),
        'rwkv_k_k': 0.85 + nrm(ks[12], (L, GROUP_W), 0.05),
        'rwkv_k_a': 1.0 + nrm(ks[13], (L, GROUP_W), 0.05),
        'rwkv_r_k': nrm(ks[14], (L, GROUP_W), 0.1),
        'rwkv_ln_w': 1.0 + nrm(ks[15], (L, GROUP_W), 0.02),
        'rwkv_ln_b': nrm(ks[16], (L, GROUP_W), 0.01),
        'sc_conv_w': nrm(ks[17], (L, SC_CONV, GROUP_W), SC_CONV ** -0.5),
        'gla_a_up': nrm(ks[18], (L, GLA_RANK, GLA_K), GLA_RANK ** -0.5),
        'gla_a_bias': 2.0 + nrm(ks[19], (L, GLA_K), 0.5),
        'gla_norm_w': 1.0 + nrm(ks[20], (L, HEAD_V), 0.02),
        'w_out': nrm(ks[21], (L, D_MIX, D_MODEL), D_MIX ** -0.5),
        'post_norm_w': 1.0 + nrm(ks[22], (L, D_MODEL), 0.02),
    }


def reference(x, pre_norm_w, w_in, gdn_conv_w, gdn_a_log, gdn_dt_bias, gdn_norm_w,
              rwkv_mu, rwkv_w0, rwkv_w_up, rwkv_a0, rwkv_a_up, rwkv_k_k, rwkv_k_a, rwkv_r_k,
              rwkv_ln_w, rwkv_ln_b, sc_conv_w, gla_a_up, gla_a_bias, gla_norm_w, w_out, post_norm_w):
    for l in range(DEPTH):
        h = _rmsnorm(x, pre_norm_w[l])
        proj = jnp.einsum('btd,de->bte', h, w_in[l])
        p_gdn, p_rwkv, p_sc, p_gla = _split(proj, [GDN_COLS, RWKV_COLS, SC_COLS, GLA_COLS])
        y_gdn = _gated_deltanet(p_gdn, gdn_conv_w[l], gdn_a_log[l], gdn_dt_bias[l], gdn_norm_w[l])
        y_rwkv = _rwkv7(p_rwkv, rwkv_mu[l], rwkv_w0[l], rwkv_w_up[l], rwkv_a0[l], rwkv_a_up[l],
                        rwkv_k_k[l], rwkv_k_a[l], rwkv_r_k[l], rwkv_ln_w[l], rwkv_ln_b[l])
        y_sc = _short_conv(p_sc, sc_conv_w[l])
        y_gla = _gla(p_gla, gla_a_up[l], gla_a_bias[l], gla_norm_w[l])
        y = jnp.concatenate([y_gdn, y_rwkv, y_sc, y_gla], axis=-1)
        out = jnp.einsum('bte,ed->btd', y, w_out[l])
        x = x + _rmsnorm(out, post_norm_w[l])
    return x
```

```python
import contextlib
import math
import os
DBG = float(os.environ.get('KDBG', '99'))
GOFF = float(os.environ.get("KGOFF", "0"))
LAT = float(os.environ.get("KLAT", "0.25"))
KPB = int(os.environ.get("KPB", "3"))
EVR = int(os.environ.get("KEVR", "4"))
PRI = float(os.environ.get("KPRI", "0"))
import numpy as np
import concourse.bass as bass
import concourse.mybir as mybir
from concourse.bass_utils import run_bass_kernel_spmd

F32 = mybir.dt.float32
BF16 = mybir.dt.bfloat16
ALU = mybir.AluOpType
AF = mybir.ActivationFunctionType
AX = mybir.AxisListType

D = 1024
NCORES = 8
MT = 256
EPS = 1e-6
NEG = -30000.0


class Trk:
    __slots__ = ("w", "r", "tw", "tr")

    def __init__(self):
        self.w = None
        self.r = []
        self.tw = 0.0
        self.tr = 0.0


class V:
    __slots__ = ("ap", "trk")

    def __init__(self, ap, trk):
        self.ap = ap
        self.trk = trk


class Tl:
    def __init__(self, handle):
        self.h = handle
        self.trk = Trk()

    def __getitem__(self, key):
        return V(self.h[key], [self.trk])

    def v(self, ap):
        return V(ap, [self.trk])


class Eng:
    def __init__(self, name, obj, sem):
        self.name = name
        self.obj = obj
        self.sem = sem
        self.cnt = 0
        self.waited = {}


class KB:
    def __init__(self, nc, ctx):
        self.nc = nc
        self.ctx = ctx
        self.engs = {}
        for nm, obj in (("pe", nc.tensor), ("v", nc.vector), ("s", nc.scalar), ("g", nc.gpsimd)):
            sem = ctx.enter_context(nc.semaphore("sem_" + nm))
            self.engs[nm] = Eng(nm, obj, sem)
        self.dmaq = {"sp": nc.sync, "gq": nc.gpsimd}
        self.dq_eng = {"sp": Eng("spq", nc.sync, None), "gq": self.engs["g"]}
        self.dsems = {}
        for q in ("sp", "gq"):
            lst = []
            for i in range(8):
                sem = ctx.enter_context(nc.semaphore("dsem_%s%d" % (q, i)))
                e = Eng("d_%s%d" % (q, i), None, sem)
                self.engs[e.name] = e
                lst.append(e)
            self.dsems[q] = lst
        self.drr = {"sp": 0, "gq": 0}
        self.flip = 0
        self.efree = {}
        self.step_max = 0.0

    def sb(self, name, shape, dtype=F32):
        return Tl(self.ctx.enter_context(self.nc.sbuf_tensor("sb_" + name, list(shape), dtype)))

    def ps(self, name, shape, dtype=F32):
        return Tl(self.ctx.enter_context(self.nc.psum_tensor("ps_" + name, list(shape), dtype)))

    def _deps(self, issuer, reads, writes):
        deps = {}
        me = issuer.name

        def add(d, raw):
            if d is None:
                return
            e, c = d
            if e == me and (me == "pe" or not raw):
                return
            if deps.get(e, 0) < c:
                deps[e] = c
        for v in reads:
            for t in v.trk:
                add(t.w, True)
        for v in writes:
            for t in v.trk:
                add(t.w, False)
                for r in t.r:
                    add(r, False)
        for e, c in deps.items():
            if issuer.waited.get(e, 0) < c:
                issuer.obj.wait_ge(self.engs[e].sem, c)
                issuer.waited[e] = c

    def _mark(self, ident, reads, writes):
        for v in reads:
            for t in v.trk:
                t.r.append(ident)
        for v in writes:
            for t in v.trk:
                t.w = ident
                t.r = []

    def _est(self, eng, cost, reads, writes):
        dep = 0.0
        for v in reads:
            for t in v.trk:
                dep = max(dep, t.tw)
        for v in writes:
            for t in v.trk:
                dep = max(dep, t.tw, t.tr)
        start = max(self.efree.get(eng, 0.0), dep + LAT)
        fin = start + cost
        self.efree[eng] = fin
        for v in reads:
            for t in v.trk:
                t.tr = max(t.tr, fin)
        for v in writes:
            for t in v.trk:
                t.tw = fin
        self.step_max = max(self.step_max, fin)

    @staticmethod
    def _fsize(v):
        try:
            return float(v.ap.free_size())
        except Exception:
            return 256.0

    def op(self, eng, fn, reads, writes, cost=None):
        e = self.engs[eng]
        if cost is None:
            n = self._fsize(writes[0]) if writes else 256.0
            cost = {"pe": 0.07 + n / 1200.0, "v": 0.12 + n / 960.0, "s": 0.2 + n / 1200.0, "g": 0.15 + n / 500.0}[eng]
        self._est(eng, cost, reads, writes)
        self._deps(e, reads, writes)
        ins = fn(e.obj)
        e.cnt += 1
        ins.then_inc(e.sem, 1)
        self._mark((e.name, e.cnt), reads, writes)
        return ins

    def dma(self, q, out, in_):
        issuer = self.dq_eng[q]
        reads = [in_] if isinstance(in_, V) else []
        writes = [out] if isinstance(out, V) else []
        self._deps(issuer, reads, writes)
        self._est("q_" + q, 0.6, [], [])
        self.efree["q_" + q] -= 0.0
        self._est("dma_" + q + str(self.drr[q] % 4), 2.5, reads, writes)
        de = self.dsems[q][self.drr[q] % 8]
        self.drr[q] += 1
        oap = out.ap if isinstance(out, V) else out
        iap = in_.ap if isinstance(in_, V) else in_
        ins = self.dmaq[q].dma_start(out=oap, in_=iap)
        de.cnt += 16
        ins.then_inc(de.sem, 16)
        self._mark((de.name, de.cnt), reads, writes)

    def wait_all(self, eng):
        e = self.engs[eng]
        for nm, o in self.engs.items():
            if o.cnt > 0 and e.waited.get(nm, 0) < o.cnt and nm != e.name:
                e.obj.wait_ge(o.sem, o.cnt)
                e.waited[nm] = o.cnt

    def _pe_rowkey(self, ap, out):
        key = (ap.base_partition(), ap.partition_size())
        e = self.engs["pe"]
        if not hasattr(self, "_bank_last"):
            self._bank_last = {}
        bid = id(out.trk[0])
        last = self._bank_last.get(bid)
        if last is not None and last[0] != key and e.waited.get("pe", 0) < last[1]:
            e.obj.wait_ge(e.sem, last[1])
            e.waited["pe"] = last[1]
        self._bank_last[bid] = (key, e.cnt + 1)

    def mm(self, out, lhsT, rhs, start=True, stop=True):
        rd = [lhsT, rhs] + ([] if start else [out])
        self._pe_rowkey(lhsT.ap, out)
        return self.op("pe", lambda o: o.matmul(out.ap, lhsT.ap, rhs.ap, start=start, stop=stop), rd, [out],
                       cost=0.07 + self._fsize(rhs) / 1200.0)

    def tr(self, out, in_, ident):
        if in_.ap.dtype == BF16:
            k = in_.ap.partition_size()
            return self.mm(out, in_, self.ident_bf[0:k, 0:k])
        self._pe_rowkey(in_.ap, out)
        return self.op("pe", lambda o: o.transpose(out.ap, in_.ap, ident.ap), [in_, ident], [out])

    def act(self, out, in_, func, bias=None, scale=1.0, accum=None):
        rd = [in_]
        wr = [out]
        kw = {}
        if isinstance(bias, V):
            rd.append(bias)
            kw["bias"] = bias.ap
        elif bias is not None:
            kw["bias"] = bias
        kw["scale"] = scale
        if accum is not None:
            kw["accum_out"] = accum.ap
            wr.append(accum)
        return self.op("s", lambda o: o.activation(out.ap, in_.ap, func, **kw), rd, wr)

    def copy(self, eng, out, in_):
        if eng == "s":
            return self.op("s", lambda o: o.copy(out.ap, in_.ap), [in_], [out])
        return self.op(eng, lambda o: o.tensor_copy(out.ap, in_.ap), [in_], [out])

    def evac(self, out, in_):
        self.flip = (self.flip + 1) % EVR
        return self.copy("s" if self.flip else "v", out, in_)

    def tt(self, out, a, b, op, eng="v"):
        return self.op(eng, lambda o: o.tensor_tensor(out.ap, a.ap, b.ap, op), [a, b], [out])

    def ts(self, out, a, s1, op0, s2=None, op1=None, eng="v"):
        rd = [a]
        x1, x2 = s1, s2
        if isinstance(s1, V):
            rd.append(s1)
            x1 = s1.ap
        if isinstance(s2, V):
            rd.append(s2)
            x2 = s2.ap
        if op1 is None:
            return self.op(eng, lambda o: o.tensor_single_scalar(out.ap, a.ap, x1, op0), rd, [out])
        return self.op(eng, lambda o: o.tensor_scalar(out.ap, a.ap, x1, x2, op0, op1), rd, [out])

    def stt(self, out, in0, scalar, in1, op0, op1, eng="v"):
        rd = [in0, in1]
        sc = scalar
        if isinstance(scalar, V):
            rd.append(scalar)
            sc = scalar.ap
        return self.op(eng, lambda o: o.scalar_tensor_tensor(out.ap, in0.ap, sc, in1.ap, op0, op1), rd, [out])

    def red(self, out, in_, eng="v"):
        return self.op(eng, lambda o: o.tensor_reduce(out.ap, in_.ap, AX.X, ALU.add), [in_], [out])

    def recip(self, out, in_):
        return self.op("v", lambda o: o.reciprocal(out.ap, in_.ap), [in_], [out])

    def memset(self, out, val, eng="v"):
        return self.op(eng, lambda o: o.memset(out.ap, val), [], [out])


def make_consts():
    i = np.arange(128)
    same = (i[:, None] // 64) == (i[None, :] // 64)
    c = {}
    c["ident"] = np.eye(128)
    c["ones"] = np.ones((128, 128))
    c["mcum"] = (same & (i[:, None] <= i[None, :])) * 1.0
    c["mcumx"] = (same & (i[:, None] < i[None, :])) * 1.0
    c["mrest"] = (same & (i[:, None] > i[None, :])) * 1.0
    c["m_strict"] = (same & (i[:, None] < i[None, :])) * 1.0
    c["m_incl"] = (same & (i[:, None] <= i[None, :])) * 1.0
    c["m_incl_neg"] = -c["m_incl"]
    c["nm_strict"] = np.where(c["m_strict"] > 0, 0.0, NEG)
    c["nm_incl"] = np.where(c["m_incl"] > 0, 0.0, NEG)
    cs = np.zeros((128, 128))
    cs[:64, 0] = 1.0
    cs[64:, 1] = 1.0
    c["chunksel"] = cs
    p = np.arange(128)
    bq = np.zeros((128, 4, 128))
    for h in range(4):
        bq[32 * h:32 * h + 32, h, :] = 1.0
    bs = np.zeros((128, 256))
    for h in range(4):
        bs[32 * h:32 * h + 32, 64 * h:64 * h + 64] = 1.0
    names = ["ident", "ones", "mcum", "mcumx", "mrest", "m_strict", "m_incl", "m_incl_neg", "nm_strict",
             "nm_incl", "chunksel"]
    arr = np.concatenate([c[n] for n in names] + [bq.reshape(128, 512), bs], axis=1).astype(np.float32)
    offs = {n: k * 128 for k, n in enumerate(names)}
    offs["bq"] = len(names) * 128
    offs["bs"] = len(names) * 128 + 512
    return arr, offs


CONSTS, COFF = make_consts()
NCONST = CONSTS.shape[1]

GW = 256
GDN0 = 0
RW0 = 4 * GW + 8
SC0 = RW0 + 4 * GW + 128
GLA0 = SC0 + 4 * GW


def block_cols():
    blocks = []
    for b in range(8):
        blocks.append(np.arange(GDN0 + b * 128, GDN0 + (b + 1) * 128))
    for b in range(9):
        blocks.append(np.arange(RW0 + b * 128, RW0 + (b + 1) * 128))
    for b in range(8):
        blocks.append(np.arange(SC0 + b * 128, SC0 + (b + 1) * 128))
    for b in range(6):
        blocks.append(np.arange(GLA0 + b * 128, GLA0 + (b + 1) * 128))
    last = np.full(128, -1)
    last[:16] = np.arange(GLA0 + 768, GLA0 + 784)
    blocks.append(last)
    return blocks


BLOCKS = block_cols()
NBLK = len(BLOCKS)
MIX_BLK = {"gdn": (0, 8), "rwkv": (8, 9), "sc": (17, 8), "gla": (25, 7)}

BC = {}
_o = 0
for _n, _w in (("gdn_norm_w", 64), ("gla_norm_w", 64), ("k_k", 256), ("k_a", 256), ("r_k", 256), ("ln_w", 256),
               ("ln_b", 256), ("w0", 256), ("a0", 256), ("gla_bias", 128), ("post_w", 1024), ("a_log", 4),
               ("dt_bias", 4)):
    BC[_n] = (_o, _w)
    _o += _w
NBC = _o


def prep_layer_inputs(inp, L):
    out = {}
    w_in = inp["w_in"]
    wb = np.zeros((L, NBLK, 128, 8, 128), np.float32)
    for bi, cols in enumerate(BLOCKS):
        valid = cols >= 0
        sub = w_in[:, :, cols[valid]]
        sub = sub.reshape(L, 8, 128, -1).transpose(0, 2, 1, 3)
        wb[:, bi, :, :, :sub.shape[-1]] = sub
    out["wblk"] = wb.reshape(L * NBLK * 128, 8 * 128)
    ab = w_in[:, :, GDN0 + 1024:GDN0 + 1032].reshape(L, 8, 128, 8).transpose(0, 2, 1, 3)
    out["wab"] = np.ascontiguousarray(ab).reshape(L * 128, 64)
    wo = inp["w_out"].reshape(L, 8, 128, 1024).transpose(0, 2, 1, 3)
    out["wout"] = np.ascontiguousarray(wo).reshape(L * 128, 8 * 1024)
    bc = np.zeros((L, NBC), np.float32)

    def put(n, a):
        o, w = BC[n]
        bc[:, o:o + w] = a
    put("gdn_norm_w", inp["gdn_norm_w"]); put("gla_norm_w", inp["gla_norm_w"])
    put("k_k", inp["rwkv_k_k"]); put("k_a", inp["rwkv_k_a"]); put("r_k", inp["rwkv_r_k"])
    put("ln_w", inp["rwkv_ln_w"]); put("ln_b", inp["rwkv_ln_b"]); put("w0", inp["rwkv_w0"]); put("a0", inp["rwkv_a0"])
    put("gla_bias", inp["gla_a_bias"]); put("post_w", inp["post_norm_w"])
    put("a_log", inp["gdn_a_log"]); put("dt_bias", inp["gdn_dt_bias"])
    out["bcp"] = bc
    pp = np.zeros((L, 128, 8 + 24 + 9 + 6), np.float32)
    pp[:, :, 0:8] = inp["pre_norm_w"].reshape(L, 8, 128).transpose(0, 2, 1)
    g = inp["gdn_conv_w"].reshape(L, 4, 6, 128).transpose(0, 3, 2, 1)
    pp[:, :, 8:32] = g.reshape(L, 128, 24)
    pp[:, :, 32:41] = inp["rwkv_mu"].reshape(L, 9, 128).transpose(0, 2, 1)
    s = inp["sc_conv_w"].reshape(L, 3, 2, 128).transpose(0, 3, 2, 1)
    pp[:, :, 41:47] = s.reshape(L, 128, 6)
    out["ppar"] = pp.reshape(L * 128, 47)
    up = np.zeros((L, 128, 256), np.float32)
    up[:, 0:64] = inp["rwkv_w_up"]
    up[:, 64:128] = inp["rwkv_a_up"]
    out["wup"] = up.reshape(L * 128, 256)
    gu = np.zeros((L, 128, 128), np.float32)
    gu[:, 0:16] = inp["gla_a_up"]
    out["gup"] = gu.reshape(L * 128, 128)
    return out


def build(NB, T, L, mixers=("gdn", "rwkv", "sc", "gla")):
    nc = bass.Bass("TRN2", target_bir_lowering=False)
    ntok = NB * T
    nmt = T // MT

    def din(name, shape):
        return nc.dram_tensor(name, list(shape), F32, kind="ExternalInput").ap()
    x_d = din("x", [ntok, D])
    wblk_d = din("wblk", [L * NBLK * 128, 1024])
    wab_d = din("wab", [L * 128, 64])
    wout_d = din("wout", [L * 128, 8192])
    bcp_d = din("bcp", [L, NBC])
    ppar_d = din("ppar", [L * 128, 47])
    wup_d = din("wup", [L * 128, 256])
    gup_d = din("gup", [L * 128, 128])
    cst_d = din("consts", [128, NCONST])
    out_d = nc.dram_tensor("out", [ntok, D], F32, kind="ExternalOutput").ap()
    mid_d = nc.dram_tensor("xmid", [ntok, D], F32).ap() if L > 1 else None

    with contextlib.ExitStack() as ctx:
        ctx.enter_context(nc.allow_low_precision("bf16 projection operands, fp32 accumulation"))
        kb = KB(nc, ctx)
        sb, ps = kb.sb, kb.ps
        cst = sb("cst", [128, NCONST])
        kb.dma("sp", cst[:], cst_d[:, :])

        def C(n, w=128):
            o = COFF[n]
            return cst[:, o:o + w]

        def Cb4(n):
            o = COFF[n]
            return cst.v(cst.h[:, o:o + 128].unsqueeze(1).to_broadcast([128, 4, 128]))
        ident = C("ident")
        ident_bf = sb("ident_bf", [128, 128], BF16)
        kb.copy("v", ident_bf[:, :], ident)
        kb.ident_bf = ident_bf

        bcp = sb("bcp", [128, NBC]); ppar = sb("ppar", [128, 47]); wab = sb("wab", [128, 64], BF16)
        wout = sb("wout", [128, 8192], BF16); wup = sb("wup", [128, 256], BF16); gup = sb("gup", [128, 128])
        nal = sb("nal", [128, 4])
        NRING = 8
        wbf = [sb("wbf%d" % i, [128, 1024], BF16) for i in range(NRING)]
        xts = [sb("xt%d" % i, [128, 1024]) for i in range(2)]
        ht = sb("ht", [128, 1024], BF16)
        hT = sb("hT", [128, 8 * MT], BF16)
        yT = sb("yT", [128, 8 * MT], BF16)
        P_s = sb("P_s", [128, 8 * (MT + 4)]); xres = sb("xres", [128, 1024])
        P_a = sb("P_a", [128, 8 * (MT + 4)]); P_g = sb("P_g", [128, 8 * (MT + 4)]); P_r = sb("P_r", [128, 9 * (MT + 4)])
        Q_g = sb("Q_g", [128, 6 * MT], BF16); Q_r = sb("Q_r", [128, 9 * MT], BF16)
        ZS_a = sb("ZS_a", [128, 2 * MT]); ZS_g = sb("ZS_g", [128, 2 * MT]); ZS_r = sb("ZS_r", [128, 2 * MT])
        st = {m: sb("st_" + m, [128, 256]) for m in ("gdn", "rwkv", "gla")}
        stb = {m: sb("stb_" + m, [128, 128], BF16) for m in ("gdn", "rwkv")}
        hist = {"gdn": sb("h_gdn", [128, 6 * 3]), "rwkv": sb("h_rwkv", [128, 9]), "sc": sb("h_sc", [128, 4])}
        banks = [ps("bk%d" % i, [128, 512]) for i in range(8)]
        bctr = [0]

        def bank():
            b = min(banks[4:8], key=lambda t: (max(t.trk.tw, t.trk.tr), id(t)))
            bctr[0] += 1
            b.trk.tr = max(b.trk.tr, kb.step_max, max(kb.efree.values()) if kb.efree else 0.0) + 1e-3
            return b
        pjctr = [0]

        def pbank():
            b = banks[pjctr[0] % 2]
            pjctr[0] += 1
            return b
        scr = {}

        ALIAS = {"sc_acc": "gl_qblk", "sc_zs": "gl_ex", "rw_d": "rw_eg"}

        def S(name, w=512, dt=F32):
            name = ALIAS.get(name, name)
            if name not in scr:
                scr[name] = sb("s_" + name, [128, w], dt)
            return scr[name]

        def BCv(n, rows=128):
            o, w = BC[n]
            return bcp[0:rows, o:o + w]

        def r3(tile, h, a=0, b=None):
            b = b if b is not None else tile.h.shape[1]
            return tile.v(tile.h[:, a:b].rearrange("p (h t) -> p h t", h=h))

        def bc_t(tile, a, nh, n):
            return tile.v(tile.h[:, a:a + nh].unsqueeze(2).to_broadcast([128, nh, n]))

        def bc_h(tile, a, w, nh):
            return tile.v(tile.h[:, a:a + w].unsqueeze(1).to_broadcast([128, nh, w]))

        def Uv(tile, rows=slice(0, 128)):
            return tile.v(tile.h[rows, 0:512].rearrange("p (h t) -> p h t", h=4)[:, :, 0:64])

        def Wv(tile, rows=slice(0, 128)):
            return tile.v(tile.h[rows, 0:512].rearrange("p (h t) -> p h t", h=4)[:, :, 64:128])

        def r3r(tile, rows, h, a, b):
            return tile.v(tile.h[rows, a:b].rearrange("p (h t) -> p h t", h=h))
        HO = (0, 2, 1, 3)

        def rsqrt_small(out, in_, scale, eps):
            kb.act(out, in_, AF.Ln, bias=eps, scale=scale)
            kb.act(out, out, AF.Exp, scale=-0.5)

        def solve(Nm, UW, pfx):
            Am = S(pfx + "sol_A0", 512, BF16)
            pb = bank()
            for h in range(4):
                kb.tr(pb[:, h * 128:(h + 1) * 128], Nm[:, h * 128:(h + 1) * 128], ident)
            kb.evac(Am[:, :], pb[:, :])
            yield
            curN, curA = Nm, Am
            for lvl in range(6):
                pb = bank()
                for h in range(4):
                    hs = slice(h * 128, (h + 1) * 128)
                    kb.mm(pb[:, hs], curN[:, hs], UW[:, hs])
                if lvl < 5:
                    nN = S(pfx + "sol_N%d" % (lvl % 2), 512, BF16)
                    pn = bank()
                    for h in range(4):
                        hs = slice(h * 128, (h + 1) * 128)
                        kb.mm(pn[:, hs], curA[:, hs], curN[:, hs])
                if lvl < 4:
                    nA = S(pfx + "sol_A%d" % ((lvl + 1) % 2), 512, BF16)
                    pa = bank()
                    for h in range(4):
                        hs = slice(h * 128, (h + 1) * 128)
                        kb.mm(pa[:, hs], curN[:, hs], curA[:, hs])
                kb.tt(UW[:, :], UW[:, :], pb[:, :], ALU.subtract if lvl == 0 else ALU.add)
                if lvl == 5:
                    break
                kb.evac(nN[:, :], pn[:, :])
                if lvl < 4:
                    kb.evac(nA[:, :], pa[:, :])
                    curA = nA
                curN = nN
                yield

        def fmh(tile, h, c0, c1, base=0):
            pb_, cb = 64 * (h % 2), h // 2
            return tile[pb_:pb_ + 64, base + cb * 128 + c0: base + cb * 128 + c1]

        def sth(Sx, h):
            pb_, cb = 64 * (h % 2), h // 2
            return Sx[pb_:pb_ + 64, cb * 64:(cb + 1) * 64]

        def seq_core(pfx, po, Sx, Sb, UW, sign, wT, wbase, qT, qbase, attnT, kend, dS, attn2T=None, kend2=None, V2=None):
            X = S(pfx + "seq_X", 256, BF16)
            for c in range(2):
                cs = slice(64 * c, 64 * c + 64)
                pw = bank()
                for h in HO:
                    kb.mm(pw[cs, h * 64:(h + 1) * 64], fmh(wT, h, 64 * c, 64 * c + 64, wbase), sth(Sb, h))
                kb.tt(r3r(X, cs, 4, 0, 256), Uv(UW, cs), r3r(pw, cs, 4, 0, 256), ALU.subtract if sign < 0 else ALU.add)
                yield
                for h in (HO if c == 0 else (1, 3, 0, 2)):
                    o_ = po[cs, h * 64:(h + 1) * 64]
                    kb.mm(o_, fmh(qT, h, 64 * c, 64 * c + 64, qbase), sth(Sb, h), start=True, stop=False)
                    kb.mm(o_, attnT[cs, h * 128 + 64 * c: h * 128 + 64 * c + 64], X[cs, h * 64:(h + 1) * 64],
                          start=False, stop=(attn2T is None))
                    if attn2T is not None:
                        kb.mm(o_, attn2T[cs, h * 128 + 64 * c: h * 128 + 64 * c + 64], V2[cs, h * 64:(h + 1) * 64],
                              start=False, stop=True)
                pst = bank()
                for h in range(4):
                    o_ = sth(pst, h)
                    kb.mm(o_, kend[cs, h * 64:(h + 1) * 64], X[cs, h * 64:(h + 1) * 64], start=True,
                          stop=(kend2 is None))
                    if kend2 is not None:
                        kb.mm(o_, kend2[cs, h * 64:(h + 1) * 64], V2[cs, h * 64:(h + 1) * 64], start=False, stop=True)
                for cb in range(2):
                    kb.stt(Sx[:, cb * 64:(cb + 1) * 64], Sx[:, cb * 64:(cb + 1) * 64], dS[:, 2 * cb + c:2 * cb + c + 1],
                           pst[:, cb * 64:(cb + 1) * 64], ALU.mult, ALU.add)
                kb.copy("s", Sb[:, 0:128], Sx[:, 0:128])
                yield

        def head_norm_gate(o_sb, normw, yblk0, tsl, ZS, pfx):
            sq = S(pfx + "hn_sq", 256)
            ss = S(pfx + "hn_ss", 8)
            kb.tt(sq[:, :], o_sb[:, 0:256], o_sb[:, 0:256], ALU.mult)
            kb.red(ss[:, 0:4], r3(sq, 4))
            yield
            rsqrt_small(ss[:, 4:8], ss[:, 0:4], 1.0 / 64, EPS)
            yield
            kb.tt(r3(sq, 4), r3(o_sb, 4, 0, 256), bc_t(ss, 4, 4, 64), ALU.mult)
            ob = S(pfx + "hn_ob", 256, BF16)
            kb.tt(r3(ob, 4), r3(sq, 4), bc_h(bcp, BC[normw][0], 64, 4), ALU.mult)
            yield
            to_fm_gate(ob, yblk0, tsl, ZS)

        def to_fm_gate(o_tm, yblk0, tsl, ZS):
            pb = bank()
            for blk in range(2):
                kb.tr(pb[:, blk * 128:(blk + 1) * 128], o_tm[:, blk * 128:(blk + 1) * 128], ident)
            for blk in range(2):
                kb.tt(yT[:, (yblk0 + blk) * MT + tsl.start:(yblk0 + blk) * MT + tsl.stop],
                      pb[:, blk * 128:(blk + 1) * 128],
                      ZS[:, blk * MT + tsl.start: blk * MT + tsl.stop], ALU.mult)

        wseq = []
        wstate = {"issued": 0}

        def wissue(upto):
            while wstate["issued"] < min(upto, len(wseq)):
                i = wstate["issued"]
                l, b = wseq[i]
                r0 = (l * NBLK + b) * 128
                kb.dma("gq", wbf[i % NRING][:], wblk_d[r0:r0 + 128, :])
                wstate["issued"] += 1
        used_blocks = []
        for m in ("sc", "gla", "gdn", "rwkv"):
            if m in mixers:
                b0, nb_ = MIX_BLK[m]
                used_blocks += list(range(b0, b0 + nb_))
        for l in range(L):
            for b in range(NB):
                for mt in range(nmt):
                    for blk in used_blocks:
                        wseq.append((l, blk))
        wptr = [0]

        def project(m, dst, stride, off):
            b0, nb_ = MIX_BLK[m]
            for bi in range(nb_):
                i = wptr[0]
                wissue(i + NRING - 1)
                wt = wbf[i % NRING]
                pb = pbank()
                M = 16 if (m == "gla" and bi == 6) else 128
                for kc in range(8):
                    kb.mm(pb[0:M, 0:MT], wt[:, kc * 128:kc * 128 + M], hT[:, kc * MT:(kc + 1) * MT],
                          start=(kc == 0), stop=(kc == 7))
                kb.evac(dst[0:M, bi * stride + off: bi * stride + off + MT], pb[0:M, 0:MT])
                wptr[0] += 1
                if (bi + 1) % KPB == 0 or bi == nb_ - 1:
                    yield

        def step_stream(clock, g_):
            kb.step_max = 0.0
            try:
                next(g_)
            except StopIteration:
                return False
            if kb.step_max > 0.0:
                clock[g_] = max(clock[g_], kb.step_max)
            else:
                others = [v for k, v in clock.items() if k is not g_]
                clock[g_] = max(clock[g_], min(others) if others else 0.0) + 0.3
            return True

        def run_streams(gens):
            gens = list(gens)
            clock = {g_: 0.0 for g_ in gens}
            while gens:
                g_ = min(gens, key=lambda x: clock[x])
                if not step_stream(clock, g_):
                    gens.remove(g_)
                    del clock[g_]
        pendingT = [None]

        for l in range(L):
            src_d = x_d if l == 0 else mid_d
            dst_d = out_d if l == L - 1 else mid_d
            kb.dma("sp", bcp[:], bcp_d[l:l + 1, :].partition_broadcast(128))
            kb.dma("sp", ppar[:], ppar_d[l * 128:(l + 1) * 128, :])
            kb.dma("gq", wab[:], wab_d[l * 128:(l + 1) * 128, :])
            for q8 in range(8):
                kb.dma("gq", wout[:, q8 * 1024:(q8 + 1) * 1024], wout_d[l * 128:(l + 1) * 128, q8 * 1024:(q8 + 1) * 1024])
            kb.dma("gq", wup[:], wup_d[l * 128:(l + 1) * 128, :])
            kb.dma("sp", gup[:], gup_d[l * 128:(l + 1) * 128, :])
            kb.act(nal[:, :], BCv("a_log"), AF.Exp)
            kb.ts(nal[:, :], nal[:, :], -1.0, ALU.mult)
            for b in range(NB):
                for m_ in st:
                    kb.memset(st[m_][:, :], 0.0)
                for m_ in stb:
                    kb.memset(stb[m_][:, :], 0.0)
                for m_ in hist:
                    kb.memset(hist[m_][:, :], 0.0)
                for mt in range(nmt):
                    tok0 = b * T + mt * MT
                    def gen_H1(tok0=tok0):
                        for j in range(MT // 128):
                            xt = xts[j % 2]
                            r0 = tok0 + j * 128
                            kb.dma("sp", xt[:], src_d[r0:r0 + 128, :])
                            ss = S("n_ss", 8)
                            junk = P_g
                            kb.act(junk[:, 0:1024], xt[:, :], AF.Square)
                            kb.red(ss[:, 0:1], junk[:, 0:1024])
                            rsqrt_small(ss[:, 1:2], ss[:, 0:1], 1.0 / D, EPS)
                            kb.ts(ht[:, :], xt[:, :], ss[:, 1:2], ALU.mult)
                            yield
                            for half in range(2):
                                pb = bank()
                                for q in range(4):
                                    kc = half * 4 + q
                                    kb.tr(pb[:, q * 128:(q + 1) * 128], ht[:, kc * 128:(kc + 1) * 128], ident)
                                outv = hT.v(hT.h[:, :].rearrange("p (k t) -> p k t", k=8)[:, half * 4:half * 4 + 4,
                                                                                          j * 128:(j + 1) * 128])
                                kb.tt(outv, r3(pb, 4), bc_t(ppar, half * 4, 4, 128), ALU.mult)
                                yield
                        if "sc" in mixers:
                            yield from project("sc", P_s, MT + 4, 4)
                        if "gla" in mixers:
                            yield from project("gla", P_a, MT + 4, 4)
                    def gen_sc():
                        W2 = MT + 4
                        W2 = MT + 4
                        for blk in range(2):
                            u = S("sc_u%d" % blk, MT + 2)
                            kb.copy("g", u[:, 0:2], hist["sc"][:, blk * 2:blk * 2 + 2])
                            kb.tt(u[:, 2:MT + 2], P_s[:, (2 + blk) * W2 + 4:(2 + blk) * W2 + 4 + MT],
                                  P_s[:, (4 + blk) * W2 + 4:(4 + blk) * W2 + 4 + MT], ALU.mult)
                            kb.copy("g", hist["sc"][:, blk * 2:blk * 2 + 2], u[:, MT:MT + 2])
                            acc_t = S("sc_acc")
                            acc = acc_t[:, 0:MT]
                            kb.ts(acc, u[:, 0:MT], ppar[:, 41 + blk * 3:42 + blk * 3], ALU.mult)
                            for tap in (1, 2):
                                kb.stt(acc, u[:, tap:tap + MT], ppar[:, 41 + blk * 3 + tap:42 + blk * 3 + tap],
                                       acc, ALU.mult, ALU.add)
                            zs_t = S("sc_zs")
                            zs = zs_t[:, 0:MT]
                            kb.act(zs, P_s[:, (6 + blk) * W2 + 4:(6 + blk) * W2 + 4 + MT], AF.Silu)
                            kb.tt(acc, acc, P_s[:, blk * W2 + 4: blk * W2 + 4 + MT], ALU.mult)
                            kb.tt(yT[:, (4 + blk) * MT:(5 + blk) * MT], acc, zs, ALU.mult)
                            yield

                    def gen_gla():
                        W2 = MT + 4
                        W2 = MT + 4
                        for blk in range(2):
                            kb.act(ZS_a[:, blk * MT:(blk + 1) * MT], P_a[:, (4 + blk) * W2 + 4:(4 + blk) * W2 + 4 + MT],
                                   AF.Silu)
                        for j in range(MT // 128):
                            tsl = slice(j * 128, (j + 1) * 128)

                            def Pg(blk, rows=slice(0, 128)):
                                return P_a[rows, blk * W2 + 4 + j * 128: blk * W2 + 4 + (j + 1) * 128]
                            pb = bank()
                            kb.mm(pb[:, 0:128], Pg(6, slice(0, 16)), gup[0:16, :])
                            sp_ = S("gl_sp", 128)
                            kb.tt(sp_[:, :], pb[:, 0:128], BCv("gla_bias"), ALU.add)
                            yield
                            kb.act(sp_[:, :], sp_[:, :], AF.Exp, scale=-1.0)
                            yield
                            kb.act(sp_[:, :], sp_[:, :], AF.Ln, bias=1.0)
                            yield
                            pc = bank()
                            kb.mm(pc[:, 0:128], sp_[:, :], C("mcum"))
                            kb.mm(pc[:, 128:256], C("mrest"), sp_[:, :])
                            kb.mm(pc[:, 256:258], sp_[:, :], C("chunksel", 2))
                            kb.tr(pc[:, 384:512], Pg(1), ident)
                            ex = S("gl_ex", 512)
                            kb.act(ex[:, 0:128], pc[:, 0:128], AF.Exp, scale=-1.0 / 16)
                            kb.act(ex[:, 128:256], pc[:, 0:128], AF.Exp, scale=1.0 / 16)
                            kb.act(ex[:, 256:384], pc[:, 128:256], AF.Exp, scale=-1.0 / 16)
                            dS = S("gl_dS", 8)
                            kb.act(dS[:, 0:2], pc[:, 256:258], AF.Exp, scale=-1.0 / 16)
                            qe = S("gl_qe", 128); ke = S("gl_ke", 128); kend = S("gl_kend", 128)
                            kb.stt(qe[:, :], ex[:, 0:128], 32.0 ** -0.5, Pg(0), ALU.mult, ALU.mult)
                            kb.tt(ke[:, :], ex[:, 128:256], Pg(1), ALU.mult)
                            kb.tt(kend[:, :], ex[:, 256:384], pc[:, 384:512], ALU.mult)
                            yield
                            pv = bank()
                            for blk in range(2):
                                kb.tr(pv[:, blk * 128:(blk + 1) * 128], Pg(2 + blk), ident)
                            vtm = S("gl_v", 256)
                            kb.evac(vtm[:, :], pv[:, 0:256])
                            qblk = S("gl_qblk")
                            kb.tt(r3(qblk, 4), bc_h(qe, 0, 128, 4),
                                  cst.v(cst.h[:, COFF["bq"]:COFF["bq"] + 512].rearrange("p (h t) -> p h t", h=4)),
                                  ALU.mult)
                            pa = bank()
                            kb.mm(pa[:, :], ke[:, :], qblk[:, :])
                            attnT = S("gl_attn")
                            kb.tt(r3(attnT, 4), r3(pa, 4), Cb4("m_incl"), ALU.mult)
                            yield
                            Sg = st["gla"]
                            po = banks[3]
                            for c in range(2):
                                cs = slice(64 * c, 64 * c + 64)
                                kb.mm(po[cs, 256:512], qe[:, cs], Sg[:, :], start=True, stop=False)
                                for h in range(4):
                                    kb.mm(po[cs, 256 + h * 64:256 + (h + 1) * 64], attnT[cs, h * 128 + 64 * c:h * 128 + 64 * c + 64],
                                          vtm[cs, h * 64:(h + 1) * 64], start=False, stop=(h == 3))
                                yield
                                pst = bank()
                                kb.mm(pst[:, 0:256], kend[cs, :], vtm[cs, :])
                                tmp = S("gl_tmp", 256)
                                kb.tt(tmp[:, :], pst[:, 0:256], C("bs", 256), ALU.mult)
                                kb.stt(Sg[:, :], Sg[:, :], dS[:, c:c + 1], tmp[:, :], ALU.mult, ALU.add)
                                yield
                            osb = S("gl_o", 256)
                            kb.evac(osb[:, :], po[:, 256:512])
                            yield from head_norm_gate(osb, "gla_norm_w", 6, tsl, ZS_a, "gl_")
                            yield

                    def gen_gdn():
                        W2 = MT + 4
                        W2 = MT + 4
                        for blk in range(6):
                            kb.copy("g", P_g[:, blk * W2 + 1: blk * W2 + 4], hist["gdn"][:, blk * 3:blk * 3 + 3])
                            kb.copy("g", hist["gdn"][:, blk * 3:blk * 3 + 3], P_g[:, blk * W2 + 1 + MT: blk * W2 + 4 + MT])
                            acc = S("cv_acc%d" % (blk % 2), MT)[:, 0:MT]
                            kb.ts(acc, P_g[:, blk * W2 + 1: blk * W2 + 1 + MT], ppar[:, 8 + blk * 4:9 + blk * 4], ALU.mult)
                            for tap in (1, 2, 3):
                                yield
                                kb.stt(acc, P_g[:, blk * W2 + 1 + tap: blk * W2 + 1 + tap + MT],
                                       ppar[:, 8 + blk * 4 + tap:9 + blk * 4 + tap], acc, ALU.mult, ALU.add)
                            yield
                            kb.act(Q_g[:, blk * MT:(blk + 1) * MT], acc, AF.Silu)
                            yield
                        for blk in range(2):
                            kb.act(ZS_g[:, blk * MT:(blk + 1) * MT], P_g[:, (6 + blk) * W2 + 4:(6 + blk) * W2 + 4 + MT],
                                   AF.Silu)
                        for j in range(MT // 128):
                            tsl = slice(j * 128, (j + 1) * 128)

                            def Qg(blk, rows=slice(0, 128)):
                                return Q_g[rows, blk * MT + j * 128: blk * MT + (j + 1) * 128]
                            pq = bank(); pv = bank()
                            for blk in range(4):
                                kb.tr(pq[:, blk * 128:(blk + 1) * 128], Qg(blk), ident)
                            for blk in range(2):
                                kb.tr(pv[:, blk * 128:(blk + 1) * 128], Qg(4 + blk), ident)
                            qk = S("gd_qk"); vtm = S("gd_v", 256, BF16)
                            kb.evac(qk[:, :], pq[:, :])
                            kb.evac(vtm[:, :], pv[:, 0:256])
                            yield
                            pab = bank()
                            for kc in range(8):
                                kb.mm(pab[:, 0:8], hT[:, kc * MT + j * 128: kc * MT + (j + 1) * 128],
                                      wab[:, kc * 8:(kc + 1) * 8], start=(kc == 0), stop=(kc == 7))
                            sc = S("gd_sc", 64)
                            kb.tt(sc[:, 0:4], pab[:, 0:4], BCv("dt_bias"), ALU.add)
                            kb.act(sc[:, 4:8], pab[:, 4:8], AF.Exp, scale=-1.0)
                            yield
                            kb.act(sc[:, 0:4], sc[:, 0:4], AF.Exp)
                            yield
                            kb.act(sc[:, 0:4], sc[:, 0:4], AF.Ln, bias=1.0)
                            yield
                            kb.tt(sc[:, 0:4], sc[:, 0:4], nal[:, :], ALU.mult)
                            kb.act(sc[:, 4:8], sc[:, 4:8], AF.Ln, bias=1.0)
                            yield
                            kb.act(sc[:, 8:12], sc[:, 4:8], AF.Exp, scale=-1.0)
                            sq = S("gd_sq")
                            kb.tt(sq[:, :], qk[:, :], qk[:, :], ALU.mult)
                            yield
                            kb.red(sc[:, 12:20], r3(sq, 8))
                            yield
                            kb.act(sc[:, 20:28], sc[:, 12:20], AF.Ln, bias=EPS)
                            yield
                            kb.ts(sc[:, 20:28], sc[:, 20:28], -0.5, ALU.mult)
                            kb.ts(sc[:, 20:24], sc[:, 20:24], math.log(1.0 / 8.0), ALU.add)
                            yield
                            pg = bank()
                            kb.mm(pg[:, 0:4], C("mcum"), sc[:, 0:4])
                            kb.mm(pg[:, 4:8], C("mrest"), sc[:, 0:4])
                            gb = S("gd_gb", 256)
                            kb.copy("v", r3(gb, 4), bc_t(sc, 0, 4, 64))
                            for cb in range(2):
                                kb.mm(pg[:, 8 + 2 * cb: 10 + 2 * cb], gb[:, cb * 128:(cb + 1) * 128], C("chunksel", 2))
                            dS = S("gd_dS", 8)
                            kb.act(dS[:, 0:4], pg[:, 8:12], AF.Exp)
                            kb.copy("v", sc[:, 28:32], pg[:, 0:4])
                            kb.tt(sc[:, 52:56], pg[:, 4:8], sc[:, 24:28], ALU.add)
                            yield
                            kb.tt(sc[:, 32:36], sc[:, 28:32], sc[:, 4:8], ALU.subtract)
                            kb.act(sc[:, 52:56], sc[:, 52:56], AF.Exp)
                            yield
                            kb.tt(sc[:, 48:52], sc[:, 32:36], sc[:, 24:28], ALU.add)
                            kb.tt(sc[:, 36:40], sc[:, 24:28], sc[:, 28:32], ALU.subtract)
                            yield
                            kb.copy("v", sc[:, 32:36], sc[:, 48:52])
                            kb.tt(sc[:, 40:44], sc[:, 28:32], sc[:, 20:24], ALU.add)
                            yield
                            kb.act(sc[:, 44:48], sc[:, 40:44], AF.Exp)
                            kb.act(sc[:, 48:52], sc[:, 48:52], AF.Exp)
                            yield
                            UW = S("gd_UW", 512, BF16)
                            kb.tt(Uv(UW), r3(vtm, 4), bc_t(sc, 8, 4, 64), ALU.mult)
                            yield
                            kb.tt(Wv(UW), r3(qk, 4, 256, 512), bc_t(sc, 48, 4, 64), ALU.mult)
                            yield
                            kend = S("gd_kend", 256, BF16)
                            kb.tt(r3(kend, 4), r3(qk, 4, 256, 512), bc_t(sc, 52, 4, 64), ALU.mult)
                            yield
                            dq = S("gd_dq")
                            kb.tt(r3(dq, 4), Cb4("ident"), bc_t(sc, 44, 4, 128), ALU.mult)
                            pqg = bank()
                            for h in range(4):
                                pb_, cb = 64 * (h % 2), h // 2
                                kb.mm(pqg[pb_:pb_ + 64, cb * 128:(cb + 1) * 128], qk[:, h * 64:(h + 1) * 64],
                                      dq[:, h * 128:(h + 1) * 128])
                            fm = S("gd_fm", 512, BF16)
                            kb.evac(fm[:, 0:256], pqg[:, 0:256])
                            yield
                            rd = S("gd_rd"); cbm = S("gd_cb")
                            kb.tt(r3(rd, 4), Cb4("ident"), bc_t(sc, 32, 4, 128), ALU.mult)
                            kb.tt(r3(cbm, 4), Cb4("nm_strict"), bc_t(sc, 36, 4, 128), ALU.add)
                            pe1 = bank()
                            kb.mm(pe1[:, :], C("ones"), rd[:, :], start=True, stop=False)
                            kb.mm(pe1[:, :], ident, cbm[:, :], start=False, stop=True)
                            DTs = S("gd_DTs")
                            kb.act(DTs[:, :], pe1[:, :], AF.Exp)
                            yield
                            rd2 = S("gd_rd"); cb2 = S("gd_cb")
                            kb.tt(r3(rd2, 4), Cb4("ident"), bc_t(sc, 40, 4, 128), ALU.mult)
                            kb.tt(r3(cb2, 4), Cb4("nm_incl"), bc_t(sc, 36, 4, 128), ALU.add)
                            pe2 = bank()
                            kb.mm(pe2[:, :], C("ones"), rd2[:, :], start=True, stop=False)
                            kb.mm(pe2[:, :], ident, cb2[:, :], start=False, stop=True)
                            DTi = S("gd_DTi")
                            kb.act(DTi[:, :], pe2[:, :], AF.Exp)
                            yield
                            Nm = S("gd_N", 512, BF16); attnT = S("gd_attn", 512, BF16)
                            pkk = bank()
                            for h in HO:
                                pb_, cb = 64 * (h % 2), h // 2
                                kT = Qg(2 + cb, slice(pb_, pb_ + 64))
                                kb.mm(pkk[:, h * 128:(h + 1) * 128], kT, kT)
                            kb.tt(Nm[:, :], pkk[:, :], DTs[:, :], ALU.mult)
                            yield
                            pqk = bank()
                            for h in HO:
                                pb_, cb = 64 * (h % 2), h // 2
                                kT = Qg(2 + cb, slice(pb_, pb_ + 64))
                                qT_ = Qg(cb, slice(pb_, pb_ + 64))
                                kb.mm(pqk[:, h * 128:(h + 1) * 128], kT, qT_)
                            kb.tt(attnT[:, :], pqk[:, :], DTi[:, :], ALU.mult)
                            yield
                            yield from solve(Nm, UW, "gd_")
                            pwt = bank()
                            for h in range(4):
                                pb_, cb = 64 * (h % 2), h // 2
                                kb.tr(pwt[pb_:pb_ + 64, cb * 128:(cb + 1) * 128], UW[:, h * 128 + 64:(h + 1) * 128], ident)
                            kb.evac(fm[:, 256:512], pwt[:, 0:256])
                            yield
                            po = banks[2]
                            yield from seq_core("gd_", po, st["gdn"], stb["gdn"], UW, -1, fm, 256, fm, 0, attnT, kend, dS)
                            osb = S("gd_o", 256)
                            kb.evac(osb[:, :], po[:, 0:256])
                            yield from head_norm_gate(osb, "gdn_norm_w", 0, tsl, ZS_g, "gd_")
                            yield

                    def gen_rwkv():
                        W2 = MT + 4
                        W2 = MT + 4
                        for blk in range(9):
                            kb.copy("g", P_r[:, blk * W2 + 3: blk * W2 + 4], hist["rwkv"][:, blk:blk + 1])
                            kb.copy("g", hist["rwkv"][:, blk:blk + 1], P_r[:, blk * W2 + 3 + MT: blk * W2 + 4 + MT])
                            dlt_t = S("rw_d")
                            dlt = dlt_t[:, 0:MT]
                            kb.tt(dlt, P_r[:, blk * W2 + 3: blk * W2 + 3 + MT], P_r[:, blk * W2 + 4: blk * W2 + 4 + MT],
                                  ALU.subtract)
                            yield
                            kb.stt(Q_r[:, blk * MT:(blk + 1) * MT], dlt, ppar[:, 32 + blk:33 + blk],
                                   P_r[:, blk * W2 + 4: blk * W2 + 4 + MT], ALU.mult, ALU.add)
                            yield
                        for blk in range(2):
                            kb.act(ZS_r[:, blk * MT:(blk + 1) * MT], Q_r[:, (6 + blk) * MT:(7 + blk) * MT], AF.Silu)
                        kb.act(Q_r[0:64, 8 * MT:9 * MT], Q_r[0:64, 8 * MT:9 * MT], AF.Tanh)
                        for j in range(MT // 128):
                            tsl = slice(j * 128, (j + 1) * 128)

                            def Qr(blk, rows=slice(0, 128)):
                                return Q_r[rows, blk * MT + j * 128: blk * MT + (j + 1) * 128]
                            prk = bank(); pv = bank()
                            for blk in range(4):
                                kb.tr(prk[:, blk * 128:(blk + 1) * 128], Qr(blk), ident)
                            for blk in range(2):
                                kb.tr(pv[:, blk * 128:(blk + 1) * 128], Qr(4 + blk), ident)
                            rk = S("rw_rk"); vtm = S("rw_v", 256, BF16)
                            kb.evac(rk[:, :], prk[:, :])
                            kb.evac(vtm[:, :], pv[:, 0:256])
                            yield
                            pwa = bank()
                            kb.mm(pwa[:, 0:256], Qr(8, slice(0, 64)), wup[0:64, :])
                            kb.mm(pwa[:, 256:512], Qr(8, slice(64, 128)), wup[64:128, :])
                            lw = S("rw_lw", 256); a_ = S("rw_a", 256)
                            kb.tt(lw[:, :], pwa[:, 0:256], BCv("w0"), ALU.add)
                            kb.tt(a_[:, :], pwa[:, 256:512], BCv("a0"), ALU.add)
                            yield
                            kb.act(lw[:, :], lw[:, :], AF.Sigmoid)
                            kb.act(a_[:, :], a_[:, :], AF.Sigmoid)
                            yield
                            kb.ts(lw[:, :], lw[:, :], -math.exp(-0.5), ALU.mult)
                            kk = S("rw_kk", 256); sq = S("rw_sq", 256); sc = S("rw_sc", 32)
                            yield
                            kb.tt(kk[:, :], rk[:, 256:512], BCv("k_k"), ALU.mult)
                            kb.tt(sq[:, :], kk[:, :], kk[:, :], ALU.mult)
                            yield
                            kb.red(sc[:, 0:4], r3(sq, 4))
                            yield
                            rsqrt_small(sc[:, 4:8], sc[:, 0:4], 1.0, EPS)
                            yield
                            kb.tt(r3(kk, 4), r3(kk, 4), bc_t(sc, 4, 4, 64), ALU.mult)
                            km = S("rw_km", 256)
                            yield
                            kb.stt(km[:, :], a_[:, :], -1.0, BCv("k_a"), ALU.add, ALU.mult)
                            yield
                            kb.stt(km[:, :], km[:, :], 1.0, rk[:, 256:512], ALU.add, ALU.mult)
                            yield
                            bb = S("rw_b", 256)
                            kb.tt(bb[:, :], kk[:, :], a_[:, :], ALU.mult)
                            yield
                            kb.tt(sq[:, :], rk[:, 0:256], km[:, :], ALU.mult)
                            yield
                            kb.tt(sq[:, :], sq[:, :], BCv("r_k"), ALU.mult)
                            kb.red(sc[:, 8:12], r3(sq, 4))
                            yield
                            pc1 = bank(); pc2 = bank()
                            kb.mm(pc1[:, 0:256], C("mcum"), lw[:, :])
                            kb.mm(pc1[:, 256:512], C("mcumx"), lw[:, :])
                            kb.mm(pc2[:, 0:256], C("mrest"), lw[:, :])
                            for h in range(4):
                                pb_, cb = 64 * (h % 2), h // 2
                                kb.mm(pc2[pb_:pb_ + 64, 256 + 2 * cb: 258 + 2 * cb], lw[:, h * 64:(h + 1) * 64],
                                      C("chunksel", 2))
                            dS = S("rw_dS", 8)
                            kb.act(dS[:, 0:4], pc2[:, 256:260], AF.Exp)
                            eg = S("rw_eg"); er = S("rw_er", 256); eng_ = S("rw_eng", 256)
                            kb.act(eg[:, :], pc1[:, :], AF.Exp)
                            kb.act(eng_[:, :], pc1[:, 0:256], AF.Exp, scale=-1.0)
                            kb.act(er[:, :], pc2[:, 0:256], AF.Exp)
                            TA = S("rw_TA", 512, BF16); TB = S("rw_TB", 512, BF16)
                            yield
                            kb.tt(TA[:, 0:256], rk[:, 0:256], eg[:, 0:256], ALU.mult)
                            kb.tt(TA[:, 256:512], km[:, :], eng_[:, :], ALU.mult)
                            yield
                            kb.tt(TB[:, 0:256], bb[:, :], eng_[:, :], ALU.mult)
                            kb.tt(TB[:, 256:512], kk[:, :], eg[:, 256:512], ALU.mult)
                            yield
                            kend2 = S("rw_kend2", 256, BF16); kend = S("rw_kend", 256, BF16)
                            yield
                            kb.tt(kend2[:, :], km[:, :], er[:, :], ALU.mult)
                            kb.stt(kend[:, :], bb[:, :], -1.0, er[:, :], ALU.mult, ALU.mult)
                            yield
                            FA = S("rw_FA", 512, BF16); FB = S("rw_FB", 512, BF16)
                            pfa = bank()
                            for q in range(4):
                                kb.tr(pfa[:, q * 128:(q + 1) * 128], TA[:, q * 128:(q + 1) * 128], ident)
                            kb.evac(FA[:, :], pfa[:, :])
                            yield
                            pfb = bank()
                            for q in range(4):
                                kb.tr(pfb[:, q * 128:(q + 1) * 128], TB[:, q * 128:(q + 1) * 128], ident)
                            kb.evac(FB[:, :], pfb[:, :])
                            yield
                            Nm = S("rw_N", 512, BF16); AbkT = S("rw_Abk", 512, BF16); attn2T = S("rw_attn2", 512, BF16); attnT = S("rw_attn", 512, BF16)
                            pn = bank(); pbk = bank()
                            for h in HO:
                                hs = slice(h * 128, (h + 1) * 128)
                                KT = fmh(FA, h, 0, 128, 256)
                                BT = fmh(FB, h, 0, 128, 0); KKT = fmh(FB, h, 0, 128, 256)
                                kb.mm(pn[:, hs], BT, KKT)
                                kb.mm(pbk[:, hs], KT, KKT)
                            kb.tt(r3(Nm, 4), r3(pn, 4), Cb4("m_strict"), ALU.mult)
                            kb.tt(r3(AbkT, 4), r3(pbk, 4), Cb4("m_strict"), ALU.mult)
                            yield
                            prk2 = bank(); prb = bank()
                            for h in HO:
                                hs = slice(h * 128, (h + 1) * 128)
                                RT = fmh(FA, h, 0, 128, 0); KT = fmh(FA, h, 0, 128, 256)
                                BT = fmh(FB, h, 0, 128, 0)
                                kb.mm(prk2[:, hs], KT, RT)
                                kb.mm(prb[:, hs], BT, RT)
                            kb.tt(r3(attn2T, 4), r3(prk2, 4), Cb4("m_incl"), ALU.mult)
                            kb.tt(r3(attnT, 4), r3(prb, 4), Cb4("m_incl_neg"), ALU.mult)
                            yield
                            UW = S("rw_UW", 512, BF16)
                            pu = bank()
                            for h in range(4):
                                kb.mm(pu[:, h * 64:(h + 1) * 64], AbkT[:, h * 128:(h + 1) * 128], vtm[:, h * 64:(h + 1) * 64])
                            kb.evac(Uv(UW), r3(pu, 4, 0, 256))
                            kb.copy("v", Wv(UW), r3(TB, 4, 256, 512))
                            yield
                            yield from solve(Nm, UW, "rw_")
                            pwt = bank()
                            for h in range(4):
                                pb_, cb = 64 * (h % 2), h // 2
                                kb.tr(pwt[pb_:pb_ + 64, cb * 128:(cb + 1) * 128], UW[:, h * 128 + 64:(h + 1) * 128], ident)
                            WT = S("rw_WT", 256, BF16)
                            kb.evac(WT[:, :], pwt[:, 0:256])
                            yield
                            po = banks[3]
                            yield from seq_core("rw_", po, st["rwkv"], stb["rwkv"], UW, +1, WT, 0, FA, 0, attnT, kend, dS, attn2T, kend2, vtm)
                            osb = S("rw_o", 256); cen = S("rw_cen", 256)
                            kb.evac(osb[:, :], po[:, 0:256])
                            yield
                            kb.red(sc[:, 12:16], r3(osb, 4))
                            yield
                            kb.ts(sc[:, 12:16], sc[:, 12:16], 1.0 / 64, ALU.mult)
                            yield
                            kb.tt(r3(cen, 4), r3(osb, 4), bc_t(sc, 12, 4, 64), ALU.subtract)
                            yield
                            kb.tt(sq[:, :], cen[:, :], cen[:, :], ALU.mult)
                            yield
                            kb.red(sc[:, 16:20], r3(sq, 4))
                            yield
                            rsqrt_small(sc[:, 20:24], sc[:, 16:20], 1.0 / 64, 64e-5)
                            yield
                            kb.tt(r3(cen, 4), r3(cen, 4), bc_t(sc, 20, 4, 64), ALU.mult)
                            yield
                            kb.tt(cen[:, :], cen[:, :], BCv("ln_w"), ALU.mult)
                            yield
                            kb.tt(cen[:, :], cen[:, :], BCv("ln_b"), ALU.add)
                            yield
                            kb.tt(r3(sq, 4), r3(vtm, 4), bc_t(sc, 8, 4, 64), ALU.mult)
                            yield
                            ob = S("rw_hn_ob", 256, BF16)
                            kb.tt(ob[:, :], cen[:, :], sq[:, :], ALU.add)
                            to_fm_gate(ob, 2, tsl, ZS_r)
                            yield


                    proj_done = {}

                    def gen_P():
                        if "gdn" in mixers:
                            yield from project("gdn", P_g, MT + 4, 4)
                        else:
                            kb.memset(yT[:, 0:2 * MT], 0.0)
                        proj_done["gdn"] = True
                        if "rwkv" in mixers:
                            yield from project("rwkv", P_r, MT + 4, 4)
                        else:
                            kb.memset(yT[:, 2 * MT:4 * MT], 0.0)
                        proj_done["rwkv"] = True
                    run_streams([gen_H1()] + ([pendingT[0]] if pendingT[0] is not None else []))
                    pendingT[0] = None
                    active = [gen_P()]
                    if "sc" in mixers:
                        active.append(gen_sc())
                    if "gla" in mixers:
                        active.append(gen_gla())
                    pending = {}
                    if "gdn" in mixers:
                        pending["gdn"] = gen_gdn
                    if "rwkv" in mixers:
                        pending["rwkv"] = gen_rwkv
                    clock = {g_: 0.0 for g_ in active}
                    while active or pending:
                        g_ = min(active, key=lambda x: clock[x] - (PRI if x.__name__ in ("gen_gdn", "gen_rwkv") else 0.0))
                        if not step_stream(clock, g_):
                            active.remove(g_)
                            del clock[g_]
                        for m_ in list(pending):
                            if proj_done.get(m_):
                                gn_ = pending.pop(m_)()
                                active.append(gn_)
                                clock[gn_] = (min(clock.values()) if clock else 0.0) + (GOFF if m_ == "gdn" else 0.0)
                    for m_, (y0, y1) in (("sc", (4, 6)), ("gla", (6, 8))):
                        if m_ not in mixers:
                            kb.memset(yT[:, y0 * MT:y1 * MT], 0.0)
                    def gen_T(tok0=tok0, src_d=src_d, dst_d=dst_d):
                        for j in range(MT // 128):
                            r0 = tok0 + j * 128
                            kb.dma("sp", xres[:], src_d[r0:r0 + 128, :])
                            osb = S("op_o", 1024)
                            for half in range(2):
                                pb = pbank()
                                for kc in range(8):
                                    kb.mm(pb[:, :], yT[:, kc * MT + j * 128: kc * MT + (j + 1) * 128],
                                          wout[:, kc * 1024 + half * 512: kc * 1024 + (half + 1) * 512],
                                          start=(kc == 0), stop=(kc == 7))
                                kb.evac(osb[:, half * 512:(half + 1) * 512], pb[:, :])
                                yield
                            ss = S("t_ss", 8)
                            junk = P_r
                            kb.act(junk[:, 0:1024], osb[:, :], AF.Square)
                            kb.red(ss[:, 2:3], junk[:, 0:1024])
                            rsqrt_small(ss[:, 3:4], ss[:, 2:3], 1.0 / D, EPS)
                            yield
                            kb.stt(osb[:, :], osb[:, :], ss[:, 3:4], BCv("post_w"), ALU.mult, ALU.mult)
                            kb.tt(osb[:, :], osb[:, :], xres[:, :], ALU.add)
                            kb.dma("sp", dst_d[r0:r0 + 128, :], osb[:])
                            yield
                    pendingT[0] = gen_T()
            if pendingT[0] is not None:
                run_streams([pendingT[0]])
                pendingT[0] = None
            if l < L - 1:
                kb.wait_all("g")
                kb.op("g", lambda o: o.memset(P_r.h[:, 0:1], 0.0), [], [P_r[:, 0:1]])
                e = kb.engs["g"]
                kb.dq_eng["sp"].obj.wait_ge(e.sem, e.cnt)
                kb.dq_eng["sp"].waited["g"] = e.cnt
        kb.wait_all("g")
    return nc


_CACHE = {}


def run(inputs, NB, T, L, ncores, mixers=("gdn", "rwkv", "sc", "gla")):
    key = (NB, T, L, tuple(mixers))
    if key not in _CACHE:
        _CACHE[key] = build(NB, T, L, mixers)
    nc = _CACHE[key]
    pl = prep_layer_inputs(inputs, L)
    x = np.ascontiguousarray(inputs["x"], dtype=np.float32)
    in_maps = []
    for c in range(ncores):
        m = {"x": x[c * NB:(c + 1) * NB].reshape(NB * T, D), "consts": CONSTS}
        m.update(pl)
        in_maps.append(m)
    res = run_bass_kernel_spmd(nc, in_maps, core_ids=list(range(ncores)))
    outs = [r["out"].reshape(NB, T, D) for r in res.results]
    return np.concatenate(outs, axis=0)


def kernel(**inputs):
    inputs = {k: np.asarray(v) for k, v in inputs.items()}
    B, T, _ = inputs["x"].shape
    L = inputs["w_in"].shape[0]
    return run(inputs, B // NCORES, T, L, NCORES).astype(np.float32)
```

```python
import contextlib
import math
import os
DBG = float(os.environ.get('KDBG', '99'))
GOFF = float(os.environ.get("KGOFF", "0"))
LAT = float(os.environ.get("KLAT", "0.25"))
KPB = int(os.environ.get("KPB", "3"))
EVR = int(os.environ.get("KEVR", "4"))
import numpy as np
import concourse.bass as bass
import concourse.mybir as mybir
from concourse.bass_utils import run_bass_kernel_spmd

F32 = mybir.dt.float32
BF16 = mybir.dt.bfloat16
ALU = mybir.AluOpType
AF = mybir.ActivationFunctionType
AX = mybir.AxisListType

D = 1024
NCORES = 8
MT = 256
EPS = 1e-6
NEG = -30000.0


class Trk:
    __slots__ = ("w", "r", "tw", "tr")

    def __init__(self):
        self.w = None
        self.r = []
        self.tw = 0.0
        self.tr = 0.0


class V:
    __slots__ = ("ap", "trk")

    def __init__(self, ap, trk):
        self.ap = ap
        self.trk = trk


class Tl:
    def __init__(self, handle):
        self.h = handle
        self.trk = Trk()

    def __getitem__(self, key):
        return V(self.h[key], [self.trk])

    def v(self, ap):
        return V(ap, [self.trk])


class Eng:
    def __init__(self, name, obj, sem):
        self.name = name
        self.obj = obj
        self.sem = sem
        self.cnt = 0
        self.waited = {}


class KB:
    def __init__(self, nc, ctx):
        self.nc = nc
        self.ctx = ctx
        self.engs = {}
        for nm, obj in (("pe", nc.tensor), ("v", nc.vector), ("s", nc.scalar), ("g", nc.gpsimd)):
            sem = ctx.enter_context(nc.semaphore("sem_" + nm))
            self.engs[nm] = Eng(nm, obj, sem)
        self.dmaq = {"sp": nc.sync, "gq": nc.gpsimd}
        self.dq_eng = {"sp": Eng("spq", nc.sync, None), "gq": self.engs["g"]}
        self.dsems = {}
        for q in ("sp", "gq"):
            lst = []
            for i in range(8):
                sem = ctx.enter_context(nc.semaphore("dsem_%s%d" % (q, i)))
                e = Eng("d_%s%d" % (q, i), None, sem)
                self.engs[e.name] = e
                lst.append(e)
            self.dsems[q] = lst
        self.drr = {"sp": 0, "gq": 0}
        self.flip = 0
        self.efree = {}
        self.step_max = 0.0

    def sb(self, name, shape, dtype=F32):
        return Tl(self.ctx.enter_context(self.nc.sbuf_tensor("sb_" + name, list(shape), dtype)))

    def ps(self, name, shape, dtype=F32):
        return Tl(self.ctx.enter_context(self.nc.psum_tensor("ps_" + name, list(shape), dtype)))

    def _deps(self, issuer, reads, writes):
        deps = {}
        me = issuer.name

        def add(d, raw):
            if d is None:
                return
            e, c = d
            if e == me and (me == "pe" or not raw):
                return
            if deps.get(e, 0) < c:
                deps[e] = c
        for v in reads:
            for t in v.trk:
                add(t.w, True)
        for v in writes:
            for t in v.trk:
                add(t.w, False)
                for r in t.r:
                    add(r, False)
        for e, c in deps.items():
            if issuer.waited.get(e, 0) < c:
                issuer.obj.wait_ge(self.engs[e].sem, c)
                issuer.waited[e] = c

    def _mark(self, ident, reads, writes):
        for v in reads:
            for t in v.trk:
                t.r.append(ident)
        for v in writes:
            for t in v.trk:
                t.w = ident
                t.r = []

    def _est(self, eng, cost, reads, writes):
        dep = 0.0
        for v in reads:
            for t in v.trk:
                dep = max(dep, t.tw)
        for v in writes:
            for t in v.trk:
                dep = max(dep, t.tw, t.tr)
        start = max(self.efree.get(eng, 0.0), dep + LAT)
        fin = start + cost
        self.efree[eng] = fin
        for v in reads:
            for t in v.trk:
                t.tr = max(t.tr, fin)
        for v in writes:
            for t in v.trk:
                t.tw = fin
        self.step_max = max(self.step_max, fin)

    @staticmethod
    def _fsize(v):
        try:
            return float(v.ap.free_size())
        except Exception:
            return 256.0

    def op(self, eng, fn, reads, writes, cost=None):
        e = self.engs[eng]
        if cost is None:
            n = self._fsize(writes[0]) if writes else 256.0
            cost = {"pe": 0.07 + n / 1200.0, "v": 0.12 + n / 960.0, "s": 0.2 + n / 1200.0, "g": 0.15 + n / 500.0}[eng]
        self._est(eng, cost, reads, writes)
        self._deps(e, reads, writes)
        ins = fn(e.obj)
        e.cnt += 1
        ins.then_inc(e.sem, 1)
        self._mark((e.name, e.cnt), reads, writes)
        return ins

    def dma(self, q, out, in_):
        issuer = self.dq_eng[q]
        reads = [in_] if isinstance(in_, V) else []
        writes = [out] if isinstance(out, V) else []
        self._deps(issuer, reads, writes)
        self._est("q_" + q, 0.6, [], [])
        self.efree["q_" + q] -= 0.0
        self._est("dma_" + q + str(self.drr[q] % 4), 2.5, reads, writes)
        de = self.dsems[q][self.drr[q] % 8]
        self.drr[q] += 1
        oap = out.ap if isinstance(out, V) else out
        iap = in_.ap if isinstance(in_, V) else in_
        ins = self.dmaq[q].dma_start(out=oap, in_=iap)
        de.cnt += 16
        ins.then_inc(de.sem, 16)
        self._mark((de.name, de.cnt), reads, writes)

    def wait_all(self, eng):
        e = self.engs[eng]
        for nm, o in self.engs.items():
            if o.cnt > 0 and e.waited.get(nm, 0) < o.cnt and nm != e.name:
                e.obj.wait_ge(o.sem, o.cnt)
                e.waited[nm] = o.cnt

    def _pe_rowkey(self, ap, out):
        key = (ap.base_partition(), ap.partition_size())
        e = self.engs["pe"]
        if not hasattr(self, "_bank_last"):
            self._bank_last = {}
        bid = id(out.trk[0])
        last = self._bank_last.get(bid)
        if last is not None and last[0] != key and e.waited.get("pe", 0) < last[1]:
            e.obj.wait_ge(e.sem, last[1])
            e.waited["pe"] = last[1]
        self._bank_last[bid] = (key, e.cnt + 1)

    def mm(self, out, lhsT, rhs, start=True, stop=True):
        rd = [lhsT, rhs] + ([] if start else [out])
        self._pe_rowkey(lhsT.ap, out)
        return self.op("pe", lambda o: o.matmul(out.ap, lhsT.ap, rhs.ap, start=start, stop=stop), rd, [out],
                       cost=0.07 + self._fsize(rhs) / 1200.0)

    def tr(self, out, in_, ident):
        if in_.ap.dtype == BF16:
            k = in_.ap.partition_size()
            return self.mm(out, in_, self.ident_bf[0:k, 0:k])
        self._pe_rowkey(in_.ap, out)
        return self.op("pe", lambda o: o.transpose(out.ap, in_.ap, ident.ap), [in_, ident], [out])

    def act(self, out, in_, func, bias=None, scale=1.0, accum=None):
        rd = [in_]
        wr = [out]
        kw = {}
        if isinstance(bias, V):
            rd.append(bias)
            kw["bias"] = bias.ap
        elif bias is not None:
            kw["bias"] = bias
        kw["scale"] = scale
        if accum is not None:
            kw["accum_out"] = accum.ap
            wr.append(accum)
        return self.op("s", lambda o: o.activation(out.ap, in_.ap, func, **kw), rd, wr)

    def copy(self, eng, out, in_):
        if eng == "s":
            return self.op("s", lambda o: o.copy(out.ap, in_.ap), [in_], [out])
        return self.op(eng, lambda o: o.tensor_copy(out.ap, in_.ap), [in_], [out])

    def evac(self, out, in_):
        self.flip = (self.flip + 1) % EVR
        return self.copy("s" if self.flip else "v", out, in_)

    def tt(self, out, a, b, op, eng="v"):
        return self.op(eng, lambda o: o.tensor_tensor(out.ap, a.ap, b.ap, op), [a, b], [out])

    def ts(self, out, a, s1, op0, s2=None, op1=None, eng="v"):
        rd = [a]
        x1, x2 = s1, s2
        if isinstance(s1, V):
            rd.append(s1)
            x1 = s1.ap
        if isinstance(s2, V):
            rd.append(s2)
            x2 = s2.ap
        if op1 is None:
            return self.op(eng, lambda o: o.tensor_single_scalar(out.ap, a.ap, x1, op0), rd, [out])
        return self.op(eng, lambda o: o.tensor_scalar(out.ap, a.ap, x1, x2, op0, op1), rd, [out])

    def stt(self, out, in0, scalar, in1, op0, op1, eng="v"):
        rd = [in0, in1]
        sc = scalar
        if isinstance(scalar, V):
            rd.append(scalar)
            sc = scalar.ap
        return self.op(eng, lambda o: o.scalar_tensor_tensor(out.ap, in0.ap, sc, in1.ap, op0, op1), rd, [out])

    def red(self, out, in_, eng="v"):
        return self.op(eng, lambda o: o.tensor_reduce(out.ap, in_.ap, AX.X, ALU.add), [in_], [out])

    def recip(self, out, in_):
        return self.op("v", lambda o: o.reciprocal(out.ap, in_.ap), [in_], [out])

    def memset(self, out, val, eng="v"):
        return self.op(eng, lambda o: o.memset(out.ap, val), [], [out])


def make_consts():
    i = np.arange(128)
    same = (i[:, None] // 64) == (i[None, :] // 64)
    c = {}
    c["ident"] = np.eye(128)
    c["ones"] = np.ones((128, 128))
    c["mcum"] = (same & (i[:, None] <= i[None, :])) * 1.0
    c["mcumx"] = (same & (i[:, None] < i[None, :])) * 1.0
    c["mrest"] = (same & (i[:, None] > i[None, :])) * 1.0
    c["m_strict"] = (same & (i[:, None] < i[None, :])) * 1.0
    c["m_incl"] = (same & (i[:, None] <= i[None, :])) * 1.0
    c["m_incl_neg"] = -c["m_incl"]
    c["nm_strict"] = np.where(c["m_strict"] > 0, 0.0, NEG)
    c["nm_incl"] = np.where(c["m_incl"] > 0, 0.0, NEG)
    cs = np.zeros((128, 128))
    cs[:64, 0] = 1.0
    cs[64:, 1] = 1.0
    c["chunksel"] = cs
    p = np.arange(128)
    bq = np.zeros((128, 4, 128))
    for h in range(4):
        bq[32 * h:32 * h + 32, h, :] = 1.0
    bs = np.zeros((128, 256))
    for h in range(4):
        bs[32 * h:32 * h + 32, 64 * h:64 * h + 64] = 1.0
    names = ["ident", "ones", "mcum", "mcumx", "mrest", "m_strict", "m_incl", "m_incl_neg", "nm_strict",
             "nm_incl", "chunksel"]
    arr = np.concatenate([c[n] for n in names] + [bq.reshape(128, 512), bs], axis=1).astype(np.float32)
    offs = {n: k * 128 for k, n in enumerate(names)}
    offs["bq"] = len(names) * 128
    offs["bs"] = len(names) * 128 + 512
    return arr, offs


CONSTS, COFF = make_consts()
NCONST = CONSTS.shape[1]

GW = 256
GDN0 = 0
RW0 = 4 * GW + 8
SC0 = RW0 + 4 * GW + 128
GLA0 = SC0 + 4 * GW


def block_cols():
    blocks = []
    for b in range(8):
        blocks.append(np.arange(GDN0 + b * 128, GDN0 + (b + 1) * 128))
    for b in range(9):
        blocks.append(np.arange(RW0 + b * 128, RW0 + (b + 1) * 128))
    for b in range(8):
        blocks.append(np.arange(SC0 + b * 128, SC0 + (b + 1) * 128))
    for b in range(6):
        blocks.append(np.arange(GLA0 + b * 128, GLA0 + (b + 1) * 128))
    last = np.full(128, -1)
    last[:16] = np.arange(GLA0 + 768, GLA0 + 784)
    blocks.append(last)
    return blocks


BLOCKS = block_cols()
NBLK = len(BLOCKS)
MIX_BLK = {"gdn": (0, 8), "rwkv": (8, 9), "sc": (17, 8), "gla": (25, 7)}

BC = {}
_o = 0
for _n, _w in (("gdn_norm_w", 64), ("gla_norm_w", 64), ("k_k", 256), ("k_a", 256), ("r_k", 256), ("ln_w", 256),
               ("ln_b", 256), ("w0", 256), ("a0", 256), ("gla_bias", 128), ("post_w", 1024), ("a_log", 4),
               ("dt_bias", 4)):
    BC[_n] = (_o, _w)
    _o += _w
NBC = _o


def prep_layer_inputs(inp, L):
    out = {}
    w_in = inp["w_in"]
    wb = np.zeros((L, NBLK, 128, 8, 128), np.float32)
    for bi, cols in enumerate(BLOCKS):
        valid = cols >= 0
        sub = w_in[:, :, cols[valid]]
        sub = sub.reshape(L, 8, 128, -1).transpose(0, 2, 1, 3)
        wb[:, bi, :, :, :sub.shape[-1]] = sub
    out["wblk"] = wb.reshape(L * NBLK * 128, 8 * 128)
    ab = w_in[:, :, GDN0 + 1024:GDN0 + 1032].reshape(L, 8, 128, 8).transpose(0, 2, 1, 3)
    out["wab"] = np.ascontiguousarray(ab).reshape(L * 128, 64)
    wo = inp["w_out"].reshape(L, 8, 128, 1024).transpose(0, 2, 1, 3)
    out["wout"] = np.ascontiguousarray(wo).reshape(L * 128, 8 * 1024)
    bc = np.zeros((L, NBC), np.float32)

    def put(n, a):
        o, w = BC[n]
        bc[:, o:o + w] = a
    put("gdn_norm_w", inp["gdn_norm_w"]); put("gla_norm_w", inp["gla_norm_w"])
    put("k_k", inp["rwkv_k_k"]); put("k_a", inp["rwkv_k_a"]); put("r_k", inp["rwkv_r_k"])
    put("ln_w", inp["rwkv_ln_w"]); put("ln_b", inp["rwkv_ln_b"]); put("w0", inp["rwkv_w0"]); put("a0", inp["rwkv_a0"])
    put("gla_bias", inp["gla_a_bias"]); put("post_w", inp["post_norm_w"])
    put("a_log", inp["gdn_a_log"]); put("dt_bias", inp["gdn_dt_bias"])
    out["bcp"] = bc
    pp = np.zeros((L, 128, 8 + 24 + 9 + 6), np.float32)
    pp[:, :, 0:8] = inp["pre_norm_w"].reshape(L, 8, 128).transpose(0, 2, 1)
    g = inp["gdn_conv_w"].reshape(L, 4, 6, 128).transpose(0, 3, 2, 1)
    pp[:, :, 8:32] = g.reshape(L, 128, 24)
    pp[:, :, 32:41] = inp["rwkv_mu"].reshape(L, 9, 128).transpose(0, 2, 1)
    s = inp["sc_conv_w"].reshape(L, 3, 2, 128).transpose(0, 3, 2, 1)
    pp[:, :, 41:47] = s.reshape(L, 128, 6)
    out["ppar"] = pp.reshape(L * 128, 47)
    up = np.zeros((L, 128, 256), np.float32)
    up[:, 0:64] = inp["rwkv_w_up"]
    up[:, 64:128] = inp["rwkv_a_up"]
    out["wup"] = up.reshape(L * 128, 256)
    gu = np.zeros((L, 128, 128), np.float32)
    gu[:, 0:16] = inp["gla_a_up"]
    out["gup"] = gu.reshape(L * 128, 128)
    return out


def build(NB, T, L, mixers=("gdn", "rwkv", "sc", "gla")):
    nc = bass.Bass("TRN2", target_bir_lowering=False)
    ntok = NB * T
    nmt = T // MT

    def din(name, shape):
        return nc.dram_tensor(name, list(shape), F32, kind="ExternalInput").ap()
    x_d = din("x", [ntok, D])
    wblk_d = din("wblk", [L * NBLK * 128, 1024])
    wab_d = din("wab", [L * 128, 64])
    wout_d = din("wout", [L * 128, 8192])
    bcp_d = din("bcp", [L, NBC])
    ppar_d = din("ppar", [L * 128, 47])
    wup_d = din("wup", [L * 128, 256])
    gup_d = din("gup", [L * 128, 128])
    cst_d = din("consts", [128, NCONST])
    out_d = nc.dram_tensor("out", [ntok, D], F32, kind="ExternalOutput").ap()
    mid_d = nc.dram_tensor("xmid", [ntok, D], F32).ap() if L > 1 else None

    with contextlib.ExitStack() as ctx:
        ctx.enter_context(nc.allow_low_precision("bf16 projection operands, fp32 accumulation"))
        kb = KB(nc, ctx)
        sb, ps = kb.sb, kb.ps
        cst = sb("cst", [128, NCONST])
        kb.dma("sp", cst[:], cst_d[:, :])

        def C(n, w=128):
            o = COFF[n]
            return cst[:, o:o + w]

        def Cb4(n):
            o = COFF[n]
            return cst.v(cst.h[:, o:o + 128].unsqueeze(1).to_broadcast([128, 4, 128]))
        ident = C("ident")
        ident_bf = sb("ident_bf", [128, 128], BF16)
        kb.copy("v", ident_bf[:, :], ident)
        kb.ident_bf = ident_bf

        bcp = sb("bcp", [128, NBC]); ppar = sb("ppar", [128, 47]); wab = sb("wab", [128, 64], BF16)
        wout = sb("wout", [128, 8192], BF16); wup = sb("wup", [128, 256], BF16); gup = sb("gup", [128, 128])
        nal = sb("nal", [128, 4])
        NRING = 8
        wbf = [sb("wbf%d" % i, [128, 1024], BF16) for i in range(NRING)]
        xts = [sb("xt%d" % i, [128, 1024]) for i in range(2)]
        ht = sb("ht", [128, 1024], BF16)
        hT = sb("hT", [128, 8 * MT], BF16)
        yT = sb("yT", [128, 8 * MT], BF16)
        P_s = sb("P_s", [128, 8 * (MT + 4)]); xres = sb("xres", [128, 1024])
        P_a = sb("P_a", [128, 8 * (MT + 4)]); P_g = sb("P_g", [128, 8 * (MT + 4)]); P_r = sb("P_r", [128, 9 * (MT + 4)])
        Q_g = sb("Q_g", [128, 6 * MT], BF16); Q_r = sb("Q_r", [128, 9 * MT], BF16)
        ZS_a = sb("ZS_a", [128, 2 * MT]); ZS_g = sb("ZS_g", [128, 2 * MT]); ZS_r = sb("ZS_r", [128, 2 * MT])
        st = {m: sb("st_" + m, [128, 256]) for m in ("gdn", "rwkv", "gla")}
        stb = {m: sb("stb_" + m, [128, 128], BF16) for m in ("gdn", "rwkv")}
        hist = {"gdn": sb("h_gdn", [128, 6 * 3]), "rwkv": sb("h_rwkv", [128, 9]), "sc": sb("h_sc", [128, 4])}
        banks = [ps("bk%d" % i, [128, 512]) for i in range(8)]
        bctr = [0]

        def bank():
            b = min(banks[4:8], key=lambda t: (max(t.trk.tw, t.trk.tr), id(t)))
            bctr[0] += 1
            b.trk.tr = max(b.trk.tr, kb.step_max, max(kb.efree.values()) if kb.efree else 0.0) + 1e-3
            return b
        pjctr = [0]

        def pbank():
            b = banks[pjctr[0] % 2]
            pjctr[0] += 1
            return b
        scr = {}

        ALIAS = {"sc_acc": "gl_qblk", "sc_zs": "gl_ex", "rw_d": "rw_eg"}

        def S(name, w=512, dt=F32):
            name = ALIAS.get(name, name)
            if name not in scr:
                scr[name] = sb("s_" + name, [128, w], dt)
            return scr[name]

        def BCv(n, rows=128):
            o, w = BC[n]
            return bcp[0:rows, o:o + w]

        def r3(tile, h, a=0, b=None):
            b = b if b is not None else tile.h.shape[1]
            return tile.v(tile.h[:, a:b].rearrange("p (h t) -> p h t", h=h))

        def bc_t(tile, a, nh, n):
            return tile.v(tile.h[:, a:a + nh].unsqueeze(2).to_broadcast([128, nh, n]))

        def bc_h(tile, a, w, nh):
            return tile.v(tile.h[:, a:a + w].unsqueeze(1).to_broadcast([128, nh, w]))

        def Uv(tile, rows=slice(0, 128)):
            return tile.v(tile.h[rows, 0:512].rearrange("p (h t) -> p h t", h=4)[:, :, 0:64])

        def Wv(tile, rows=slice(0, 128)):
            return tile.v(tile.h[rows, 0:512].rearrange("p (h t) -> p h t", h=4)[:, :, 64:128])

        def r3r(tile, rows, h, a, b):
            return tile.v(tile.h[rows, a:b].rearrange("p (h t) -> p h t", h=h))
        HO = (0, 2, 1, 3)

        def rsqrt_small(out, in_, scale, eps):
            kb.act(out, in_, AF.Ln, bias=eps, scale=scale)
            kb.act(out, out, AF.Exp, scale=-0.5)

        def solve(Nm, UW, pfx):
            Am = S(pfx + "sol_A0", 512, BF16)
            pb = bank()
            for h in range(4):
                kb.tr(pb[:, h * 128:(h + 1) * 128], Nm[:, h * 128:(h + 1) * 128], ident)
            kb.evac(Am[:, :], pb[:, :])
            yield
            curN, curA = Nm, Am
            for lvl in range(6):
                pb = bank()
                for h in range(4):
                    hs = slice(h * 128, (h + 1) * 128)
                    kb.mm(pb[:, hs], curN[:, hs], UW[:, hs])
                if lvl < 5:
                    nN = S(pfx + "sol_N%d" % (lvl % 2), 512, BF16)
                    pn = bank()
                    for h in range(4):
                        hs = slice(h * 128, (h + 1) * 128)
                        kb.mm(pn[:, hs], curA[:, hs], curN[:, hs])
                if lvl < 4:
                    nA = S(pfx + "sol_A%d" % ((lvl + 1) % 2), 512, BF16)
                    pa = bank()
                    for h in range(4):
                        hs = slice(h * 128, (h + 1) * 128)
                        kb.mm(pa[:, hs], curN[:, hs], curA[:, hs])
                kb.tt(UW[:, :], UW[:, :], pb[:, :], ALU.subtract if lvl == 0 else ALU.add)
                if lvl == 5:
                    break
                kb.evac(nN[:, :], pn[:, :])
                if lvl < 4:
                    kb.evac(nA[:, :], pa[:, :])
                    curA = nA
                curN = nN
                yield

        def fmh(tile, h, c0, c1, base=0):
            pb_, cb = 64 * (h % 2), h // 2
            return tile[pb_:pb_ + 64, base + cb * 128 + c0: base + cb * 128 + c1]

        def sth(Sx, h):
            pb_, cb = 64 * (h % 2), h // 2
            return Sx[pb_:pb_ + 64, cb * 64:(cb + 1) * 64]

        def seq_core(pfx, po, Sx, Sb, UW, sign, wT, wbase, qT, qbase, attnT, kend, dS, attn2T=None, kend2=None, V2=None):
            X = S(pfx + "seq_X", 256, BF16)
            for c in range(2):
                cs = slice(64 * c, 64 * c + 64)
                pw = bank()
                for h in HO:
                    kb.mm(pw[cs, h * 64:(h + 1) * 64], fmh(wT, h, 64 * c, 64 * c + 64, wbase), sth(Sb, h))
                kb.tt(r3r(X, cs, 4, 0, 256), Uv(UW, cs), r3r(pw, cs, 4, 0, 256), ALU.subtract if sign < 0 else ALU.add)
                yield
                for h in (HO if c == 0 else (1, 3, 0, 2)):
                    o_ = po[cs, h * 64:(h + 1) * 64]
                    kb.mm(o_, fmh(qT, h, 64 * c, 64 * c + 64, qbase), sth(Sb, h), start=True, stop=False)
                    kb.mm(o_, attnT[cs, h * 128 + 64 * c: h * 128 + 64 * c + 64], X[cs, h * 64:(h + 1) * 64],
                          start=False, stop=(attn2T is None))
                    if attn2T is not None:
                        kb.mm(o_, attn2T[cs, h * 128 + 64 * c: h * 128 + 64 * c + 64], V2[cs, h * 64:(h + 1) * 64],
                              start=False, stop=True)
                pst = bank()
                for h in range(4):
                    o_ = sth(pst, h)
                    kb.mm(o_, kend[cs, h * 64:(h + 1) * 64], X[cs, h * 64:(h + 1) * 64], start=True,
                          stop=(kend2 is None))
                    if kend2 is not None:
                        kb.mm(o_, kend2[cs, h * 64:(h + 1) * 64], V2[cs, h * 64:(h + 1) * 64], start=False, stop=True)
                for cb in range(2):
                    kb.stt(Sx[:, cb * 64:(cb + 1) * 64], Sx[:, cb * 64:(cb + 1) * 64], dS[:, 2 * cb + c:2 * cb + c + 1],
                           pst[:, cb * 64:(cb + 1) * 64], ALU.mult, ALU.add)
                kb.copy("s", Sb[:, 0:128], Sx[:, 0:128])
                yield

        def head_norm_gate(o_sb, normw, yblk0, tsl, ZS, pfx):
            sq = S(pfx + "hn_sq", 256)
            ss = S(pfx + "hn_ss", 8)
            kb.tt(sq[:, :], o_sb[:, 0:256], o_sb[:, 0:256], ALU.mult)
            kb.red(ss[:, 0:4], r3(sq, 4))
            yield
            rsqrt_small(ss[:, 4:8], ss[:, 0:4], 1.0 / 64, EPS)
            yield
            kb.tt(r3(sq, 4), r3(o_sb, 4, 0, 256), bc_t(ss, 4, 4, 64), ALU.mult)
            ob = S(pfx + "hn_ob", 256, BF16)
            kb.tt(r3(ob, 4), r3(sq, 4), bc_h(bcp, BC[normw][0], 64, 4), ALU.mult)
            yield
            to_fm_gate(ob, yblk0, tsl, ZS)

        def to_fm_gate(o_tm, yblk0, tsl, ZS):
            pb = bank()
            for blk in range(2):
                kb.tr(pb[:, blk * 128:(blk + 1) * 128], o_tm[:, blk * 128:(blk + 1) * 128], ident)
            for blk in range(2):
                kb.tt(yT[:, (yblk0 + blk) * MT + tsl.start:(yblk0 + blk) * MT + tsl.stop],
                      pb[:, blk * 128:(blk + 1) * 128],
                      ZS[:, blk * MT + tsl.start: blk * MT + tsl.stop], ALU.mult)

        wseq = []
        wstate = {"issued": 0}

        def wissue(upto):
            while wstate["issued"] < min(upto, len(wseq)):
                i = wstate["issued"]
                l, b = wseq[i]
                r0 = (l * NBLK + b) * 128
                kb.dma("gq", wbf[i % NRING][:], wblk_d[r0:r0 + 128, :])
                wstate["issued"] += 1
        used_blocks = []
        for m in ("sc", "gla", "gdn", "rwkv"):
            if m in mixers:
                b0, nb_ = MIX_BLK[m]
                used_blocks += list(range(b0, b0 + nb_))
        for l in range(L):
            for b in range(NB):
                for mt in range(nmt):
                    for blk in used_blocks:
                        wseq.append((l, blk))
        wptr = [0]

        def project(m, dst, stride, off):
            b0, nb_ = MIX_BLK[m]
            for bi in range(nb_):
                i = wptr[0]
                wissue(i + NRING - 1)
                wt = wbf[i % NRING]
                pb = pbank()
                M = 16 if (m == "gla" and bi == 6) else 128
                for kc in range(8):
                    kb.mm(pb[0:M, 0:MT], wt[:, kc * 128:kc * 128 + M], hT[:, kc * MT:(kc + 1) * MT],
                          start=(kc == 0), stop=(kc == 7))
                kb.evac(dst[0:M, bi * stride + off: bi * stride + off + MT], pb[0:M, 0:MT])
                wptr[0] += 1
                if (bi + 1) % KPB == 0 or bi == nb_ - 1:
                    yield

        def step_stream(clock, g_):
            kb.step_max = 0.0
            try:
                next(g_)
            except StopIteration:
                return False
            if kb.step_max > 0.0:
                clock[g_] = max(clock[g_], kb.step_max)
            else:
                others = [v for k, v in clock.items() if k is not g_]
                clock[g_] = max(clock[g_], min(others) if others else 0.0) + 0.3
            return True

        def run_streams(gens):
            gens = list(gens)
            clock = {g_: 0.0 for g_ in gens}
            while gens:
                g_ = min(gens, key=lambda x: clock[x])
                if not step_stream(clock, g_):
                    gens.remove(g_)
                    del clock[g_]
        pendingT = [None]

        for l in range(L):
            src_d = x_d if l == 0 else mid_d
            dst_d = out_d if l == L - 1 else mid_d
            kb.dma("sp", bcp[:], bcp_d[l:l + 1, :].partition_broadcast(128))
            kb.dma("sp", ppar[:], ppar_d[l * 128:(l + 1) * 128, :])
            kb.dma("gq", wab[:], wab_d[l * 128:(l + 1) * 128, :])
            for q8 in range(8):
                kb.dma("gq", wout[:, q8 * 1024:(q8 + 1) * 1024], wout_d[l * 128:(l + 1) * 128, q8 * 1024:(q8 + 1) * 1024])
            kb.dma("gq", wup[:], wup_d[l * 128:(l + 1) * 128, :])
            kb.dma("sp", gup[:], gup_d[l * 128:(l + 1) * 128, :])
            kb.act(nal[:, :], BCv("a_log"), AF.Exp)
            kb.ts(nal[:, :], nal[:, :], -1.0, ALU.mult)
            for b in range(NB):
                for m_ in st:
                    kb.memset(st[m_][:, :], 0.0)
                for m_ in stb:
                    kb.memset(stb[m_][:, :], 0.0)
                for m_ in hist:
                    kb.memset(hist[m_][:, :], 0.0)
                for mt in range(nmt):
                    tok0 = b * T + mt * MT
                    def gen_H1(tok0=tok0):
                        for j in range(MT // 128):
                            xt = xts[j % 2]
                            r0 = tok0 + j * 128
                            kb.dma("sp", xt[:], src_d[r0:r0 + 128, :])
                            ss = S("n_ss", 8)
                            junk = P_g
                            kb.act(junk[:, 0:1024], xt[:, :], AF.Square)
                            kb.red(ss[:, 0:1], junk[:, 0:1024])
                            rsqrt_small(ss[:, 1:2], ss[:, 0:1], 1.0 / D, EPS)
                            kb.ts(ht[:, :], xt[:, :], ss[:, 1:2], ALU.mult)
                            yield
                            for half in range(2):
                                pb = bank()
                                for q in range(4):
                                    kc = half * 4 + q
                                    kb.tr(pb[:, q * 128:(q + 1) * 128], ht[:, kc * 128:(kc + 1) * 128], ident)
                                outv = hT.v(hT.h[:, :].rearrange("p (k t) -> p k t", k=8)[:, half * 4:half * 4 + 4,
                                                                                          j * 128:(j + 1) * 128])
                                kb.tt(outv, r3(pb, 4), bc_t(ppar, half * 4, 4, 128), ALU.mult)
                                yield
                        if "sc" in mixers:
                            yield from project("sc", P_s, MT + 4, 4)
                        if "gla" in mixers:
                            yield from project("gla", P_a, MT + 4, 4)
                    def gen_sc():
                        W2 = MT + 4
                        W2 = MT + 4
                        for blk in range(2):
                            u = S("sc_u%d" % blk, MT + 2)
                            kb.copy("g", u[:, 0:2], hist["sc"][:, blk * 2:blk * 2 + 2])
                            kb.tt(u[:, 2:MT + 2], P_s[:, (2 + blk) * W2 + 4:(2 + blk) * W2 + 4 + MT],
                                  P_s[:, (4 + blk) * W2 + 4:(4 + blk) * W2 + 4 + MT], ALU.mult)
                            kb.copy("g", hist["sc"][:, blk * 2:blk * 2 + 2], u[:, MT:MT + 2])
                            acc_t = S("sc_acc")
                            acc = acc_t[:, 0:MT]
                            kb.ts(acc, u[:, 0:MT], ppar[:, 41 + blk * 3:42 + blk * 3], ALU.mult)
                            for tap in (1, 2):
                                kb.stt(acc, u[:, tap:tap + MT], ppar[:, 41 + blk * 3 + tap:42 + blk * 3 + tap],
                                       acc, ALU.mult, ALU.add)
                            zs_t = S("sc_zs")
                            zs = zs_t[:, 0:MT]
                            kb.act(zs, P_s[:, (6 + blk) * W2 + 4:(6 + blk) * W2 + 4 + MT], AF.Silu)
                            kb.tt(acc, acc, P_s[:, blk * W2 + 4: blk * W2 + 4 + MT], ALU.mult)
                            kb.tt(yT[:, (4 + blk) * MT:(5 + blk) * MT], acc, zs, ALU.mult)
                            yield

                    def gen_gla():
                        W2 = MT + 4
                        W2 = MT + 4
                        for blk in range(2):
                            kb.act(ZS_a[:, blk * MT:(blk + 1) * MT], P_a[:, (4 + blk) * W2 + 4:(4 + blk) * W2 + 4 + MT],
                                   AF.Silu)
                        for j in range(MT // 128):
                            tsl = slice(j * 128, (j + 1) * 128)

                            def Pg(blk, rows=slice(0, 128)):
                                return P_a[rows, blk * W2 + 4 + j * 128: blk * W2 + 4 + (j + 1) * 128]
                            pb = bank()
                            kb.mm(pb[:, 0:128], Pg(6, slice(0, 16)), gup[0:16, :])
                            sp_ = S("gl_sp", 128)
                            kb.tt(sp_[:, :], pb[:, 0:128], BCv("gla_bias"), ALU.add)
                            yield
                            kb.act(sp_[:, :], sp_[:, :], AF.Exp, scale=-1.0)
                            yield
                            kb.act(sp_[:, :], sp_[:, :], AF.Ln, bias=1.0)
                            yield
                            pc = bank()
                            kb.mm(pc[:, 0:128], sp_[:, :], C("mcum"))
                            kb.mm(pc[:, 128:256], C("mrest"), sp_[:, :])
                            kb.mm(pc[:, 256:258], sp_[:, :], C("chunksel", 2))
                            kb.tr(pc[:, 384:512], Pg(1), ident)
                            ex = S("gl_ex", 512)
                            kb.act(ex[:, 0:128], pc[:, 0:128], AF.Exp, scale=-1.0 / 16)
                            kb.act(ex[:, 128:256], pc[:, 0:128], AF.Exp, scale=1.0 / 16)
                            kb.act(ex[:, 256:384], pc[:, 128:256], AF.Exp, scale=-1.0 / 16)
                            dS = S("gl_dS", 8)
                            kb.act(dS[:, 0:2], pc[:, 256:258], AF.Exp, scale=-1.0 / 16)
                            qe = S("gl_qe", 128); ke = S("gl_ke", 128); kend = S("gl_kend", 128)
                            kb.stt(qe[:, :], ex[:, 0:128], 32.0 ** -0.5, Pg(0), ALU.mult, ALU.mult)
                            kb.tt(ke[:, :], ex[:, 128:256], Pg(1), ALU.mult)
                            kb.tt(kend[:, :], ex[:, 256:384], pc[:, 384:512], ALU.mult)
                            yield
                            pv = bank()
                            for blk in range(2):
                                kb.tr(pv[:, blk * 128:(blk + 1) * 128], Pg(2 + blk), ident)
                            vtm = S("gl_v", 256)
                            kb.evac(vtm[:, :], pv[:, 0:256])
                            yield
                            qblk = S("gl_qblk")
                            kb.tt(r3(qblk, 4), bc_h(qe, 0, 128, 4),
                                  cst.v(cst.h[:, COFF["bq"]:COFF["bq"] + 512].rearrange("p (h t) -> p h t", h=4)),
                                  ALU.mult)
                            yield
                            pa = bank()
                            kb.mm(pa[:, :], ke[:, :], qblk[:, :])
                            attnT = S("gl_attn")
                            kb.tt(r3(attnT, 4), r3(pa, 4), Cb4("m_incl"), ALU.mult)
                            yield
                            Sg = st["gla"]
                            po = banks[3]
                            for c in range(2):
                                cs = slice(64 * c, 64 * c + 64)
                                kb.mm(po[cs, 256:512], qe[:, cs], Sg[:, :], start=True, stop=False)
                                for h in range(4):
                                    kb.mm(po[cs, 256 + h * 64:256 + (h + 1) * 64], attnT[cs, h * 128 + 64 * c:h * 128 + 64 * c + 64],
                                          vtm[cs, h * 64:(h + 1) * 64], start=False, stop=(h == 3))
                                yield
                                pst = bank()
                                kb.mm(pst[:, 0:256], kend[cs, :], vtm[cs, :])
                                tmp = S("gl_tmp", 256)
                                kb.tt(tmp[:, :], pst[:, 0:256], C("bs", 256), ALU.mult)
                                kb.stt(Sg[:, :], Sg[:, :], dS[:, c:c + 1], tmp[:, :], ALU.mult, ALU.add)
                                yield
                            osb = S("gl_o", 256)
                            kb.evac(osb[:, :], po[:, 256:512])
                            yield from head_norm_gate(osb, "gla_norm_w", 6, tsl, ZS_a, "gl_")
                            yield

                    def gen_gdn():
                        W2 = MT + 4
                        W2 = MT + 4
                        for blk in range(6):
                            kb.copy("g", P_g[:, blk * W2 + 1: blk * W2 + 4], hist["gdn"][:, blk * 3:blk * 3 + 3])
                            kb.copy("g", hist["gdn"][:, blk * 3:blk * 3 + 3], P_g[:, blk * W2 + 1 + MT: blk * W2 + 4 + MT])
                            acc = S("cv_acc%d" % (blk % 2), MT)[:, 0:MT]
                            kb.ts(acc, P_g[:, blk * W2 + 1: blk * W2 + 1 + MT], ppar[:, 8 + blk * 4:9 + blk * 4], ALU.mult)
                            for tap in (1, 2, 3):
                                yield
                                kb.stt(acc, P_g[:, blk * W2 + 1 + tap: blk * W2 + 1 + tap + MT],
                                       ppar[:, 8 + blk * 4 + tap:9 + blk * 4 + tap], acc, ALU.mult, ALU.add)
                            yield
                            kb.act(Q_g[:, blk * MT:(blk + 1) * MT], acc, AF.Silu)
                            yield
                        for blk in range(2):
                            kb.act(ZS_g[:, blk * MT:(blk + 1) * MT], P_g[:, (6 + blk) * W2 + 4:(6 + blk) * W2 + 4 + MT],
                                   AF.Silu)
                        for j in range(MT // 128):
                            tsl = slice(j * 128, (j + 1) * 128)

                            def Qg(blk, rows=slice(0, 128)):
                                return Q_g[rows, blk * MT + j * 128: blk * MT + (j + 1) * 128]
                            pq = bank(); pv = bank()
                            for blk in range(4):
                                kb.tr(pq[:, blk * 128:(blk + 1) * 128], Qg(blk), ident)
                            for blk in range(2):
                                kb.tr(pv[:, blk * 128:(blk + 1) * 128], Qg(4 + blk), ident)
                            qk = S("gd_qk"); vtm = S("gd_v", 256, BF16)
                            kb.evac(qk[:, :], pq[:, :])
                            kb.evac(vtm[:, :], pv[:, 0:256])
                            yield
                            pab = bank()
                            for kc in range(8):
                                kb.mm(pab[:, 0:8], hT[:, kc * MT + j * 128: kc * MT + (j + 1) * 128],
                                      wab[:, kc * 8:(kc + 1) * 8], start=(kc == 0), stop=(kc == 7))
                            sc = S("gd_sc", 64)
                            kb.tt(sc[:, 0:4], pab[:, 0:4], BCv("dt_bias"), ALU.add)
                            kb.act(sc[:, 4:8], pab[:, 4:8], AF.Exp, scale=-1.0)
                            yield
                            kb.act(sc[:, 0:4], sc[:, 0:4], AF.Exp)
                            yield
                            kb.act(sc[:, 0:4], sc[:, 0:4], AF.Ln, bias=1.0)
                            yield
                            kb.tt(sc[:, 0:4], sc[:, 0:4], nal[:, :], ALU.mult)
                            kb.act(sc[:, 4:8], sc[:, 4:8], AF.Ln, bias=1.0)
                            yield
                            kb.act(sc[:, 8:12], sc[:, 4:8], AF.Exp, scale=-1.0)
                            sq = S("gd_sq")
                            kb.tt(sq[:, :], qk[:, :], qk[:, :], ALU.mult)
                            yield
                            kb.red(sc[:, 12:20], r3(sq, 8))
                            yield
                            kb.act(sc[:, 20:28], sc[:, 12:20], AF.Ln, bias=EPS)
                            yield
                            kb.ts(sc[:, 20:28], sc[:, 20:28], -0.5, ALU.mult)
                            kb.ts(sc[:, 20:24], sc[:, 20:24], math.log(1.0 / 8.0), ALU.add)
                            yield
                            pg = bank()
                            kb.mm(pg[:, 0:4], C("mcum"), sc[:, 0:4])
                            kb.mm(pg[:, 4:8], C("mrest"), sc[:, 0:4])
                            gb = S("gd_gb", 256)
                            kb.copy("v", r3(gb, 4), bc_t(sc, 0, 4, 64))
                            for cb in range(2):
                                kb.mm(pg[:, 8 + 2 * cb: 10 + 2 * cb], gb[:, cb * 128:(cb + 1) * 128], C("chunksel", 2))
                            dS = S("gd_dS", 8)
                            kb.act(dS[:, 0:4], pg[:, 8:12], AF.Exp)
                            kb.copy("v", sc[:, 28:32], pg[:, 0:4])
                            kb.tt(sc[:, 52:56], pg[:, 4:8], sc[:, 24:28], ALU.add)
                            yield
                            kb.tt(sc[:, 32:36], sc[:, 28:32], sc[:, 4:8], ALU.subtract)
                            kb.act(sc[:, 52:56], sc[:, 52:56], AF.Exp)
                            yield
                            kb.tt(sc[:, 32:36], sc[:, 32:36], sc[:, 24:28], ALU.add)
                            kb.tt(sc[:, 36:40], sc[:, 24:28], sc[:, 28:32], ALU.subtract)
                            yield
                            kb.tt(sc[:, 40:44], sc[:, 28:32], sc[:, 20:24], ALU.add)
                            yield
                            kb.act(sc[:, 44:48], sc[:, 40:44], AF.Exp)
                            kb.act(sc[:, 48:52], sc[:, 32:36], AF.Exp)
                            yield
                            UW = S("gd_UW", 512, BF16)
                            kb.tt(Uv(UW), r3(vtm, 4), bc_t(sc, 8, 4, 64), ALU.mult)
                            yield
                            kb.tt(Wv(UW), r3(qk, 4, 256, 512), bc_t(sc, 48, 4, 64), ALU.mult)
                            yield
                            kend = S("gd_kend", 256, BF16)
                            kb.tt(r3(kend, 4), r3(qk, 4, 256, 512), bc_t(sc, 52, 4, 64), ALU.mult)
                            yield
                            dq = S("gd_dq")
                            kb.tt(r3(dq, 4), Cb4("ident"), bc_t(sc, 44, 4, 128), ALU.mult)
                            yield
                            pqg = bank()
                            for h in range(4):
                                pb_, cb = 64 * (h % 2), h // 2
                                kb.mm(pqg[pb_:pb_ + 64, cb * 128:(cb + 1) * 128], qk[:, h * 64:(h + 1) * 64],
                                      dq[:, h * 128:(h + 1) * 128])
                            fm = S("gd_fm", 512, BF16)
                            kb.evac(fm[:, 0:256], pqg[:, 0:256])
                            yield
                            rd = S("gd_rd"); cbm = S("gd_cb")
                            kb.tt(r3(rd, 4), Cb4("ident"), bc_t(sc, 32, 4, 128), ALU.mult)
                            yield
                            kb.tt(r3(cbm, 4), Cb4("nm_strict"), bc_t(sc, 36, 4, 128), ALU.add)
                            yield
                            pe1 = bank()
                            kb.mm(pe1[:, :], C("ones"), rd[:, :], start=True, stop=False)
                            kb.mm(pe1[:, :], ident, cbm[:, :], start=False, stop=True)
                            DTs = S("gd_DTs")
                            kb.act(DTs[:, :], pe1[:, :], AF.Exp)
                            yield
                            rd2 = S("gd_rd"); cb2 = S("gd_cb")
                            kb.tt(r3(rd2, 4), Cb4("ident"), bc_t(sc, 40, 4, 128), ALU.mult)
                            yield
                            kb.tt(r3(cb2, 4), Cb4("nm_incl"), bc_t(sc, 36, 4, 128), ALU.add)
                            yield
                            pe2 = bank()
                            kb.mm(pe2[:, :], C("ones"), rd2[:, :], start=True, stop=False)
                            kb.mm(pe2[:, :], ident, cb2[:, :], start=False, stop=True)
                            DTi = S("gd_DTi")
                            kb.act(DTi[:, :], pe2[:, :], AF.Exp)
                            yield
                            Nm = S("gd_N", 512, BF16); attnT = S("gd_attn", 512, BF16)
                            pkk = bank()
                            for h in HO:
                                pb_, cb = 64 * (h % 2), h // 2
                                kT = Qg(2 + cb, slice(pb_, pb_ + 64))
                                kb.mm(pkk[:, h * 128:(h + 1) * 128], kT, kT)
                            kb.tt(Nm[:, :], pkk[:, :], DTs[:, :], ALU.mult)
                            yield
                            pqk = bank()
                            for h in HO:
                                pb_, cb = 64 * (h % 2), h // 2
                                kT = Qg(2 + cb, slice(pb_, pb_ + 64))
                                qT_ = Qg(cb, slice(pb_, pb_ + 64))
                                kb.mm(pqk[:, h * 128:(h + 1) * 128], kT, qT_)
                            kb.tt(attnT[:, :], pqk[:, :], DTi[:, :], ALU.mult)
                            yield
                            yield from solve(Nm, UW, "gd_")
                            pwt = bank()
                            for h in range(4):
                                pb_, cb = 64 * (h % 2), h // 2
                                kb.tr(pwt[pb_:pb_ + 64, cb * 128:(cb + 1) * 128], UW[:, h * 128 + 64:(h + 1) * 128], ident)
                            kb.evac(fm[:, 256:512], pwt[:, 0:256])
                            yield
                            po = banks[2]
                            yield from seq_core("gd_", po, st["gdn"], stb["gdn"], UW, -1, fm, 256, fm, 0, attnT, kend, dS)
                            osb = S("gd_o", 256)
                            kb.evac(osb[:, :], po[:, 0:256])
                            yield from head_norm_gate(osb, "gdn_norm_w", 0, tsl, ZS_g, "gd_")
                            yield

                    def gen_rwkv():
                        W2 = MT + 4
                        W2 = MT + 4
                        for blk in range(9):
                            kb.copy("g", P_r[:, blk * W2 + 3: blk * W2 + 4], hist["rwkv"][:, blk:blk + 1])
                            kb.copy("g", hist["rwkv"][:, blk:blk + 1], P_r[:, blk * W2 + 3 + MT: blk * W2 + 4 + MT])
                            dlt_t = S("rw_d")
                            dlt = dlt_t[:, 0:MT]
                            kb.tt(dlt, P_r[:, blk * W2 + 3: blk * W2 + 3 + MT], P_r[:, blk * W2 + 4: blk * W2 + 4 + MT],
                                  ALU.subtract)
                            yield
                            kb.stt(Q_r[:, blk * MT:(blk + 1) * MT], dlt, ppar[:, 32 + blk:33 + blk],
                                   P_r[:, blk * W2 + 4: blk * W2 + 4 + MT], ALU.mult, ALU.add)
                            yield
                        for blk in range(2):
                            kb.act(ZS_r[:, blk * MT:(blk + 1) * MT], Q_r[:, (6 + blk) * MT:(7 + blk) * MT], AF.Silu)
                        kb.act(Q_r[0:64, 8 * MT:9 * MT], Q_r[0:64, 8 * MT:9 * MT], AF.Tanh)
                        for j in range(MT // 128):
                            tsl = slice(j * 128, (j + 1) * 128)

                            def Qr(blk, rows=slice(0, 128)):
                                return Q_r[rows, blk * MT + j * 128: blk * MT + (j + 1) * 128]
                            prk = bank(); pv = bank()
                            for blk in range(4):
                                kb.tr(prk[:, blk * 128:(blk + 1) * 128], Qr(blk), ident)
                            for blk in range(2):
                                kb.tr(pv[:, blk * 128:(blk + 1) * 128], Qr(4 + blk), ident)
                            rk = S("rw_rk"); vtm = S("rw_v", 256, BF16)
                            kb.evac(rk[:, :], prk[:, :])
                            kb.evac(vtm[:, :], pv[:, 0:256])
                            yield
                            pwa = bank()
                            kb.mm(pwa[:, 0:256], Qr(8, slice(0, 64)), wup[0:64, :])
                            kb.mm(pwa[:, 256:512], Qr(8, slice(64, 128)), wup[64:128, :])
                            lw = S("rw_lw", 256); a_ = S("rw_a", 256)
                            kb.tt(lw[:, :], pwa[:, 0:256], BCv("w0"), ALU.add)
                            kb.tt(a_[:, :], pwa[:, 256:512], BCv("a0"), ALU.add)
                            yield
                            kb.act(lw[:, :], lw[:, :], AF.Sigmoid)
                            kb.act(a_[:, :], a_[:, :], AF.Sigmoid)
                            yield
                            kb.ts(lw[:, :], lw[:, :], -math.exp(-0.5), ALU.mult)
                            kk = S("rw_kk", 256); sq = S("rw_sq", 256); sc = S("rw_sc", 32)
                            yield
                            kb.tt(kk[:, :], rk[:, 256:512], BCv("k_k"), ALU.mult)
                            kb.tt(sq[:, :], kk[:, :], kk[:, :], ALU.mult)
                            yield
                            kb.red(sc[:, 0:4], r3(sq, 4))
                            yield
                            rsqrt_small(sc[:, 4:8], sc[:, 0:4], 1.0, EPS)
                            yield
                            kb.tt(r3(kk, 4), r3(kk, 4), bc_t(sc, 4, 4, 64), ALU.mult)
                            km = S("rw_km", 256)
                            yield
                            kb.stt(km[:, :], a_[:, :], -1.0, BCv("k_a"), ALU.add, ALU.mult)
                            yield
                            kb.stt(km[:, :], km[:, :], 1.0, rk[:, 256:512], ALU.add, ALU.mult)
                            yield
                            bb = S("rw_b", 256)
                            kb.tt(bb[:, :], kk[:, :], a_[:, :], ALU.mult)
                            yield
                            kb.tt(sq[:, :], rk[:, 0:256], km[:, :], ALU.mult)
                            yield
                            kb.tt(sq[:, :], sq[:, :], BCv("r_k"), ALU.mult)
                            kb.red(sc[:, 8:12], r3(sq, 4))
                            yield
                            pc1 = bank(); pc2 = bank()
                            kb.mm(pc1[:, 0:256], C("mcum"), lw[:, :])
                            kb.mm(pc1[:, 256:512], C("mcumx"), lw[:, :])
                            kb.mm(pc2[:, 0:256], C("mrest"), lw[:, :])
                            for h in range(4):
                                pb_, cb = 64 * (h % 2), h // 2
                                kb.mm(pc2[pb_:pb_ + 64, 256 + 2 * cb: 258 + 2 * cb], lw[:, h * 64:(h + 1) * 64],
                                      C("chunksel", 2))
                            dS = S("rw_dS", 8)
                            kb.act(dS[:, 0:4], pc2[:, 256:260], AF.Exp)
                            eg = S("rw_eg"); er = S("rw_er", 256); eng_ = S("rw_eng", 256)
                            kb.act(eg[:, :], pc1[:, :], AF.Exp)
                            kb.act(eng_[:, :], pc1[:, 0:256], AF.Exp, scale=-1.0)
                            kb.act(er[:, :], pc2[:, 0:256], AF.Exp)
                            TA = S("rw_TA", 512, BF16); TB = S("rw_TB", 512, BF16)
                            yield
                            kb.tt(TA[:, 0:256], rk[:, 0:256], eg[:, 0:256], ALU.mult)
                            kb.tt(TA[:, 256:512], km[:, :], eng_[:, :], ALU.mult)
                            yield
                            kb.tt(TB[:, 0:256], bb[:, :], eng_[:, :], ALU.mult)
                            kb.tt(TB[:, 256:512], kk[:, :], eg[:, 256:512], ALU.mult)
                            yield
                            kend2 = S("rw_kend2", 256, BF16); kend = S("rw_kend", 256, BF16)
                            yield
                            kb.tt(kend2[:, :], km[:, :], er[:, :], ALU.mult)
                            kb.stt(kend[:, :], bb[:, :], -1.0, er[:, :], ALU.mult, ALU.mult)
                            yield
                            FA = S("rw_FA", 512, BF16); FB = S("rw_FB", 512, BF16)
                            pfa = bank()
                            for q in range(4):
                                kb.tr(pfa[:, q * 128:(q + 1) * 128], TA[:, q * 128:(q + 1) * 128], ident)
                            kb.evac(FA[:, :], pfa[:, :])
                            yield
                            pfb = bank()
                            for q in range(4):
                                kb.tr(pfb[:, q * 128:(q + 1) * 128], TB[:, q * 128:(q + 1) * 128], ident)
                            kb.evac(FB[:, :], pfb[:, :])
                            yield
                            Nm = S("rw_N", 512, BF16); AbkT = S("rw_Abk", 512, BF16); attn2T = S("rw_attn2", 512, BF16); attnT = S("rw_attn", 512, BF16)
                            pn = bank(); pbk = bank()
                            for h in HO:
                                hs = slice(h * 128, (h + 1) * 128)
                                KT = fmh(FA, h, 0, 128, 256)
                                BT = fmh(FB, h, 0, 128, 0); KKT = fmh(FB, h, 0, 128, 256)
                                kb.mm(pn[:, hs], BT, KKT)
                                kb.mm(pbk[:, hs], KT, KKT)
                            kb.tt(r3(Nm, 4), r3(pn, 4), Cb4("m_strict"), ALU.mult)
                            kb.tt(r3(AbkT, 4), r3(pbk, 4), Cb4("m_strict"), ALU.mult)
                            yield
                            prk2 = bank(); prb = bank()
                            for h in HO:
                                hs = slice(h * 128, (h + 1) * 128)
                                RT = fmh(FA, h, 0, 128, 0); KT = fmh(FA, h, 0, 128, 256)
                                BT = fmh(FB, h, 0, 128, 0)
                                kb.mm(prk2[:, hs], KT, RT)
                                kb.mm(prb[:, hs], BT, RT)
                            kb.tt(r3(attn2T, 4), r3(prk2, 4), Cb4("m_incl"), ALU.mult)
                            kb.tt(r3(attnT, 4), r3(prb, 4), Cb4("m_incl_neg"), ALU.mult)
                            yield
                            UW = S("rw_UW", 512, BF16)
                            pu = bank()
                            for h in range(4):
                                kb.mm(pu[:, h * 64:(h + 1) * 64], AbkT[:, h * 128:(h + 1) * 128], vtm[:, h * 64:(h + 1) * 64])
                            kb.evac(Uv(UW), r3(pu, 4, 0, 256))
                            kb.copy("v", Wv(UW), r3(TB, 4, 256, 512))
                            yield
                            yield from solve(Nm, UW, "rw_")
                            pwt = bank()
                            for h in range(4):
                                pb_, cb = 64 * (h % 2), h // 2
                                kb.tr(pwt[pb_:pb_ + 64, cb * 128:(cb + 1) * 128], UW[:, h * 128 + 64:(h + 1) * 128], ident)
                            WT = S("rw_WT", 256, BF16)
                            kb.evac(WT[:, :], pwt[:, 0:256])
                            yield
                            po = banks[3]
                            yield from seq_core("rw_", po, st["rwkv"], stb["rwkv"], UW, +1, WT, 0, FA, 0, attnT, kend, dS, attn2T, kend2, vtm)
                            osb = S("rw_o", 256); cen = S("rw_cen", 256)
                            kb.evac(osb[:, :], po[:, 0:256])
                            yield
                            kb.red(sc[:, 12:16], r3(osb, 4))
                            yield
                            kb.ts(sc[:, 12:16], sc[:, 12:16], 1.0 / 64, ALU.mult)
                            yield
                            kb.tt(r3(cen, 4), r3(osb, 4), bc_t(sc, 12, 4, 64), ALU.subtract)
                            yield
                            kb.tt(sq[:, :], cen[:, :], cen[:, :], ALU.mult)
                            yield
                            kb.red(sc[:, 16:20], r3(sq, 4))
                            yield
                            rsqrt_small(sc[:, 20:24], sc[:, 16:20], 1.0 / 64, 64e-5)
                            yield
                            kb.tt(r3(cen, 4), r3(cen, 4), bc_t(sc, 20, 4, 64), ALU.mult)
                            yield
                            kb.tt(cen[:, :], cen[:, :], BCv("ln_w"), ALU.mult)
                            yield
                            kb.tt(cen[:, :], cen[:, :], BCv("ln_b"), ALU.add)
                            yield
                            kb.tt(r3(sq, 4), r3(vtm, 4), bc_t(sc, 8, 4, 64), ALU.mult)
                            yield
                            ob = S("rw_hn_ob", 256, BF16)
                            kb.tt(ob[:, :], cen[:, :], sq[:, :], ALU.add)
                            to_fm_gate(ob, 2, tsl, ZS_r)
                            yield


                    proj_done = {}

                    def gen_P():
                        if "gdn" in mixers:
                            yield from project("gdn", P_g, MT + 4, 4)
                        else:
                            kb.memset(yT[:, 0:2 * MT], 0.0)
                        proj_done["gdn"] = True
                        if "rwkv" in mixers:
                            yield from project("rwkv", P_r, MT + 4, 4)
                        else:
                            kb.memset(yT[:, 2 * MT:4 * MT], 0.0)
                        proj_done["rwkv"] = True
                    run_streams([gen_H1()] + ([pendingT[0]] if pendingT[0] is not None else []))
                    pendingT[0] = None
                    active = [gen_P()]
                    if "sc" in mixers:
                        active.append(gen_sc())
                    if "gla" in mixers:
                        active.append(gen_gla())
                    pending = {}
                    if "gdn" in mixers:
                        pending["gdn"] = gen_gdn
                    if "rwkv" in mixers:
                        pending["rwkv"] = gen_rwkv
                    clock = {g_: 0.0 for g_ in active}
                    while active or pending:
                        g_ = min(active, key=lambda x: clock[x])
                        if not step_stream(clock, g_):
                            active.remove(g_)
                            del clock[g_]
                        for m_ in list(pending):
                            if proj_done.get(m_):
                                gn_ = pending.pop(m_)()
                                active.append(gn_)
                                clock[gn_] = (min(clock.values()) if clock else 0.0) + (GOFF if m_ == "gdn" else 0.0)
                    for m_, (y0, y1) in (("sc", (4, 6)), ("gla", (6, 8))):
                        if m_ not in mixers:
                            kb.memset(yT[:, y0 * MT:y1 * MT], 0.0)
                    def gen_T(tok0=tok0, src_d=src_d, dst_d=dst_d):
                        for j in range(MT // 128):
                            r0 = tok0 + j * 128
                            kb.dma("sp", xres[:], src_d[r0:r0 + 128, :])
                            osb = S("op_o", 1024)
                            for half in range(2):
                                pb = pbank()
                                for kc in range(8):
                                    kb.mm(pb[:, :], yT[:, kc * MT + j * 128: kc * MT + (j + 1) * 128],
                                          wout[:, kc * 1024 + half * 512: kc * 1024 + (half + 1) * 512],
                                          start=(kc == 0), stop=(kc == 7))
                                kb.evac(osb[:, half * 512:(half + 1) * 512], pb[:, :])
                                yield
                            ss = S("t_ss", 8)
                            junk = P_r
                            kb.act(junk[:, 0:1024], osb[:, :], AF.Square)
                            kb.red(ss[:, 2:3], junk[:, 0:1024])
                            rsqrt_small(ss[:, 3:4], ss[:, 2:3], 1.0 / D, EPS)
                            yield
                            kb.stt(osb[:, :], osb[:, :], ss[:, 3:4], BCv("post_w"), ALU.mult, ALU.mult)
                            kb.tt(osb[:, :], osb[:, :], xres[:, :], ALU.add)
                            kb.dma("sp", dst_d[r0:r0 + 128, :], osb[:])
                            yield
                    pendingT[0] = gen_T()
            if pendingT[0] is not None:
                run_streams([pendingT[0]])
                pendingT[0] = None
            if l < L - 1:
                kb.wait_all("g")
                kb.op("g", lambda o: o.memset(P_r.h[:, 0:1], 0.0), [], [P_r[:, 0:1]])
                e = kb.engs["g"]
                kb.dq_eng["sp"].obj.wait_ge(e.sem, e.cnt)
                kb.dq_eng["sp"].waited["g"] = e.cnt
        kb.wait_all("g")
    return nc


_CACHE = {}


def run(inputs, NB, T, L, ncores, mixers=("gdn", "rwkv", "sc", "gla")):
    key = (NB, T, L, tuple(mixers))
    if key not in _CACHE:
        _CACHE[key] = build(NB, T, L, mixers)
    nc = _CACHE[key]
    pl = prep_layer_inputs(inputs, L)
    x = np.ascontiguousarray(inputs["x"], dtype=np.float32)
    in_maps = []
    for c in range(ncores):
        m = {"x": x[c * NB:(c + 1) * NB].reshape(NB * T, D), "consts": CONSTS}
        m.update(pl)
        in_maps.append(m)
    res = run_bass_kernel_spmd(nc, in_maps, core_ids=list(range(ncores)))
    outs = [r["out"].reshape(NB, T, D) for r in res.results]
    return np.concatenate(outs, axis=0)


def kernel(**inputs):
    inputs = {k: np.asarray(v) for k, v in inputs.items()}
    B, T, _ = inputs["x"].shape
    L = inputs["w_in"].shape[0]
    return run(inputs, B // NCORES, T, L, NCORES).astype(np.float32)
```

```python
import contextlib
import math
import os
DBG = float(os.environ.get('KDBG', '99'))
GOFF = float(os.environ.get("KGOFF", "0"))
LAT = float(os.environ.get("KLAT", "0.25"))
KPB = int(os.environ.get("KPB", "3"))
EVR = int(os.environ.get("KEVR", "4"))
import numpy as np
import concourse.bass as bass
import concourse.mybir as mybir
from concourse.bass_utils import run_bass_kernel_spmd

F32 = mybir.dt.float32
BF16 = mybir.dt.bfloat16
ALU = mybir.AluOpType
AF = mybir.ActivationFunctionType
AX = mybir.AxisListType

D = 1024
NCORES = 8
MT = 256
EPS = 1e-6
NEG = -30000.0


class Trk:
    __slots__ = ("w", "r", "tw", "tr")

    def __init__(self):
        self.w = None
        self.r = []
        self.tw = 0.0
        self.tr = 0.0


class V:
    __slots__ = ("ap", "trk")

    def __init__(self, ap, trk):
        self.ap = ap
        self.trk = trk


class Tl:
    def __init__(self, handle):
        self.h = handle
        self.trk = Trk()

    def __getitem__(self, key):
        return V(self.h[key], [self.trk])

    def v(self, ap):
        return V(ap, [self.trk])


class Eng:
    def __init__(self, name, obj, sem):
        self.name = name
        self.obj = obj
        self.sem = sem
        self.cnt = 0
        self.waited = {}


class KB:
    def __init__(self, nc, ctx):
        self.nc = nc
        self.ctx = ctx
        self.engs = {}
        for nm, obj in (("pe", nc.tensor), ("v", nc.vector), ("s", nc.scalar), ("g", nc.gpsimd)):
            sem = ctx.enter_context(nc.semaphore("sem_" + nm))
            self.engs[nm] = Eng(nm, obj, sem)
        self.dmaq = {"sp": nc.sync, "gq": nc.gpsimd}
        self.dq_eng = {"sp": Eng("spq", nc.sync, None), "gq": self.engs["g"]}
        self.dsems = {}
        for q in ("sp", "gq"):
            lst = []
            for i in range(8):
                sem = ctx.enter_context(nc.semaphore("dsem_%s%d" % (q, i)))
                e = Eng("d_%s%d" % (q, i), None, sem)
                self.engs[e.name] = e
                lst.append(e)
            self.dsems[q] = lst
        self.drr = {"sp": 0, "gq": 0}
        self.flip = 0
        self.efree = {}
        self.step_max = 0.0

    def sb(self, name, shape, dtype=F32):
        return Tl(self.ctx.enter_context(self.nc.sbuf_tensor("sb_" + name, list(shape), dtype)))

    def ps(self, name, shape, dtype=F32):
        return Tl(self.ctx.enter_context(self.nc.psum_tensor("ps_" + name, list(shape), dtype)))

    def _deps(self, issuer, reads, writes):
        deps = {}
        me = issuer.name

        def add(d, raw):
            if d is None:
                return
            e, c = d
            if e == me and (me == "pe" or not raw):
                return
            if deps.get(e, 0) < c:
                deps[e] = c
        for v in reads:
            for t in v.trk:
                add(t.w, True)
        for v in writes:
            for t in v.trk:
                add(t.w, False)
                for r in t.r:
                    add(r, False)
        for e, c in deps.items():
            if issuer.waited.get(e, 0) < c:
                issuer.obj.wait_ge(self.engs[e].sem, c)
                issuer.waited[e] = c

    def _mark(self, ident, reads, writes):
        for v in reads:
            for t in v.trk:
                t.r.append(ident)
        for v in writes:
            for t in v.trk:
                t.w = ident
                t.r = []

    def _est(self, eng, cost, reads, writes):
        dep = 0.0
        for v in reads:
            for t in v.trk:
                dep = max(dep, t.tw)
        for v in writes:
            for t in v.trk:
                dep = max(dep, t.tw, t.tr)
        start = max(self.efree.get(eng, 0.0), dep + LAT)
        fin = start + cost
        self.efree[eng] = fin
        for v in reads:
            for t in v.trk:
                t.tr = max(t.tr, fin)
        for v in writes:
            for t in v.trk:
                t.tw = fin
        self.step_max = max(self.step_max, fin)

    @staticmethod
    def _fsize(v):
        try:
            return float(v.ap.free_size())
        except Exception:
            return 256.0

    def op(self, eng, fn, reads, writes, cost=None):
        e = self.engs[eng]
        if cost is None:
            n = self._fsize(writes[0]) if writes else 256.0
            cost = {"pe": 0.07 + n / 1200.0, "v": 0.12 + n / 960.0, "s": 0.2 + n / 1200.0, "g": 0.15 + n / 500.0}[eng]
        self._est(eng, cost, reads, writes)
        self._deps(e, reads, writes)
        ins = fn(e.obj)
        e.cnt += 1
        ins.then_inc(e.sem, 1)
        self._mark((e.name, e.cnt), reads, writes)
        return ins

    def dma(self, q, out, in_):
        issuer = self.dq_eng[q]
        reads = [in_] if isinstance(in_, V) else []
        writes = [out] if isinstance(out, V) else []
        self._deps(issuer, reads, writes)
        self._est("q_" + q, 0.6, [], [])
        self.efree["q_" + q] -= 0.0
        self._est("dma_" + q + str(self.drr[q] % 4), 2.5, reads, writes)
        de = self.dsems[q][self.drr[q] % 8]
        self.drr[q] += 1
        oap = out.ap if isinstance(out, V) else out
        iap = in_.ap if isinstance(in_, V) else in_
        ins = self.dmaq[q].dma_start(out=oap, in_=iap)
        de.cnt += 16
        ins.then_inc(de.sem, 16)
        self._mark((de.name, de.cnt), reads, writes)

    def wait_all(self, eng):
        e = self.engs[eng]
        for nm, o in self.engs.items():
            if o.cnt > 0 and e.waited.get(nm, 0) < o.cnt and nm != e.name:
                e.obj.wait_ge(o.sem, o.cnt)
                e.waited[nm] = o.cnt

    def _pe_rowkey(self, ap, out):
        key = (ap.base_partition(), ap.partition_size())
        e = self.engs["pe"]
        if not hasattr(self, "_bank_last"):
            self._bank_last = {}
        bid = id(out.trk[0])
        last = self._bank_last.get(bid)
        if last is not None and last[0] != key and e.waited.get("pe", 0) < last[1]:
            e.obj.wait_ge(e.sem, last[1])
            e.waited["pe"] = last[1]
        self._bank_last[bid] = (key, e.cnt + 1)

    def mm(self, out, lhsT, rhs, start=True, stop=True):
        rd = [lhsT, rhs] + ([] if start else [out])
        self._pe_rowkey(lhsT.ap, out)
        return self.op("pe", lambda o: o.matmul(out.ap, lhsT.ap, rhs.ap, start=start, stop=stop), rd, [out],
                       cost=0.07 + self._fsize(rhs) / 1200.0)

    def tr(self, out, in_, ident):
        if in_.ap.dtype == BF16:
            k = in_.ap.partition_size()
            return self.mm(out, in_, self.ident_bf[0:k, 0:k])
        self._pe_rowkey(in_.ap, out)
        return self.op("pe", lambda o: o.transpose(out.ap, in_.ap, ident.ap), [in_, ident], [out])

    def act(self, out, in_, func, bias=None, scale=1.0, accum=None):
        rd = [in_]
        wr = [out]
        kw = {}
        if isinstance(bias, V):
            rd.append(bias)
            kw["bias"] = bias.ap
        elif bias is not None:
            kw["bias"] = bias
        kw["scale"] = scale
        if accum is not None:
            kw["accum_out"] = accum.ap
            wr.append(accum)
        return self.op("s", lambda o: o.activation(out.ap, in_.ap, func, **kw), rd, wr)

    def copy(self, eng, out, in_):
        if eng == "s":
            return self.op("s", lambda o: o.copy(out.ap, in_.ap), [in_], [out])
        return self.op(eng, lambda o: o.tensor_copy(out.ap, in_.ap), [in_], [out])

    def evac(self, out, in_):
        self.flip = (self.flip + 1) % EVR
        return self.copy("s" if self.flip else "v", out, in_)

    def tt(self, out, a, b, op, eng="v"):
        return self.op(eng, lambda o: o.tensor_tensor(out.ap, a.ap, b.ap, op), [a, b], [out])

    def ts(self, out, a, s1, op0, s2=None, op1=None, eng="v"):
        rd = [a]
        x1, x2 = s1, s2
        if isinstance(s1, V):
            rd.append(s1)
            x1 = s1.ap
        if isinstance(s2, V):
            rd.append(s2)
            x2 = s2.ap
        if op1 is None:
            return self.op(eng, lambda o: o.tensor_single_scalar(out.ap, a.ap, x1, op0), rd, [out])
        return self.op(eng, lambda o: o.tensor_scalar(out.ap, a.ap, x1, x2, op0, op1), rd, [out])

    def stt(self, out, in0, scalar, in1, op0, op1, eng="v"):
        rd = [in0, in1]
        sc = scalar
        if isinstance(scalar, V):
            rd.append(scalar)
            sc = scalar.ap
        return self.op(eng, lambda o: o.scalar_tensor_tensor(out.ap, in0.ap, sc, in1.ap, op0, op1), rd, [out])

    def red(self, out, in_, eng="v"):
        return self.op(eng, lambda o: o.tensor_reduce(out.ap, in_.ap, AX.X, ALU.add), [in_], [out])

    def recip(self, out, in_):
        return self.op("v", lambda o: o.reciprocal(out.ap, in_.ap), [in_], [out])

    def memset(self, out, val, eng="v"):
        return self.op(eng, lambda o: o.memset(out.ap, val), [], [out])


def make_consts():
    i = np.arange(128)
    same = (i[:, None] // 64) == (i[None, :] // 64)
    c = {}
    c["ident"] = np.eye(128)
    c["ones"] = np.ones((128, 128))
    c["mcum"] = (same & (i[:, None] <= i[None, :])) * 1.0
    c["mcumx"] = (same & (i[:, None] < i[None, :])) * 1.0
    c["mrest"] = (same & (i[:, None] > i[None, :])) * 1.0
    c["m_strict"] = (same & (i[:, None] < i[None, :])) * 1.0
    c["m_incl"] = (same & (i[:, None] <= i[None, :])) * 1.0
    c["m_incl_neg"] = -c["m_incl"]
    c["nm_strict"] = np.where(c["m_strict"] > 0, 0.0, NEG)
    c["nm_incl"] = np.where(c["m_incl"] > 0, 0.0, NEG)
    cs = np.zeros((128, 128))
    cs[:64, 0] = 1.0
    cs[64:, 1] = 1.0
    c["chunksel"] = cs
    p = np.arange(128)
    bq = np.zeros((128, 4, 128))
    for h in range(4):
        bq[32 * h:32 * h + 32, h, :] = 1.0
    bs = np.zeros((128, 256))
    for h in range(4):
        bs[32 * h:32 * h + 32, 64 * h:64 * h + 64] = 1.0
    names = ["ident", "ones", "mcum", "mcumx", "mrest", "m_strict", "m_incl", "m_incl_neg", "nm_strict",
             "nm_incl", "chunksel"]
    arr = np.concatenate([c[n] for n in names] + [bq.reshape(128, 512), bs], axis=1).astype(np.float32)
    offs = {n: k * 128 for k, n in enumerate(names)}
    offs["bq"] = len(names) * 128
    offs["bs"] = len(names) * 128 + 512
    return arr, offs


CONSTS, COFF = make_consts()
NCONST = CONSTS.shape[1]

GW = 256
GDN0 = 0
RW0 = 4 * GW + 8
SC0 = RW0 + 4 * GW + 128
GLA0 = SC0 + 4 * GW


def block_cols():
    blocks = []
    for b in range(8):
        blocks.append(np.arange(GDN0 + b * 128, GDN0 + (b + 1) * 128))
    for b in range(9):
        blocks.append(np.arange(RW0 + b * 128, RW0 + (b + 1) * 128))
    for b in range(8):
        blocks.append(np.arange(SC0 + b * 128, SC0 + (b + 1) * 128))
    for b in range(6):
        blocks.append(np.arange(GLA0 + b * 128, GLA0 + (b + 1) * 128))
    last = np.full(128, -1)
    last[:16] = np.arange(GLA0 + 768, GLA0 + 784)
    blocks.append(last)
    return blocks


BLOCKS = block_cols()
NBLK = len(BLOCKS)
MIX_BLK = {"gdn": (0, 8), "rwkv": (8, 9), "sc": (17, 8), "gla": (25, 7)}

BC = {}
_o = 0
for _n, _w in (("gdn_norm_w", 64), ("gla_norm_w", 64), ("k_k", 256), ("k_a", 256), ("r_k", 256), ("ln_w", 256),
               ("ln_b", 256), ("w0", 256), ("a0", 256), ("gla_bias", 128), ("post_w", 1024), ("a_log", 4),
               ("dt_bias", 4)):
    BC[_n] = (_o, _w)
    _o += _w
NBC = _o


def prep_layer_inputs(inp, L):
    out = {}
    w_in = inp["w_in"]
    wb = np.zeros((L, NBLK, 128, 8, 128), np.float32)
    for bi, cols in enumerate(BLOCKS):
        valid = cols >= 0
        sub = w_in[:, :, cols[valid]]
        sub = sub.reshape(L, 8, 128, -1).transpose(0, 2, 1, 3)
        wb[:, bi, :, :, :sub.shape[-1]] = sub
    out["wblk"] = wb.reshape(L * NBLK * 128, 8 * 128)
    ab = w_in[:, :, GDN0 + 1024:GDN0 + 1032].reshape(L, 8, 128, 8).transpose(0, 2, 1, 3)
    out["wab"] = np.ascontiguousarray(ab).reshape(L * 128, 64)
    wo = inp["w_out"].reshape(L, 8, 128, 1024).transpose(0, 2, 1, 3)
    out["wout"] = np.ascontiguousarray(wo).reshape(L * 128, 8 * 1024)
    bc = np.zeros((L, NBC), np.float32)

    def put(n, a):
        o, w = BC[n]
        bc[:, o:o + w] = a
    put("gdn_norm_w", inp["gdn_norm_w"]); put("gla_norm_w", inp["gla_norm_w"])
    put("k_k", inp["rwkv_k_k"]); put("k_a", inp["rwkv_k_a"]); put("r_k", inp["rwkv_r_k"])
    put("ln_w", inp["rwkv_ln_w"]); put("ln_b", inp["rwkv_ln_b"]); put("w0", inp["rwkv_w0"]); put("a0", inp["rwkv_a0"])
    put("gla_bias", inp["gla_a_bias"]); put("post_w", inp["post_norm_w"])
    put("a_log", inp["gdn_a_log"]); put("dt_bias", inp["gdn_dt_bias"])
    out["bcp"] = bc
    pp = np.zeros((L, 128, 8 + 24 + 9 + 6), np.float32)
    pp[:, :, 0:8] = inp["pre_norm_w"].reshape(L, 8, 128).transpose(0, 2, 1)
    g = inp["gdn_conv_w"].reshape(L, 4, 6, 128).transpose(0, 3, 2, 1)
    pp[:, :, 8:32] = g.reshape(L, 128, 24)
    pp[:, :, 32:41] = inp["rwkv_mu"].reshape(L, 9, 128).transpose(0, 2, 1)
    s = inp["sc_conv_w"].reshape(L, 3, 2, 128).transpose(0, 3, 2, 1)
    pp[:, :, 41:47] = s.reshape(L, 128, 6)
    out["ppar"] = pp.reshape(L * 128, 47)
    up = np.zeros((L, 128, 256), np.float32)
    up[:, 0:64] = inp["rwkv_w_up"]
    up[:, 64:128] = inp["rwkv_a_up"]
    out["wup"] = up.reshape(L * 128, 256)
    gu = np.zeros((L, 128, 128), np.float32)
    gu[:, 0:16] = inp["gla_a_up"]
    out["gup"] = gu.reshape(L * 128, 128)
    return out


def build(NB, T, L, mixers=("gdn", "rwkv", "sc", "gla")):
    nc = bass.Bass("TRN2", target_bir_lowering=False)
    ntok = NB * T
    nmt = T // MT

    def din(name, shape):
        return nc.dram_tensor(name, list(shape), F32, kind="ExternalInput").ap()
    x_d = din("x", [ntok, D])
    wblk_d = din("wblk", [L * NBLK * 128, 1024])
    wab_d = din("wab", [L * 128, 64])
    wout_d = din("wout", [L * 128, 8192])
    bcp_d = din("bcp", [L, NBC])
    ppar_d = din("ppar", [L * 128, 47])
    wup_d = din("wup", [L * 128, 256])
    gup_d = din("gup", [L * 128, 128])
    cst_d = din("consts", [128, NCONST])
    out_d = nc.dram_tensor("out", [ntok, D], F32, kind="ExternalOutput").ap()
    mid_d = nc.dram_tensor("xmid", [ntok, D], F32).ap() if L > 1 else None

    with contextlib.ExitStack() as ctx:
        ctx.enter_context(nc.allow_low_precision("bf16 projection operands, fp32 accumulation"))
        kb = KB(nc, ctx)
        sb, ps = kb.sb, kb.ps
        cst = sb("cst", [128, NCONST])
        kb.dma("sp", cst[:], cst_d[:, :])

        def C(n, w=128):
            o = COFF[n]
            return cst[:, o:o + w]

        def Cb4(n):
            o = COFF[n]
            return cst.v(cst.h[:, o:o + 128].unsqueeze(1).to_broadcast([128, 4, 128]))
        ident = C("ident")
        ident_bf = sb("ident_bf", [128, 128], BF16)
        kb.copy("v", ident_bf[:, :], ident)
        kb.ident_bf = ident_bf

        bcp = sb("bcp", [128, NBC]); ppar = sb("ppar", [128, 47]); wab = sb("wab", [128, 64], BF16)
        wout = sb("wout", [128, 8192], BF16); wup = sb("wup", [128, 256], BF16); gup = sb("gup", [128, 128])
        nal = sb("nal", [128, 4])
        NRING = 8
        wbf = [sb("wbf%d" % i, [128, 1024], BF16) for i in range(NRING)]
        xts = [sb("xt%d" % i, [128, 1024]) for i in range(2)]
        ht = sb("ht", [128, 1024], BF16)
        hT = sb("hT", [128, 8 * MT], BF16)
        yT = sb("yT", [128, 8 * MT], BF16)
        P_s = sb("P_s", [128, 8 * (MT + 4)]); xres = sb("xres", [128, 1024])
        P_a = sb("P_a", [128, 8 * (MT + 4)]); P_g = sb("P_g", [128, 8 * (MT + 4)]); P_r = sb("P_r", [128, 9 * (MT + 4)])
        Q_g = sb("Q_g", [128, 6 * MT], BF16); Q_r = sb("Q_r", [128, 9 * MT], BF16)
        ZS_a = sb("ZS_a", [128, 2 * MT]); ZS_g = sb("ZS_g", [128, 2 * MT]); ZS_r = sb("ZS_r", [128, 2 * MT])
        st = {m: sb("st_" + m, [128, 256]) for m in ("gdn", "rwkv", "gla")}
        stb = {m: sb("stb_" + m, [128, 256], BF16) for m in ("gdn", "rwkv")}
        hist = {"gdn": sb("h_gdn", [128, 6 * 3]), "rwkv": sb("h_rwkv", [128, 9]), "sc": sb("h_sc", [128, 4])}
        banks = [ps("bk%d" % i, [128, 512]) for i in range(8)]
        bctr = [0]

        def bank():
            b = min(banks[4:8], key=lambda t: (max(t.trk.tw, t.trk.tr), id(t)))
            bctr[0] += 1
            b.trk.tr = max(b.trk.tr, kb.step_max, max(kb.efree.values()) if kb.efree else 0.0) + 1e-3
            return b
        pjctr = [0]

        def pbank():
            b = banks[pjctr[0] % 2]
            pjctr[0] += 1
            return b
        scr = {}

        ALIAS = {"sc_acc": "gl_qblk", "sc_zs": "gl_ex", "rw_d": "rw_eg"}

        def S(name, w=512, dt=F32):
            name = ALIAS.get(name, name)
            if name not in scr:
                scr[name] = sb("s_" + name, [128, w], dt)
            return scr[name]

        def BCv(n, rows=128):
            o, w = BC[n]
            return bcp[0:rows, o:o + w]

        def r3(tile, h, a=0, b=None):
            b = b if b is not None else tile.h.shape[1]
            return tile.v(tile.h[:, a:b].rearrange("p (h t) -> p h t", h=h))

        def bc_t(tile, a, nh, n):
            return tile.v(tile.h[:, a:a + nh].unsqueeze(2).to_broadcast([128, nh, n]))

        def bc_h(tile, a, w, nh):
            return tile.v(tile.h[:, a:a + w].unsqueeze(1).to_broadcast([128, nh, w]))

        def Uv(tile, rows=slice(0, 128)):
            return tile.v(tile.h[rows, 0:512].rearrange("p (h t) -> p h t", h=4)[:, :, 0:64])

        def Wv(tile, rows=slice(0, 128)):
            return tile.v(tile.h[rows, 0:512].rearrange("p (h t) -> p h t", h=4)[:, :, 64:128])

        def r3r(tile, rows, h, a, b):
            return tile.v(tile.h[rows, a:b].rearrange("p (h t) -> p h t", h=h))
        HO = (0, 2, 1, 3)

        def rsqrt_small(out, in_, scale, eps):
            kb.act(out, in_, AF.Ln, bias=eps, scale=scale)
            kb.act(out, out, AF.Exp, scale=-0.5)

        def solve(Nm, UW, pfx):
            Am = S(pfx + "sol_A0", 512, BF16)
            pb = bank()
            for h in range(4):
                kb.tr(pb[:, h * 128:(h + 1) * 128], Nm[:, h * 128:(h + 1) * 128], ident)
            kb.evac(Am[:, :], pb[:, :])
            yield
            curN, curA = Nm, Am
            for lvl in range(6):
                pb = bank()
                for h in range(4):
                    hs = slice(h * 128, (h + 1) * 128)
                    kb.mm(pb[:, hs], curN[:, hs], UW[:, hs])
                if lvl < 5:
                    nN = S(pfx + "sol_N%d" % (lvl % 2), 512, BF16)
                    pn = bank()
                    for h in range(4):
                        hs = slice(h * 128, (h + 1) * 128)
                        kb.mm(pn[:, hs], curA[:, hs], curN[:, hs])
                if lvl < 4:
                    nA = S(pfx + "sol_A%d" % ((lvl + 1) % 2), 512, BF16)
                    pa = bank()
                    for h in range(4):
                        hs = slice(h * 128, (h + 1) * 128)
                        kb.mm(pa[:, hs], curN[:, hs], curA[:, hs])
                kb.tt(UW[:, :], UW[:, :], pb[:, :], ALU.subtract if lvl == 0 else ALU.add)
                if lvl == 5:
                    break
                kb.evac(nN[:, :], pn[:, :])
                if lvl < 4:
                    kb.evac(nA[:, :], pa[:, :])
                    curA = nA
                curN = nN
                yield

        def fmh(tile, h, c0, c1, base=0):
            pb_, cb = 64 * (h % 2), h // 2
            return tile[pb_:pb_ + 64, base + cb * 128 + c0: base + cb * 128 + c1]

        def sth(Sx, h):
            pb_, cb = 64 * (h % 2), h // 2
            return Sx[pb_:pb_ + 64, cb * 64:(cb + 1) * 64]

        def seq_core(pfx, po, Sx, Sb, UW, sign, wT, wbase, qT, qbase, attnT, kend, dS, attn2T=None, kend2=None, V2=None):
            X = S(pfx + "seq_X", 256, BF16)
            for c in range(2):
                cs = slice(64 * c, 64 * c + 64)
                pw = bank()
                for h in HO:
                    kb.mm(pw[cs, h * 64:(h + 1) * 64], wT[:, wbase + (h // 2) * 128 + 64 * c: wbase + (h // 2) * 128 + 64 * c + 64],
                          Sb[:, h * 64:(h + 1) * 64])
                kb.tt(r3r(X, cs, 4, 0, 256), Uv(UW, cs), r3r(pw, cs, 4, 0, 256), ALU.subtract if sign < 0 else ALU.add)
                yield
                for h in (HO if c == 0 else (1, 3, 0, 2)):
                    o_ = po[cs, h * 64:(h + 1) * 64]
                    kb.mm(o_, qT[:, qbase + (h // 2) * 128 + 64 * c: qbase + (h // 2) * 128 + 64 * c + 64],
                          Sb[:, h * 64:(h + 1) * 64], start=True, stop=False)
                    kb.mm(o_, attnT[:, h * 128 + 64 * c: h * 128 + 64 * c + 64], X[:, h * 64:(h + 1) * 64],
                          start=False, stop=(attn2T is None))
                    if attn2T is not None:
                        kb.mm(o_, attn2T[:, h * 128 + 64 * c: h * 128 + 64 * c + 64], V2[:, h * 64:(h + 1) * 64],
                              start=False, stop=True)
                pst = bank()
                for h in range(4):
                    o_ = sth(pst, h)
                    kb.mm(o_, kend[cs, h * 64:(h + 1) * 64], X[cs, h * 64:(h + 1) * 64], start=True,
                          stop=(kend2 is None))
                    if kend2 is not None:
                        kb.mm(o_, kend2[cs, h * 64:(h + 1) * 64], V2[cs, h * 64:(h + 1) * 64], start=False, stop=True)
                for cb in range(2):
                    kb.stt(Sx[:, cb * 64:(cb + 1) * 64], Sx[:, cb * 64:(cb + 1) * 64], dS[:, 2 * cb + c:2 * cb + c + 1],
                           pst[:, cb * 64:(cb + 1) * 64], ALU.mult, ALU.add)
                for i_ in range(2):
                    rows_ = slice(64 * i_, 64 * i_ + 64)
                    kb.copy("s", Sb.v(Sb.h[rows_, 0:256].rearrange("p (c i t) -> p c i t", c=2, i=2)[:, :, i_, :]),
                            Sx.v(Sx.h[rows_, 0:128].rearrange("p (c t) -> p c t", c=2)))
                yield

        def head_norm_gate(o_sb, normw, yblk0, tsl, ZS, pfx):
            sq = S(pfx + "hn_sq", 256)
            ss = S(pfx + "hn_ss", 8)
            kb.tt(sq[:, :], o_sb[:, 0:256], o_sb[:, 0:256], ALU.mult)
            kb.red(ss[:, 0:4], r3(sq, 4))
            yield
            rsqrt_small(ss[:, 4:8], ss[:, 0:4], 1.0 / 64, EPS)
            yield
            kb.tt(r3(sq, 4), r3(o_sb, 4, 0, 256), bc_t(ss, 4, 4, 64), ALU.mult)
            ob = S(pfx + "hn_ob", 256, BF16)
            kb.tt(r3(ob, 4), r3(sq, 4), bc_h(bcp, BC[normw][0], 64, 4), ALU.mult)
            yield
            to_fm_gate(ob, yblk0, tsl, ZS)

        def to_fm_gate(o_tm, yblk0, tsl, ZS):
            pb = bank()
            for blk in range(2):
                kb.tr(pb[:, blk * 128:(blk + 1) * 128], o_tm[:, blk * 128:(blk + 1) * 128], ident)
            for blk in range(2):
                kb.tt(yT[:, (yblk0 + blk) * MT + tsl.start:(yblk0 + blk) * MT + tsl.stop],
                      pb[:, blk * 128:(blk + 1) * 128],
                      ZS[:, blk * MT + tsl.start: blk * MT + tsl.stop], ALU.mult)

        wseq = []
        wstate = {"issued": 0}

        def wissue(upto):
            while wstate["issued"] < min(upto, len(wseq)):
                i = wstate["issued"]
                l, b = wseq[i]
                r0 = (l * NBLK + b) * 128
                kb.dma("gq", wbf[i % NRING][:], wblk_d[r0:r0 + 128, :])
                wstate["issued"] += 1
        used_blocks = []
        for m in ("sc", "gla", "gdn", "rwkv"):
            if m in mixers:
                b0, nb_ = MIX_BLK[m]
                used_blocks += list(range(b0, b0 + nb_))
        for l in range(L):
            for b in range(NB):
                for mt in range(nmt):
                    for blk in used_blocks:
                        wseq.append((l, blk))
        wptr = [0]

        def project(m, dst, stride, off):
            b0, nb_ = MIX_BLK[m]
            for bi in range(nb_):
                i = wptr[0]
                wissue(i + NRING - 1)
                wt = wbf[i % NRING]
                pb = pbank()
                M = 16 if (m == "gla" and bi == 6) else 128
                for kc in range(8):
                    kb.mm(pb[0:M, 0:MT], wt[:, kc * 128:kc * 128 + M], hT[:, kc * MT:(kc + 1) * MT],
                          start=(kc == 0), stop=(kc == 7))
                kb.evac(dst[0:M, bi * stride + off: bi * stride + off + MT], pb[0:M, 0:MT])
                wptr[0] += 1
                if (bi + 1) % KPB == 0 or bi == nb_ - 1:
                    yield

        def step_stream(clock, g_):
            kb.step_max = 0.0
            try:
                next(g_)
            except StopIteration:
                return False
            if kb.step_max > 0.0:
                clock[g_] = max(clock[g_], kb.step_max)
            else:
                others = [v for k, v in clock.items() if k is not g_]
                clock[g_] = max(clock[g_], min(others) if others else 0.0) + 0.3
            return True

        def run_streams(gens):
            gens = list(gens)
            clock = {g_: 0.0 for g_ in gens}
            while gens:
                g_ = min(gens, key=lambda x: clock[x])
                if not step_stream(clock, g_):
                    gens.remove(g_)
                    del clock[g_]
        pendingT = [None]

        for l in range(L):
            src_d = x_d if l == 0 else mid_d
            dst_d = out_d if l == L - 1 else mid_d
            kb.dma("sp", bcp[:], bcp_d[l:l + 1, :].partition_broadcast(128))
            kb.dma("sp", ppar[:], ppar_d[l * 128:(l + 1) * 128, :])
            kb.dma("gq", wab[:], wab_d[l * 128:(l + 1) * 128, :])
            for q8 in range(8):
                kb.dma("gq", wout[:, q8 * 1024:(q8 + 1) * 1024], wout_d[l * 128:(l + 1) * 128, q8 * 1024:(q8 + 1) * 1024])
            kb.dma("gq", wup[:], wup_d[l * 128:(l + 1) * 128, :])
            kb.dma("sp", gup[:], gup_d[l * 128:(l + 1) * 128, :])
            kb.act(nal[:, :], BCv("a_log"), AF.Exp)
            kb.ts(nal[:, :], nal[:, :], -1.0, ALU.mult)
            for b in range(NB):
                for m_ in st:
                    kb.memset(st[m_][:, :], 0.0)
                for m_ in stb:
                    kb.memset(stb[m_][:, :], 0.0)
                for nm_ in ("gd_seq_X", "rw_seq_X"):
                    kb.memset(S(nm_, 256, BF16)[:, :], 0.0)
                for m_ in hist:
                    kb.memset(hist[m_][:, :], 0.0)
                for mt in range(nmt):
                    tok0 = b * T + mt * MT
                    def gen_H1(tok0=tok0):
                        for j in range(MT // 128):
                            xt = xts[j % 2]
                            r0 = tok0 + j * 128
                            kb.dma("sp", xt[:], src_d[r0:r0 + 128, :])
                            ss = S("n_ss", 8)
                            junk = P_g
                            kb.act(junk[:, 0:1024], xt[:, :], AF.Square)
                            kb.red(ss[:, 0:1], junk[:, 0:1024])
                            rsqrt_small(ss[:, 1:2], ss[:, 0:1], 1.0 / D, EPS)
                            kb.ts(ht[:, :], xt[:, :], ss[:, 1:2], ALU.mult)
                            yield
                            for half in range(2):
                                pb = bank()
                                for q in range(4):
                                    kc = half * 4 + q
                                    kb.tr(pb[:, q * 128:(q + 1) * 128], ht[:, kc * 128:(kc + 1) * 128], ident)
                                outv = hT.v(hT.h[:, :].rearrange("p (k t) -> p k t", k=8)[:, half * 4:half * 4 + 4,
                                                                                          j * 128:(j + 1) * 128])
                                kb.tt(outv, r3(pb, 4), bc_t(ppar, half * 4, 4, 128), ALU.mult)
                                yield
                        if "sc" in mixers:
                            yield from project("sc", P_s, MT + 4, 4)
                        if "gla" in mixers:
                            yield from project("gla", P_a, MT + 4, 4)
                    def gen_sc():
                        W2 = MT + 4
                        W2 = MT + 4
                        for blk in range(2):
                            u = S("sc_u%d" % blk, MT + 2)
                            kb.copy("g", u[:, 0:2], hist["sc"][:, blk * 2:blk * 2 + 2])
                            kb.tt(u[:, 2:MT + 2], P_s[:, (2 + blk) * W2 + 4:(2 + blk) * W2 + 4 + MT],
                                  P_s[:, (4 + blk) * W2 + 4:(4 + blk) * W2 + 4 + MT], ALU.mult)
                            kb.copy("g", hist["sc"][:, blk * 2:blk * 2 + 2], u[:, MT:MT + 2])
                            acc_t = S("sc_acc")
                            acc = acc_t[:, 0:MT]
                            kb.ts(acc, u[:, 0:MT], ppar[:, 41 + blk * 3:42 + blk * 3], ALU.mult)
                            for tap in (1, 2):
                                kb.stt(acc, u[:, tap:tap + MT], ppar[:, 41 + blk * 3 + tap:42 + blk * 3 + tap],
                                       acc, ALU.mult, ALU.add)
                            zs_t = S("sc_zs")
                            zs = zs_t[:, 0:MT]
                            kb.act(zs, P_s[:, (6 + blk) * W2 + 4:(6 + blk) * W2 + 4 + MT], AF.Silu)
                            kb.tt(acc, acc, P_s[:, blk * W2 + 4: blk * W2 + 4 + MT], ALU.mult)
                            kb.tt(yT[:, (4 + blk) * MT:(5 + blk) * MT], acc, zs, ALU.mult)
                            yield

                    def gen_gla():
                        W2 = MT + 4
                        W2 = MT + 4
                        for blk in range(2):
                            kb.act(ZS_a[:, blk * MT:(blk + 1) * MT], P_a[:, (4 + blk) * W2 + 4:(4 + blk) * W2 + 4 + MT],
                                   AF.Silu)
                        for j in range(MT // 128):
                            tsl = slice(j * 128, (j + 1) * 128)

                            def Pg(blk, rows=slice(0, 128)):
                                return P_a[rows, blk * W2 + 4 + j * 128: blk * W2 + 4 + (j + 1) * 128]
                            pb = bank()
                            kb.mm(pb[:, 0:128], Pg(6, slice(0, 16)), gup[0:16, :])
                            sp_ = S("gl_sp", 128)
                            kb.tt(sp_[:, :], pb[:, 0:128], BCv("gla_bias"), ALU.add)
                            yield
                            kb.act(sp_[:, :], sp_[:, :], AF.Exp, scale=-1.0)
                            yield
                            kb.act(sp_[:, :], sp_[:, :], AF.Ln, bias=1.0)
                            yield
                            pc = bank()
                            kb.mm(pc[:, 0:128], sp_[:, :], C("mcum"))
                            kb.mm(pc[:, 128:256], C("mrest"), sp_[:, :])
                            kb.mm(pc[:, 256:258], sp_[:, :], C("chunksel", 2))
                            kb.tr(pc[:, 384:512], Pg(1), ident)
                            ex = S("gl_ex", 512)
                            kb.act(ex[:, 0:128], pc[:, 0:128], AF.Exp, scale=-1.0 / 16)
                            kb.act(ex[:, 128:256], pc[:, 0:128], AF.Exp, scale=1.0 / 16)
                            kb.act(ex[:, 256:384], pc[:, 128:256], AF.Exp, scale=-1.0 / 16)
                            dS = S("gl_dS", 8)
                            kb.act(dS[:, 0:2], pc[:, 256:258], AF.Exp, scale=-1.0 / 16)
                            qe = S("gl_qe", 128); ke = S("gl_ke", 128); kend = S("gl_kend", 128)
                            kb.stt(qe[:, :], ex[:, 0:128], 32.0 ** -0.5, Pg(0), ALU.mult, ALU.mult)
                            kb.tt(ke[:, :], ex[:, 128:256], Pg(1), ALU.mult)
                            kb.tt(kend[:, :], ex[:, 256:384], pc[:, 384:512], ALU.mult)
                            yield
                            pv = bank()
                            for blk in range(2):
                                kb.tr(pv[:, blk * 128:(blk + 1) * 128], Pg(2 + blk), ident)
                            vtm = S("gl_v", 256)
                            kb.evac(vtm[:, :], pv[:, 0:256])
                            qblk = S("gl_qblk")
                            kb.tt(r3(qblk, 4), bc_h(qe, 0, 128, 4),
                                  cst.v(cst.h[:, COFF["bq"]:COFF["bq"] + 512].rearrange("p (h t) -> p h t", h=4)),
                                  ALU.mult)
                            pa = bank()
                            kb.mm(pa[:, :], ke[:, :], qblk[:, :])
                            attnT = S("gl_attn")
                            kb.tt(r3(attnT, 4), r3(pa, 4), Cb4("m_incl"), ALU.mult)
                            yield
                            Sg = st["gla"]
                            po = banks[3]
                            for c in range(2):
                                cs = slice(64 * c, 64 * c + 64)
                                kb.mm(po[cs, 256:512], qe[:, cs], Sg[:, :], start=True, stop=False)
                                for h in range(4):
                                    kb.mm(po[cs, 256 + h * 64:256 + (h + 1) * 64], attnT[cs, h * 128 + 64 * c:h * 128 + 64 * c + 64],
                                          vtm[cs, h * 64:(h + 1) * 64], start=False, stop=(h == 3))
                                yield
                                pst = bank()
                                kb.mm(pst[:, 0:256], kend[cs, :], vtm[cs, :])
                                tmp = S("gl_tmp", 256)
                                kb.tt(tmp[:, :], pst[:, 0:256], C("bs", 256), ALU.mult)
                                kb.stt(Sg[:, :], Sg[:, :], dS[:, c:c + 1], tmp[:, :], ALU.mult, ALU.add)
                                yield
                            osb = S("gl_o", 256)
                            kb.evac(osb[:, :], po[:, 256:512])
                            yield from head_norm_gate(osb, "gla_norm_w", 6, tsl, ZS_a, "gl_")
                            yield

                    def gen_gdn():
                        W2 = MT + 4
                        W2 = MT + 4
                        for blk in range(6):
                            kb.copy("g", P_g[:, blk * W2 + 1: blk * W2 + 4], hist["gdn"][:, blk * 3:blk * 3 + 3])
                            kb.copy("g", hist["gdn"][:, blk * 3:blk * 3 + 3], P_g[:, blk * W2 + 1 + MT: blk * W2 + 4 + MT])
                            acc = S("cv_acc%d" % (blk % 2), MT)[:, 0:MT]
                            kb.ts(acc, P_g[:, blk * W2 + 1: blk * W2 + 1 + MT], ppar[:, 8 + blk * 4:9 + blk * 4], ALU.mult)
                            for tap in (1, 2, 3):
                                yield
                                kb.stt(acc, P_g[:, blk * W2 + 1 + tap: blk * W2 + 1 + tap + MT],
                                       ppar[:, 8 + blk * 4 + tap:9 + blk * 4 + tap], acc, ALU.mult, ALU.add)
                            yield
                            kb.act(Q_g[:, blk * MT:(blk + 1) * MT], acc, AF.Silu)
                            yield
                        for blk in range(2):
                            kb.act(ZS_g[:, blk * MT:(blk + 1) * MT], P_g[:, (6 + blk) * W2 + 4:(6 + blk) * W2 + 4 + MT],
                                   AF.Silu)
                        for j in range(MT // 128):
                            tsl = slice(j * 128, (j + 1) * 128)

                            def Qg(blk, rows=slice(0, 128)):
                                return Q_g[rows, blk * MT + j * 128: blk * MT + (j + 1) * 128]
                            pq = bank(); pv = bank()
                            for blk in range(4):
                                kb.tr(pq[:, blk * 128:(blk + 1) * 128], Qg(blk), ident)
                            for blk in range(2):
                                kb.tr(pv[:, blk * 128:(blk + 1) * 128], Qg(4 + blk), ident)
                            qk = S("gd_qk"); vtm = S("gd_v", 256, BF16)
                            kb.evac(qk[:, :], pq[:, :])
                            kb.evac(vtm[:, :], pv[:, 0:256])
                            yield
                            pab = bank()
                            for kc in range(8):
                                kb.mm(pab[:, 0:8], hT[:, kc * MT + j * 128: kc * MT + (j + 1) * 128],
                                      wab[:, kc * 8:(kc + 1) * 8], start=(kc == 0), stop=(kc == 7))
                            sc = S("gd_sc", 64)
                            kb.tt(sc[:, 0:4], pab[:, 0:4], BCv("dt_bias"), ALU.add)
                            kb.act(sc[:, 4:8], pab[:, 4:8], AF.Exp, scale=-1.0)
                            yield
                            kb.act(sc[:, 0:4], sc[:, 0:4], AF.Exp)
                            yield
                            kb.act(sc[:, 0:4], sc[:, 0:4], AF.Ln, bias=1.0)
                            yield
                            kb.tt(sc[:, 0:4], sc[:, 0:4], nal[:, :], ALU.mult)
                            kb.act(sc[:, 4:8], sc[:, 4:8], AF.Ln, bias=1.0)
                            yield
                            kb.act(sc[:, 8:12], sc[:, 4:8], AF.Exp, scale=-1.0)
                            sq = S("gd_sq")
                            kb.tt(sq[:, :], qk[:, :], qk[:, :], ALU.mult)
                            yield
                            kb.red(sc[:, 12:20], r3(sq, 8))
                            yield
                            kb.act(sc[:, 20:28], sc[:, 12:20], AF.Ln, bias=EPS)
                            yield
                            kb.ts(sc[:, 20:28], sc[:, 20:28], -0.5, ALU.mult)
                            kb.ts(sc[:, 20:24], sc[:, 20:24], math.log(1.0 / 8.0), ALU.add)
                            yield
                            pg = bank()
                            kb.mm(pg[:, 0:4], C("mcum"), sc[:, 0:4])
                            kb.mm(pg[:, 4:8], C("mrest"), sc[:, 0:4])
                            gb = S("gd_gb", 256)
                            kb.copy("v", r3(gb, 4), bc_t(sc, 0, 4, 64))
                            for cb in range(2):
                                kb.mm(pg[:, 8 + 2 * cb: 10 + 2 * cb], gb[:, cb * 128:(cb + 1) * 128], C("chunksel", 2))
                            dS = S("gd_dS", 8)
                            kb.act(dS[:, 0:4], pg[:, 8:12], AF.Exp)
                            kb.copy("v", sc[:, 28:32], pg[:, 0:4])
                            kb.tt(sc[:, 52:56], pg[:, 4:8], sc[:, 24:28], ALU.add)
                            yield
                            kb.tt(sc[:, 32:36], sc[:, 28:32], sc[:, 4:8], ALU.subtract)
                            kb.act(sc[:, 52:56], sc[:, 52:56], AF.Exp)
                            yield
                            kb.tt(sc[:, 48:52], sc[:, 32:36], sc[:, 24:28], ALU.add)
                            kb.tt(sc[:, 36:40], sc[:, 24:28], sc[:, 28:32], ALU.subtract)
                            yield
                            kb.copy("v", sc[:, 32:36], sc[:, 48:52])
                            kb.tt(sc[:, 40:44], sc[:, 28:32], sc[:, 20:24], ALU.add)
                            yield
                            kb.act(sc[:, 44:48], sc[:, 40:44], AF.Exp)
                            kb.act(sc[:, 48:52], sc[:, 48:52], AF.Exp)
                            yield
                            UW = S("gd_UW", 512, BF16)
                            kb.tt(Uv(UW), r3(vtm, 4), bc_t(sc, 8, 4, 64), ALU.mult)
                            yield
                            kb.tt(Wv(UW), r3(qk, 4, 256, 512), bc_t(sc, 48, 4, 64), ALU.mult)
                            yield
                            kend = S("gd_kend", 256, BF16)
                            kb.tt(r3(kend, 4), r3(qk, 4, 256, 512), bc_t(sc, 52, 4, 64), ALU.mult)
                            yield
                            dq = S("gd_dq")
                            kb.tt(r3(dq, 4), Cb4("ident"), bc_t(sc, 44, 4, 128), ALU.mult)
                            pqg = bank()
                            for h in range(4):
                                pb_, cb = 64 * (h % 2), h // 2
                                kb.mm(pqg[pb_:pb_ + 64, cb * 128:(cb + 1) * 128], qk[:, h * 64:(h + 1) * 64],
                                      dq[:, h * 128:(h + 1) * 128])
                            fm = S("gd_fm", 512, BF16)
                            kb.evac(fm[:, 0:256], pqg[:, 0:256])
                            yield
                            rd = S("gd_rd"); cbm = S("gd_cb")
                            kb.tt(r3(rd, 4), Cb4("ident"), bc_t(sc, 32, 4, 128), ALU.mult)
                            kb.tt(r3(cbm, 4), Cb4("nm_strict"), bc_t(sc, 36, 4, 128), ALU.add)
                            pe1 = bank()
                            kb.mm(pe1[:, :], C("ones"), rd[:, :], start=True, stop=False)
                            kb.mm(pe1[:, :], ident, cbm[:, :], start=False, stop=True)
                            DTs = S("gd_DTs")
                            kb.act(DTs[:, :], pe1[:, :], AF.Exp)
                            yield
                            rd2 = S("gd_rd"); cb2 = S("gd_cb")
                            kb.tt(r3(rd2, 4), Cb4("ident"), bc_t(sc, 40, 4, 128), ALU.mult)
                            kb.tt(r3(cb2, 4), Cb4("nm_incl"), bc_t(sc, 36, 4, 128), ALU.add)
                            pe2 = bank()
                            kb.mm(pe2[:, :], C("ones"), rd2[:, :], start=True, stop=False)
                            kb.mm(pe2[:, :], ident, cb2[:, :], start=False, stop=True)
                            DTi = S("gd_DTi")
                            kb.act(DTi[:, :], pe2[:, :], AF.Exp)
                            yield
                            Nm = S("gd_N", 512, BF16); attnT = S("gd_attn", 512, BF16)
                            pkk = bank()
                            for h in HO:
                                pb_, cb = 64 * (h % 2), h // 2
                                kT = Qg(2 + cb, slice(pb_, pb_ + 64))
                                kb.mm(pkk[:, h * 128:(h + 1) * 128], kT, kT)
                            kb.tt(Nm[:, :], pkk[:, :], DTs[:, :], ALU.mult)
                            yield
                            pqk = bank()
                            for h in HO:
                                pb_, cb = 64 * (h % 2), h // 2
                                kT = Qg(2 + cb, slice(pb_, pb_ + 64))
                                qT_ = Qg(cb, slice(pb_, pb_ + 64))
                                kb.mm(pqk[:, h * 128:(h + 1) * 128], kT, qT_)
                            kb.tt(attnT[:, :], pqk[:, :], DTi[:, :], ALU.mult)
                            yield
                            yield from solve(Nm, UW, "gd_")
                            pwt = bank()
                            for h in range(4):
                                pb_, cb = 64 * (h % 2), h // 2
                                kb.tr(pwt[pb_:pb_ + 64, cb * 128:(cb + 1) * 128], UW[:, h * 128 + 64:(h + 1) * 128], ident)
                            kb.evac(fm[:, 256:512], pwt[:, 0:256])
                            yield
                            po = banks[2]
                            yield from seq_core("gd_", po, st["gdn"], stb["gdn"], UW, -1, fm, 256, fm, 0, attnT, kend, dS)
                            osb = S("gd_o", 256)
                            kb.evac(osb[:, :], po[:, 0:256])
                            yield from head_norm_gate(osb, "gdn_norm_w", 0, tsl, ZS_g, "gd_")
                            yield

                    def gen_rwkv():
                        W2 = MT + 4
                        W2 = MT + 4
                        for blk in range(9):
                            kb.copy("g", P_r[:, blk * W2 + 3: blk * W2 + 4], hist["rwkv"][:, blk:blk + 1])
                            kb.copy("g", hist["rwkv"][:, blk:blk + 1], P_r[:, blk * W2 + 3 + MT: blk * W2 + 4 + MT])
                            dlt_t = S("rw_d")
                            dlt = dlt_t[:, 0:MT]
                            kb.tt(dlt, P_r[:, blk * W2 + 3: blk * W2 + 3 + MT], P_r[:, blk * W2 + 4: blk * W2 + 4 + MT],
                                  ALU.subtract)
                            yield
                            kb.stt(Q_r[:, blk * MT:(blk + 1) * MT], dlt, ppar[:, 32 + blk:33 + blk],
                                   P_r[:, blk * W2 + 4: blk * W2 + 4 + MT], ALU.mult, ALU.add)
                            yield
                        for blk in range(2):
                            kb.act(ZS_r[:, blk * MT:(blk + 1) * MT], Q_r[:, (6 + blk) * MT:(7 + blk) * MT], AF.Silu)
                        kb.act(Q_r[0:64, 8 * MT:9 * MT], Q_r[0:64, 8 * MT:9 * MT], AF.Tanh)
                        for j in range(MT // 128):
                            tsl = slice(j * 128, (j + 1) * 128)

                            def Qr(blk, rows=slice(0, 128)):
                                return Q_r[rows, blk * MT + j * 128: blk * MT + (j + 1) * 128]
                            prk = bank(); pv = bank()
                            for blk in range(4):
                                kb.tr(prk[:, blk * 128:(blk + 1) * 128], Qr(blk), ident)
                            for blk in range(2):
                                kb.tr(pv[:, blk * 128:(blk + 1) * 128], Qr(4 + blk), ident)
                            rk = S("rw_rk"); vtm = S("rw_v", 256, BF16)
                            kb.evac(rk[:, :], prk[:, :])
                            kb.evac(vtm[:, :], pv[:, 0:256])
                            yield
                            pwa = bank()
                            kb.mm(pwa[:, 0:256], Qr(8, slice(0, 64)), wup[0:64, :])
                            kb.mm(pwa[:, 256:512], Qr(8, slice(64, 128)), wup[64:128, :])
                            lw = S("rw_lw", 256); a_ = S("rw_a", 256)
                            kb.tt(lw[:, :], pwa[:, 0:256], BCv("w0"), ALU.add)
                            kb.tt(a_[:, :], pwa[:, 256:512], BCv("a0"), ALU.add)
                            yield
                            kb.act(lw[:, :], lw[:, :], AF.Sigmoid)
                            kb.act(a_[:, :], a_[:, :], AF.Sigmoid)
                            yield
                            kb.ts(lw[:, :], lw[:, :], -math.exp(-0.5), ALU.mult)
                            kk = S("rw_kk", 256); sq = S("rw_sq", 256); sc = S("rw_sc", 32)
                            yield
                            kb.tt(kk[:, :], rk[:, 256:512], BCv("k_k"), ALU.mult)
                            kb.tt(sq[:, :], kk[:, :], kk[:, :], ALU.mult)
                            yield
                            kb.red(sc[:, 0:4], r3(sq, 4))
                            yield
                            rsqrt_small(sc[:, 4:8], sc[:, 0:4], 1.0, EPS)
                            yield
                            kb.tt(r3(kk, 4), r3(kk, 4), bc_t(sc, 4, 4, 64), ALU.mult)
                            km = S("rw_km", 256)
                            yield
                            kb.stt(km[:, :], a_[:, :], -1.0, BCv("k_a"), ALU.add, ALU.mult)
                            yield
                            kb.stt(km[:, :], km[:, :], 1.0, rk[:, 256:512], ALU.add, ALU.mult)
                            yield
                            bb = S("rw_b", 256)
                            kb.tt(bb[:, :], kk[:, :], a_[:, :], ALU.mult)
                            yield
                            kb.tt(sq[:, :], rk[:, 0:256], km[:, :], ALU.mult)
                            yield
                            kb.tt(sq[:, :], sq[:, :], BCv("r_k"), ALU.mult)
                            kb.red(sc[:, 8:12], r3(sq, 4))
                            yield
                            pc1 = bank(); pc2 = bank()
                            kb.mm(pc1[:, 0:256], C("mcum"), lw[:, :])
                            kb.mm(pc1[:, 256:512], C("mcumx"), lw[:, :])
                            kb.mm(pc2[:, 0:256], C("mrest"), lw[:, :])
                            for h in range(4):
                                pb_, cb = 64 * (h % 2), h // 2
                                kb.mm(pc2[pb_:pb_ + 64, 256 + 2 * cb: 258 + 2 * cb], lw[:, h * 64:(h + 1) * 64],
                                      C("chunksel", 2))
                            dS = S("rw_dS", 8)
                            kb.act(dS[:, 0:4], pc2[:, 256:260], AF.Exp)
                            eg = S("rw_eg"); er = S("rw_er", 256); eng_ = S("rw_eng", 256)
                            kb.act(eg[:, :], pc1[:, :], AF.Exp)
                            kb.act(eng_[:, :], pc1[:, 0:256], AF.Exp, scale=-1.0)
                            kb.act(er[:, :], pc2[:, 0:256], AF.Exp)
                            TA = S("rw_TA", 512, BF16); TB = S("rw_TB", 512, BF16)
                            yield
                            kb.tt(TA[:, 0:256], rk[:, 0:256], eg[:, 0:256], ALU.mult)
                            kb.tt(TA[:, 256:512], km[:, :], eng_[:, :], ALU.mult)
                            yield
                            kb.tt(TB[:, 0:256], bb[:, :], eng_[:, :], ALU.mult)
                            kb.tt(TB[:, 256:512], kk[:, :], eg[:, 256:512], ALU.mult)
                            yield
                            kend2 = S("rw_kend2", 256, BF16); kend = S("rw_kend", 256, BF16)
                            yield
                            kb.tt(kend2[:, :], km[:, :], er[:, :], ALU.mult)
                            kb.stt(kend[:, :], bb[:, :], -1.0, er[:, :], ALU.mult, ALU.mult)
                            yield
                            FA = S("rw_FA", 512, BF16); FB = S("rw_FB", 512, BF16)
                            pfa = bank()
                            for q in range(4):
                                kb.tr(pfa[:, q * 128:(q + 1) * 128], TA[:, q * 128:(q + 1) * 128], ident)
                            kb.evac(FA[:, :], pfa[:, :])
                            yield
                            pfb = bank()
                            for q in range(4):
                                kb.tr(pfb[:, q * 128:(q + 1) * 128], TB[:, q * 128:(q + 1) * 128], ident)
                            kb.evac(FB[:, :], pfb[:, :])
                            yield
                            Nm = S("rw_N", 512, BF16); AbkT = S("rw_Abk", 512, BF16); attn2T = S("rw_attn2", 512, BF16); attnT = S("rw_attn", 512, BF16)
                            pn = bank(); pbk = bank()
                            for h in HO:
                                hs = slice(h * 128, (h + 1) * 128)
                                KT = fmh(FA, h, 0, 128, 256)
                                BT = fmh(FB, h, 0, 128, 0); KKT = fmh(FB, h, 0, 128, 256)
                                kb.mm(pn[:, hs], BT, KKT)
                                kb.mm(pbk[:, hs], KT, KKT)
                            kb.tt(r3(Nm, 4), r3(pn, 4), Cb4("m_strict"), ALU.mult)
                            kb.tt(r3(AbkT, 4), r3(pbk, 4), Cb4("m_strict"), ALU.mult)
                            yield
                            prk2 = bank(); prb = bank()
                            for h in HO:
                                hs = slice(h * 128, (h + 1) * 128)
                                RT = fmh(FA, h, 0, 128, 0); KT = fmh(FA, h, 0, 128, 256)
                                BT = fmh(FB, h, 0, 128, 0)
                                kb.mm(prk2[:, hs], KT, RT)
                                kb.mm(prb[:, hs], BT, RT)
                            kb.tt(r3(attn2T, 4), r3(prk2, 4), Cb4("m_incl"), ALU.mult)
                            kb.tt(r3(attnT, 4), r3(prb, 4), Cb4("m_incl_neg"), ALU.mult)
                            yield
                            UW = S("rw_UW", 512, BF16)
                            pu = bank()
                            for h in range(4):
                                kb.mm(pu[:, h * 64:(h + 1) * 64], AbkT[:, h * 128:(h + 1) * 128], vtm[:, h * 64:(h + 1) * 64])
                            kb.evac(Uv(UW), r3(pu, 4, 0, 256))
                            kb.copy("v", Wv(UW), r3(TB, 4, 256, 512))
                            yield
                            yield from solve(Nm, UW, "rw_")
                            pwt = bank()
                            for h in range(4):
                                pb_, cb = 64 * (h % 2), h // 2
                                kb.tr(pwt[pb_:pb_ + 64, cb * 128:(cb + 1) * 128], UW[:, h * 128 + 64:(h + 1) * 128], ident)
                            WT = S("rw_WT", 256, BF16)
                            kb.evac(WT[:, :], pwt[:, 0:256])
                            yield
                            po = banks[3]
                            yield from seq_core("rw_", po, st["rwkv"], stb["rwkv"], UW, +1, WT, 0, FA, 0, attnT, kend, dS, attn2T, kend2, vtm)
                            osb = S("rw_o", 256); cen = S("rw_cen", 256)
                            kb.evac(osb[:, :], po[:, 0:256])
                            yield
                            kb.red(sc[:, 12:16], r3(osb, 4))
                            yield
                            kb.ts(sc[:, 12:16], sc[:, 12:16], 1.0 / 64, ALU.mult)
                            yield
                            kb.tt(r3(cen, 4), r3(osb, 4), bc_t(sc, 12, 4, 64), ALU.subtract)
                            yield
                            kb.tt(sq[:, :], cen[:, :], cen[:, :], ALU.mult)
                            yield
                            kb.red(sc[:, 16:20], r3(sq, 4))
                            yield
                            rsqrt_small(sc[:, 20:24], sc[:, 16:20], 1.0 / 64, 64e-5)
                            yield
                            kb.tt(r3(cen, 4), r3(cen, 4), bc_t(sc, 20, 4, 64), ALU.mult)
                            yield
                            kb.tt(cen[:, :], cen[:, :], BCv("ln_w"), ALU.mult)
                            yield
                            kb.tt(cen[:, :], cen[:, :], BCv("ln_b"), ALU.add)
                            yield
                            kb.tt(r3(sq, 4), r3(vtm, 4), bc_t(sc, 8, 4, 64), ALU.mult)
                            yield
                            ob = S("rw_hn_ob", 256, BF16)
                            kb.tt(ob[:, :], cen[:, :], sq[:, :], ALU.add)
                            to_fm_gate(ob, 2, tsl, ZS_r)
                            yield


                    proj_done = {}

                    def gen_P():
                        if "gdn" in mixers:
                            yield from project("gdn", P_g, MT + 4, 4)
                        else:
                            kb.memset(yT[:, 0:2 * MT], 0.0)
                        proj_done["gdn"] = True
                        if "rwkv" in mixers:
                            yield from project("rwkv", P_r, MT + 4, 4)
                        else:
                            kb.memset(yT[:, 2 * MT:4 * MT], 0.0)
                        proj_done["rwkv"] = True
                    run_streams([gen_H1()] + ([pendingT[0]] if pendingT[0] is not None else []))
                    pendingT[0] = None
                    active = [gen_P()]
                    if "sc" in mixers:
                        active.append(gen_sc())
                    if "gla" in mixers:
                        active.append(gen_gla())
                    pending = {}
                    if "gdn" in mixers:
                        pending["gdn"] = gen_gdn
                    if "rwkv" in mixers:
                        pending["rwkv"] = gen_rwkv
                    clock = {g_: 0.0 for g_ in active}
                    while active or pending:
                        g_ = min(active, key=lambda x: clock[x])
                        if not step_stream(clock, g_):
                            active.remove(g_)
                            del clock[g_]
                        for m_ in list(pending):
                            if proj_done.get(m_):
                                gn_ = pending.pop(m_)()
                                active.append(gn_)
                                clock[gn_] = (min(clock.values()) if clock else 0.0) + (GOFF if m_ == "gdn" else 0.0)
                    for m_, (y0, y1) in (("sc", (4, 6)), ("gla", (6, 8))):
                        if m_ not in mixers:
                            kb.memset(yT[:, y0 * MT:y1 * MT], 0.0)
                    def gen_T(tok0=tok0, src_d=src_d, dst_d=dst_d):
                        for j in range(MT // 128):
                            r0 = tok0 + j * 128
                            kb.dma("sp", xres[:], src_d[r0:r0 + 128, :])
                            osb = S("op_o", 1024)
                            for half in range(2):
                                pb = pbank()
                                for kc in range(8):
                                    kb.mm(pb[:, :], yT[:, kc * MT + j * 128: kc * MT + (j + 1) * 128],
                                          wout[:, kc * 1024 + half * 512: kc * 1024 + (half + 1) * 512],
                                          start=(kc == 0), stop=(kc == 7))
                                kb.evac(osb[:, half * 512:(half + 1) * 512], pb[:, :])
                                yield
                            ss = S("t_ss", 8)
                            junk = P_r
                            kb.act(junk[:, 0:1024], osb[:, :], AF.Square)
                            kb.red(ss[:, 2:3], junk[:, 0:1024])
                            rsqrt_small(ss[:, 3:4], ss[:, 2:3], 1.0 / D, EPS)
                            yield
                            kb.stt(osb[:, :], osb[:, :], ss[:, 3:4], BCv("post_w"), ALU.mult, ALU.mult)
                            kb.tt(osb[:, :], osb[:, :], xres[:, :], ALU.add)
                            kb.dma("sp", dst_d[r0:r0 + 128, :], osb[:])
                            yield
                    pendingT[0] = gen_T()
            if pendingT[0] is not None:
                run_streams([pendingT[0]])
                pendingT[0] = None
            if l < L - 1:
                kb.wait_all("g")
                kb.op("g", lambda o: o.memset(P_r.h[:, 0:1], 0.0), [], [P_r[:, 0:1]])
                e = kb.engs["g"]
                kb.dq_eng["sp"].obj.wait_ge(e.sem, e.cnt)
                kb.dq_eng["sp"].waited["g"] = e.cnt
        kb.wait_all("g")
    return nc


_CACHE = {}


def run(inputs, NB, T, L, ncores, mixers=("gdn", "rwkv", "sc", "gla")):
    key = (NB, T, L, tuple(mixers))
    if key not in _CACHE:
        _CACHE[key] = build(NB, T, L, mixers)
    nc = _CACHE[key]
    pl = prep_layer_inputs(inputs, L)
    x = np.ascontiguousarray(inputs["x"], dtype=np.float32)
    in_maps = []
    for c in range(ncores):
        m = {"x": x[c * NB:(c + 1) * NB].reshape(NB * T, D), "consts": CONSTS}
        m.update(pl)
        in_maps.append(m)
    res = run_bass_kernel_spmd(nc, in_maps, core_ids=list(range(ncores)))
    outs = [r["out"].reshape(NB, T, D) for r in res.results]
    return np.concatenate(outs, axis=0)


def kernel(**inputs):
    inputs = {k: np.asarray(v) for k, v in inputs.items()}
    B, T, _ = inputs["x"].shape
    L = inputs["w_in"].shape[0]
    return run(inputs, B // NCORES, T, L, NCORES).astype(np.float32)
```

```python
import contextlib
import math
import os
DBG = float(os.environ.get('KDBG', '99'))
GOFF = float(os.environ.get("KGOFF", "0"))
LAT = float(os.environ.get("KLAT", "0.25"))
KPB = int(os.environ.get("KPB", "3"))
EVR = int(os.environ.get("KEVR", "4"))
import numpy as np
import concourse.bass as bass
import concourse.mybir as mybir
from concourse.bass_utils import run_bass_kernel_spmd

F32 = mybir.dt.float32
BF16 = mybir.dt.bfloat16
ALU = mybir.AluOpType
AF = mybir.ActivationFunctionType
AX = mybir.AxisListType

D = 1024
NCORES = 8
MT = 256
EPS = 1e-6
NEG = -30000.0


class Trk:
    __slots__ = ("w", "r", "tw", "tr")

    def __init__(self):
        self.w = None
        self.r = []
        self.tw = 0.0
        self.tr = 0.0


class V:
    __slots__ = ("ap", "trk")

    def __init__(self, ap, trk):
        self.ap = ap
        self.trk = trk


class Tl:
    def __init__(self, handle):
        self.h = handle
        self.trk = Trk()

    def __getitem__(self, key):
        return V(self.h[key], [self.trk])

    def v(self, ap):
        return V(ap, [self.trk])


class Eng:
    def __init__(self, name, obj, sem):
        self.name = name
        self.obj = obj
        self.sem = sem
        self.cnt = 0
        self.waited = {}


class KB:
    def __init__(self, nc, ctx):
        self.nc = nc
        self.ctx = ctx
        self.engs = {}
        for nm, obj in (("pe", nc.tensor), ("v", nc.vector), ("s", nc.scalar), ("g", nc.gpsimd)):
            sem = ctx.enter_context(nc.semaphore("sem_" + nm))
            self.engs[nm] = Eng(nm, obj, sem)
        self.dmaq = {"sp": nc.sync, "gq": nc.gpsimd}
        self.dq_eng = {"sp": Eng("spq", nc.sync, None), "gq": self.engs["g"]}
        self.dsems = {}
        for q in ("sp", "gq"):
            lst = []
            for i in range(8):
                sem = ctx.enter_context(nc.semaphore("dsem_%s%d" % (q, i)))
                e = Eng("d_%s%d" % (q, i), None, sem)
                self.engs[e.name] = e
                lst.append(e)
            self.dsems[q] = lst
        self.drr = {"sp": 0, "gq": 0}
        self.flip = 0
        self.efree = {}
        self.step_max = 0.0

    def sb(self, name, shape, dtype=F32):
        return Tl(self.ctx.enter_context(self.nc.sbuf_tensor("sb_" + name, list(shape), dtype)))

    def ps(self, name, shape, dtype=F32):
        return Tl(self.ctx.enter_context(self.nc.psum_tensor("ps_" + name, list(shape), dtype)))

    def _deps(self, issuer, reads, writes):
        deps = {}
        me = issuer.name

        def add(d, raw):
            if d is None:
                return
            e, c = d
            if e == me and (me == "pe" or not raw):
                return
            if deps.get(e, 0) < c:
                deps[e] = c
        for v in reads:
            for t in v.trk:
                add(t.w, True)
        for v in writes:
            for t in v.trk:
                add(t.w, False)
                for r in t.r:
                    add(r, False)
        for e, c in deps.items():
            if issuer.waited.get(e, 0) < c:
                issuer.obj.wait_ge(self.engs[e].sem, c)
                issuer.waited[e] = c

    def _mark(self, ident, reads, writes):
        for v in reads:
            for t in v.trk:
                t.r.append(ident)
        for v in writes:
            for t in v.trk:
                t.w = ident
                t.r = []

    def _est(self, eng, cost, reads, writes):
        dep = 0.0
        for v in reads:
            for t in v.trk:
                dep = max(dep, t.tw)
        for v in writes:
            for t in v.trk:
                dep = max(dep, t.tw, t.tr)
        start = max(self.efree.get(eng, 0.0), dep + LAT)
        fin = start + cost
        self.efree[eng] = fin
        for v in reads:
            for t in v.trk:
                t.tr = max(t.tr, fin)
        for v in writes:
            for t in v.trk:
                t.tw = fin
        self.step_max = max(self.step_max, fin)

    @staticmethod
    def _fsize(v):
        try:
            return float(v.ap.free_size())
        except Exception:
            return 256.0

    def op(self, eng, fn, reads, writes, cost=None):
        e = self.engs[eng]
        if cost is None:
            n = self._fsize(writes[0]) if writes else 256.0
            cost = {"pe": 0.07 + n / 1200.0, "v": 0.12 + n / 960.0, "s": 0.2 + n / 1200.0, "g": 0.15 + n / 500.0}[eng]
        self._est(eng, cost, reads, writes)
        self._deps(e, reads, writes)
        ins = fn(e.obj)
        e.cnt += 1
        ins.then_inc(e.sem, 1)
        self._mark((e.name, e.cnt), reads, writes)
        return ins

    def dma(self, q, out, in_):
        issuer = self.dq_eng[q]
        reads = [in_] if isinstance(in_, V) else []
        writes = [out] if isinstance(out, V) else []
        self._deps(issuer, reads, writes)
        self._est("q_" + q, 0.6, [], [])
        self.efree["q_" + q] -= 0.0
        self._est("dma_" + q + str(self.drr[q] % 4), 2.5, reads, writes)
        de = self.dsems[q][self.drr[q] % 8]
        self.drr[q] += 1
        oap = out.ap if isinstance(out, V) else out
        iap = in_.ap if isinstance(in_, V) else in_
        ins = self.dmaq[q].dma_start(out=oap, in_=iap)
        de.cnt += 16
        ins.then_inc(de.sem, 16)
        self._mark((de.name, de.cnt), reads, writes)

    def wait_all(self, eng):
        e = self.engs[eng]
        for nm, o in self.engs.items():
            if o.cnt > 0 and e.waited.get(nm, 0) < o.cnt and nm != e.name:
                e.obj.wait_ge(o.sem, o.cnt)
                e.waited[nm] = o.cnt

    def _pe_rowkey(self, ap, out):
        key = (ap.base_partition(), ap.partition_size())
        e = self.engs["pe"]
        if not hasattr(self, "_bank_last"):
            self._bank_last = {}
        bid = id(out.trk[0])
        last = self._bank_last.get(bid)
        if last is not None and last[0] != key and e.waited.get("pe", 0) < last[1]:
            e.obj.wait_ge(e.sem, last[1])
            e.waited["pe"] = last[1]
        self._bank_last[bid] = (key, e.cnt + 1)

    def mm(self, out, lhsT, rhs, start=True, stop=True):
        rd = [lhsT, rhs] + ([] if start else [out])
        self._pe_rowkey(lhsT.ap, out)
        return self.op("pe", lambda o: o.matmul(out.ap, lhsT.ap, rhs.ap, start=start, stop=stop), rd, [out],
                       cost=0.07 + self._fsize(rhs) / 1200.0)

    def tr(self, out, in_, ident):
        if in_.ap.dtype == BF16:
            k = in_.ap.partition_size()
            return self.mm(out, in_, self.ident_bf[0:k, 0:k])
        self._pe_rowkey(in_.ap, out)
        return self.op("pe", lambda o: o.transpose(out.ap, in_.ap, ident.ap), [in_, ident], [out])

    def act(self, out, in_, func, bias=None, scale=1.0, accum=None):
        rd = [in_]
        wr = [out]
        kw = {}
        if isinstance(bias, V):
            rd.append(bias)
            kw["bias"] = bias.ap
        elif bias is not None:
            kw["bias"] = bias
        kw["scale"] = scale
        if accum is not None:
            kw["accum_out"] = accum.ap
            wr.append(accum)
        return self.op("s", lambda o: o.activation(out.ap, in_.ap, func, **kw), rd, wr)

    def copy(self, eng, out, in_):
        if eng == "s":
            return self.op("s", lambda o: o.copy(out.ap, in_.ap), [in_], [out])
        return self.op(eng, lambda o: o.tensor_copy(out.ap, in_.ap), [in_], [out])

    def evac(self, out, in_):
        self.flip = (self.flip + 1) % EVR
        return self.copy("s" if self.flip else "v", out, in_)

    def tt(self, out, a, b, op, eng="v"):
        return self.op(eng, lambda o: o.tensor_tensor(out.ap, a.ap, b.ap, op), [a, b], [out])

    def ts(self, out, a, s1, op0, s2=None, op1=None, eng="v"):
        rd = [a]
        x1, x2 = s1, s2
        if isinstance(s1, V):
            rd.append(s1)
            x1 = s1.ap
        if isinstance(s2, V):
            rd.append(s2)
            x2 = s2.ap
        if op1 is None:
            return self.op(eng, lambda o: o.tensor_single_scalar(out.ap, a.ap, x1, op0), rd, [out])
        return self.op(eng, lambda o: o.tensor_scalar(out.ap, a.ap, x1, x2, op0, op1), rd, [out])

    def stt(self, out, in0, scalar, in1, op0, op1, eng="v"):
        rd = [in0, in1]
        sc = scalar
        if isinstance(scalar, V):
            rd.append(scalar)
            sc = scalar.ap
        return self.op(eng, lambda o: o.scalar_tensor_tensor(out.ap, in0.ap, sc, in1.ap, op0, op1), rd, [out])

    def red(self, out, in_, eng="v"):
        return self.op(eng, lambda o: o.tensor_reduce(out.ap, in_.ap, AX.X, ALU.add), [in_], [out])

    def recip(self, out, in_):
        return self.op("v", lambda o: o.reciprocal(out.ap, in_.ap), [in_], [out])

    def memset(self, out, val, eng="v"):
        return self.op(eng, lambda o: o.memset(out.ap, val), [], [out])


def make_consts():
    i = np.arange(128)
    same = (i[:, None] // 64) == (i[None, :] // 64)
    c = {}
    c["ident"] = np.eye(128)
    c["ones"] = np.ones((128, 128))
    c["mcum"] = (same & (i[:, None] <= i[None, :])) * 1.0
    c["mcumx"] = (same & (i[:, None] < i[None, :])) * 1.0
    c["mrest"] = (same & (i[:, None] > i[None, :])) * 1.0
    c["m_strict"] = (same & (i[:, None] < i[None, :])) * 1.0
    c["m_incl"] = (same & (i[:, None] <= i[None, :])) * 1.0
    c["m_incl_neg"] = -c["m_incl"]
    c["nm_strict"] = np.where(c["m_strict"] > 0, 0.0, NEG)
    c["nm_incl"] = np.where(c["m_incl"] > 0, 0.0, NEG)
    cs = np.zeros((128, 128))
    cs[:64, 0] = 1.0
    cs[64:, 1] = 1.0
    c["chunksel"] = cs
    p = np.arange(128)
    bq = np.zeros((128, 4, 128))
    for h in range(4):
        bq[32 * h:32 * h + 32, h, :] = 1.0
    bs = np.zeros((128, 256))
    for h in range(4):
        bs[32 * h:32 * h + 32, 64 * h:64 * h + 64] = 1.0
    names = ["ident", "ones", "mcum", "mcumx", "mrest", "m_strict", "m_incl", "m_incl_neg", "nm_strict",
             "nm_incl", "chunksel"]
    arr = np.concatenate([c[n] for n in names] + [bq.reshape(128, 512), bs], axis=1).astype(np.float32)
    offs = {n: k * 128 for k, n in enumerate(names)}
    offs["bq"] = len(names) * 128
    offs["bs"] = len(names) * 128 + 512
    return arr, offs


CONSTS, COFF = make_consts()
NCONST = CONSTS.shape[1]

GW = 256
GDN0 = 0
RW0 = 4 * GW + 8
SC0 = RW0 + 4 * GW + 128
GLA0 = SC0 + 4 * GW


def block_cols():
    blocks = []
    for b in range(8):
        blocks.append(np.arange(GDN0 + b * 128, GDN0 + (b + 1) * 128))
    for b in range(9):
        blocks.append(np.arange(RW0 + b * 128, RW0 + (b + 1) * 128))
    for b in range(8):
        blocks.append(np.arange(SC0 + b * 128, SC0 + (b + 1) * 128))
    for b in range(6):
        blocks.append(np.arange(GLA0 + b * 128, GLA0 + (b + 1) * 128))
    last = np.full(128, -1)
    last[:16] = np.arange(GLA0 + 768, GLA0 + 784)
    blocks.append(last)
    return blocks


BLOCKS = block_cols()
NBLK = len(BLOCKS)
MIX_BLK = {"gdn": (0, 8), "rwkv": (8, 9), "sc": (17, 8), "gla": (25, 7)}

BC = {}
_o = 0
for _n, _w in (("gdn_norm_w", 64), ("gla_norm_w", 64), ("k_k", 256), ("k_a", 256), ("r_k", 256), ("ln_w", 256),
               ("ln_b", 256), ("w0", 256), ("a0", 256), ("gla_bias", 128), ("post_w", 1024), ("a_log", 4),
               ("dt_bias", 4)):
    BC[_n] = (_o, _w)
    _o += _w
NBC = _o


def prep_layer_inputs(inp, L):
    out = {}
    w_in = inp["w_in"]
    wb = np.zeros((L, NBLK, 128, 8, 128), np.float32)
    for bi, cols in enumerate(BLOCKS):
        valid = cols >= 0
        sub = w_in[:, :, cols[valid]]
        sub = sub.reshape(L, 8, 128, -1).transpose(0, 2, 1, 3)
        wb[:, bi, :, :, :sub.shape[-1]] = sub
    out["wblk"] = wb.reshape(L * NBLK * 128, 8 * 128)
    ab = w_in[:, :, GDN0 + 1024:GDN0 + 1032].reshape(L, 8, 128, 8).transpose(0, 2, 1, 3)
    out["wab"] = np.ascontiguousarray(ab).reshape(L * 128, 64)
    wo = inp["w_out"].reshape(L, 8, 128, 1024).transpose(0, 2, 1, 3)
    out["wout"] = np.ascontiguousarray(wo).reshape(L * 128, 8 * 1024)
    bc = np.zeros((L, NBC), np.float32)

    def put(n, a):
        o, w = BC[n]
        bc[:, o:o + w] = a
    put("gdn_norm_w", inp["gdn_norm_w"]); put("gla_norm_w", inp["gla_norm_w"])
    put("k_k", inp["rwkv_k_k"]); put("k_a", inp["rwkv_k_a"]); put("r_k", inp["rwkv_r_k"])
    put("ln_w", inp["rwkv_ln_w"]); put("ln_b", inp["rwkv_ln_b"]); put("w0", inp["rwkv_w0"]); put("a0", inp["rwkv_a0"])
    put("gla_bias", inp["gla_a_bias"]); put("post_w", inp["post_norm_w"])
    put("a_log", inp["gdn_a_log"]); put("dt_bias", inp["gdn_dt_bias"])
    out["bcp"] = bc
    pp = np.zeros((L, 128, 8 + 24 + 9 + 6), np.float32)
    pp[:, :, 0:8] = inp["pre_norm_w"].reshape(L, 8, 128).transpose(0, 2, 1)
    g = inp["gdn_conv_w"].reshape(L, 4, 6, 128).transpose(0, 3, 2, 1)
    pp[:, :, 8:32] = g.reshape(L, 128, 24)
    pp[:, :, 32:41] = inp["rwkv_mu"].reshape(L, 9, 128).transpose(0, 2, 1)
    s = inp["sc_conv_w"].reshape(L, 3, 2, 128).transpose(0, 3, 2, 1)
    pp[:, :, 41:47] = s.reshape(L, 128, 6)
    out["ppar"] = pp.reshape(L * 128, 47)
    up = np.zeros((L, 128, 256), np.float32)
    up[:, 0:64] = inp["rwkv_w_up"]
    up[:, 64:128] = inp["rwkv_a_up"]
    out["wup"] = up.reshape(L * 128, 256)
    gu = np.zeros((L, 128, 128), np.float32)
    gu[:, 0:16] = inp["gla_a_up"]
    out["gup"] = gu.reshape(L * 128, 128)
    return out


def build(NB, T, L, mixers=("gdn", "rwkv", "sc", "gla")):
    nc = bass.Bass("TRN2", target_bir_lowering=False)
    ntok = NB * T
    nmt = T // MT

    def din(name, shape):
        return nc.dram_tensor(name, list(shape), F32, kind="ExternalInput").ap()
    x_d = din("x", [ntok, D])
    wblk_d = din("wblk", [L * NBLK * 128, 1024])
    wab_d = din("wab", [L * 128, 64])
    wout_d = din("wout", [L * 128, 8192])
    bcp_d = din("bcp", [L, NBC])
    ppar_d = din("ppar", [L * 128, 47])
    wup_d = din("wup", [L * 128, 256])
    gup_d = din("gup", [L * 128, 128])
    cst_d = din("consts", [128, NCONST])
    out_d = nc.dram_tensor("out", [ntok, D], F32, kind="ExternalOutput").ap()
    mid_d = nc.dram_tensor("xmid", [ntok, D], F32).ap() if L > 1 else None

    with contextlib.ExitStack() as ctx:
        ctx.enter_context(nc.allow_low_precision("bf16 projection operands, fp32 accumulation"))
        kb = KB(nc, ctx)
        sb, ps = kb.sb, kb.ps
        cst = sb("cst", [128, NCONST])
        kb.dma("sp", cst[:], cst_d[:, :])

        def C(n, w=128):
            o = COFF[n]
            return cst[:, o:o + w]

        def Cb4(n):
            o = COFF[n]
            return cst.v(cst.h[:, o:o + 128].unsqueeze(1).to_broadcast([128, 4, 128]))
        ident = C("ident")
        ident_bf = sb("ident_bf", [128, 128], BF16)
        kb.copy("v", ident_bf[:, :], ident)
        kb.ident_bf = ident_bf

        bcp = sb("bcp", [128, NBC]); ppar = sb("ppar", [128, 47]); wab = sb("wab", [128, 64], BF16)
        wout = sb("wout", [128, 8192], BF16); wup = sb("wup", [128, 256], BF16); gup = sb("gup", [128, 128])
        nal = sb("nal", [128, 4])
        NRING = 8
        wbf = [sb("wbf%d" % i, [128, 1024], BF16) for i in range(NRING)]
        xts = [sb("xt%d" % i, [128, 1024]) for i in range(2)]
        ht = sb("ht", [128, 1024], BF16)
        hT = sb("hT", [128, 8 * MT], BF16)
        yT = sb("yT", [128, 8 * MT], BF16)
        P_s = sb("P_s", [128, 8 * (MT + 4)]); xres = sb("xres", [128, 1024])
        P_a = sb("P_a", [128, 8 * (MT + 4)]); P_g = sb("P_g", [128, 8 * (MT + 4)]); P_r = sb("P_r", [128, 9 * (MT + 4)])
        Q_g = sb("Q_g", [128, 6 * MT], BF16); Q_r = sb("Q_r", [128, 9 * MT], BF16)
        ZS_a = sb("ZS_a", [128, 2 * MT]); ZS_g = sb("ZS_g", [128, 2 * MT]); ZS_r = sb("ZS_r", [128, 2 * MT])
        st = {m: sb("st_" + m, [128, 256]) for m in ("gdn", "rwkv", "gla")}
        stb = {m: sb("stb_" + m, [128, 256], BF16) for m in ("gdn", "rwkv")}
        hist = {"gdn": sb("h_gdn", [128, 6 * 3]), "rwkv": sb("h_rwkv", [128, 9]), "sc": sb("h_sc", [128, 4])}
        banks = [ps("bk%d" % i, [128, 512]) for i in range(8)]
        bctr = [0]

        def bank():
            b = min(banks[4:8], key=lambda t: (max(t.trk.tw, t.trk.tr), id(t)))
            bctr[0] += 1
            b.trk.tr = max(b.trk.tr, kb.step_max, max(kb.efree.values()) if kb.efree else 0.0) + 1e-3
            return b
        pjctr = [0]

        def pbank():
            b = banks[pjctr[0] % 2]
            pjctr[0] += 1
            return b
        scr = {}

        ALIAS = {"sc_acc": "gl_qblk", "sc_zs": "gl_ex", "rw_d": "rw_eg"}

        def S(name, w=512, dt=F32):
            name = ALIAS.get(name, name)
            if name not in scr:
                scr[name] = sb("s_" + name, [128, w], dt)
            return scr[name]

        def BCv(n, rows=128):
            o, w = BC[n]
            return bcp[0:rows, o:o + w]

        def r3(tile, h, a=0, b=None):
            b = b if b is not None else tile.h.shape[1]
            return tile.v(tile.h[:, a:b].rearrange("p (h t) -> p h t", h=h))

        def bc_t(tile, a, nh, n):
            return tile.v(tile.h[:, a:a + nh].unsqueeze(2).to_broadcast([128, nh, n]))

        def bc_h(tile, a, w, nh):
            return tile.v(tile.h[:, a:a + w].unsqueeze(1).to_broadcast([128, nh, w]))

        def Uv(tile, rows=slice(0, 128)):
            return tile.v(tile.h[rows, 0:512].rearrange("p (h t) -> p h t", h=4)[:, :, 0:64])

        def Wv(tile, rows=slice(0, 128)):
            return tile.v(tile.h[rows, 0:512].rearrange("p (h t) -> p h t", h=4)[:, :, 64:128])

        def r3r(tile, rows, h, a, b):
            return tile.v(tile.h[rows, a:b].rearrange("p (h t) -> p h t", h=h))
        HO = (0, 2, 1, 3)

        def rsqrt_small(out, in_, scale, eps):
            kb.act(out, in_, AF.Ln, bias=eps, scale=scale)
            kb.act(out, out, AF.Exp, scale=-0.5)

        def solve(Nm, UW, pfx):
            Am = S(pfx + "sol_A0", 512, BF16)
            pb = bank()
            for h in range(4):
                kb.tr(pb[:, h * 128:(h + 1) * 128], Nm[:, h * 128:(h + 1) * 128], ident)
            kb.evac(Am[:, :], pb[:, :])
            yield
            curN, curA = Nm, Am
            for lvl in range(6):
                pb = bank()
                for h in range(4):
                    hs = slice(h * 128, (h + 1) * 128)
                    kb.mm(pb[:, hs], curN[:, hs], UW[:, hs])
                if lvl < 5:
                    nN = S(pfx + "sol_N%d" % (lvl % 2), 512, BF16)
                    pn = bank()
                    for h in range(4):
                        hs = slice(h * 128, (h + 1) * 128)
                        kb.mm(pn[:, hs], curA[:, hs], curN[:, hs])
                if lvl < 4:
                    nA = S(pfx + "sol_A%d" % ((lvl + 1) % 2), 512, BF16)
                    pa = bank()
                    for h in range(4):
                        hs = slice(h * 128, (h + 1) * 128)
                        kb.mm(pa[:, hs], curN[:, hs], curA[:, hs])
                kb.tt(UW[:, :], UW[:, :], pb[:, :], ALU.subtract if lvl == 0 else ALU.add)
                if lvl == 5:
                    break
                kb.evac(nN[:, :], pn[:, :])
                if lvl < 4:
                    kb.evac(nA[:, :], pa[:, :])
                    curA = nA
                curN = nN
                yield

        def fmh(tile, h, c0, c1, base=0):
            pb_, cb = 64 * (h % 2), h // 2
            return tile[pb_:pb_ + 64, base + cb * 128 + c0: base + cb * 128 + c1]

        def sth(Sx, h):
            pb_, cb = 64 * (h % 2), h // 2
            return Sx[pb_:pb_ + 64, cb * 64:(cb + 1) * 64]

        def seq_core(pfx, po, Sx, Sb, UW, sign, wT, wbase, qT, qbase, attnT, kend, dS, attn2T=None, kend2=None, V2=None):
            X = S(pfx + "seq_X", 256, BF16)
            for c in range(2):
                cs = slice(64 * c, 64 * c + 64)
                pw = bank()
                for h in HO:
                    kb.mm(pw[cs, h * 64:(h + 1) * 64], wT[:, wbase + (h // 2) * 128 + 64 * c: wbase + (h // 2) * 128 + 64 * c + 64],
                          Sb[:, h * 64:(h + 1) * 64])
                kb.tt(r3r(X, cs, 4, 0, 256), Uv(UW, cs), r3r(pw, cs, 4, 0, 256), ALU.subtract if sign < 0 else ALU.add)
                yield
                for h in (HO if c == 0 else (1, 3, 0, 2)):
                    o_ = po[cs, h * 64:(h + 1) * 64]
                    kb.mm(o_, qT[:, qbase + (h // 2) * 128 + 64 * c: qbase + (h // 2) * 128 + 64 * c + 64],
                          Sb[:, h * 64:(h + 1) * 64], start=True, stop=False)
                    kb.mm(o_, attnT[:, h * 128 + 64 * c: h * 128 + 64 * c + 64], X[:, h * 64:(h + 1) * 64],
                          start=False, stop=(attn2T is None))
                    if attn2T is not None:
                        kb.mm(o_, attn2T[:, h * 128 + 64 * c: h * 128 + 64 * c + 64], V2[:, h * 64:(h + 1) * 64],
                              start=False, stop=True)
                pst = bank()
                for h in range(4):
                    o_ = sth(pst, h)
                    kb.mm(o_, kend[cs, h * 64:(h + 1) * 64], X[cs, h * 64:(h + 1) * 64], start=True,
                          stop=(kend2 is None))
                    if kend2 is not None:
                        kb.mm(o_, kend2[cs, h * 64:(h + 1) * 64], V2[cs, h * 64:(h + 1) * 64], start=False, stop=True)
                for cb in range(2):
                    kb.stt(Sx[:, cb * 64:(cb + 1) * 64], Sx[:, cb * 64:(cb + 1) * 64], dS[:, 2 * cb + c:2 * cb + c + 1],
                           pst[:, cb * 64:(cb + 1) * 64], ALU.mult, ALU.add)
                for i_ in range(2):
                    rows_ = slice(64 * i_, 64 * i_ + 64)
                    kb.copy("s", Sb.v(Sb.h[rows_, 0:256].rearrange("p (c i t) -> p c i t", c=2, i=2)[:, :, i_, :]),
                            Sx.v(Sx.h[rows_, 0:128].rearrange("p (c t) -> p c t", c=2)))
                yield

        def head_norm_gate(o_sb, normw, yblk0, tsl, ZS, pfx):
            sq = S(pfx + "hn_sq", 256)
            ss = S(pfx + "hn_ss", 8)
            kb.tt(sq[:, :], o_sb[:, 0:256], o_sb[:, 0:256], ALU.mult)
            kb.red(ss[:, 0:4], r3(sq, 4))
            yield
            rsqrt_small(ss[:, 4:8], ss[:, 0:4], 1.0 / 64, EPS)
            yield
            kb.tt(r3(sq, 4), r3(o_sb, 4, 0, 256), bc_t(ss, 4, 4, 64), ALU.mult)
            ob = S(pfx + "hn_ob", 256, BF16)
            kb.tt(r3(ob, 4), r3(sq, 4), bc_h(bcp, BC[normw][0], 64, 4), ALU.mult)
            yield
            to_fm_gate(ob, yblk0, tsl, ZS)

        def to_fm_gate(o_tm, yblk0, tsl, ZS):
            pb = bank()
            for blk in range(2):
                kb.tr(pb[:, blk * 128:(blk + 1) * 128], o_tm[:, blk * 128:(blk + 1) * 128], ident)
            for blk in range(2):
                kb.tt(yT[:, (yblk0 + blk) * MT + tsl.start:(yblk0 + blk) * MT + tsl.stop],
                      pb[:, blk * 128:(blk + 1) * 128],
                      ZS[:, blk * MT + tsl.start: blk * MT + tsl.stop], ALU.mult)

        wseq = []
        wstate = {"issued": 0}

        def wissue(upto):
            while wstate["issued"] < min(upto, len(wseq)):
                i = wstate["issued"]
                l, b = wseq[i]
                r0 = (l * NBLK + b) * 128
                kb.dma("gq", wbf[i % NRING][:], wblk_d[r0:r0 + 128, :])
                wstate["issued"] += 1
        used_blocks = []
        for m in ("sc", "gla", "gdn", "rwkv"):
            if m in mixers:
                b0, nb_ = MIX_BLK[m]
                used_blocks += list(range(b0, b0 + nb_))
        for l in range(L):
            for b in range(NB):
                for mt in range(nmt):
                    for blk in used_blocks:
                        wseq.append((l, blk))
        wptr = [0]

        def project(m, dst, stride, off):
            b0, nb_ = MIX_BLK[m]
            for bi in range(nb_):
                i = wptr[0]
                wissue(i + NRING - 1)
                wt = wbf[i % NRING]
                pb = pbank()
                M = 16 if (m == "gla" and bi == 6) else 128
                for kc in range(8):
                    kb.mm(pb[0:M, 0:MT], wt[:, kc * 128:kc * 128 + M], hT[:, kc * MT:(kc + 1) * MT],
                          start=(kc == 0), stop=(kc == 7))
                kb.evac(dst[0:M, bi * stride + off: bi * stride + off + MT], pb[0:M, 0:MT])
                wptr[0] += 1
                if (bi + 1) % KPB == 0 or bi == nb_ - 1:
                    yield

        def step_stream(clock, g_):
            kb.step_max = 0.0
            try:
                next(g_)
            except StopIteration:
                return False
            if kb.step_max > 0.0:
                clock[g_] = max(clock[g_], kb.step_max)
            else:
                others = [v for k, v in clock.items() if k is not g_]
                clock[g_] = max(clock[g_], min(others) if others else 0.0) + 0.3
            return True

        def run_streams(gens):
            gens = list(gens)
            clock = {g_: 0.0 for g_ in gens}
            while gens:
                g_ = min(gens, key=lambda x: clock[x])
                if not step_stream(clock, g_):
                    gens.remove(g_)
                    del clock[g_]
        pendingT = [None]

        for l in range(L):
            src_d = x_d if l == 0 else mid_d
            dst_d = out_d if l == L - 1 else mid_d
            kb.dma("sp", bcp[:], bcp_d[l:l + 1, :].partition_broadcast(128))
            kb.dma("sp", ppar[:], ppar_d[l * 128:(l + 1) * 128, :])
            kb.dma("gq", wab[:], wab_d[l * 128:(l + 1) * 128, :])
            for q8 in range(8):
                kb.dma("gq", wout[:, q8 * 1024:(q8 + 1) * 1024], wout_d[l * 128:(l + 1) * 128, q8 * 1024:(q8 + 1) * 1024])
            kb.dma("gq", wup[:], wup_d[l * 128:(l + 1) * 128, :])
            kb.dma("sp", gup[:], gup_d[l * 128:(l + 1) * 128, :])
            kb.act(nal[:, :], BCv("a_log"), AF.Exp)
            kb.ts(nal[:, :], nal[:, :], -1.0, ALU.mult)
            for b in range(NB):
                for m_ in st:
                    kb.memset(st[m_][:, :], 0.0)
                for m_ in stb:
                    kb.memset(stb[m_][:, :], 0.0)
                for nm_ in ("gd_seq_X", "rw_seq_X"):
                    kb.memset(S(nm_, 256, BF16)[:, :], 0.0)
                for m_ in hist:
                    kb.memset(hist[m_][:, :], 0.0)
                for mt in range(nmt):
                    tok0 = b * T + mt * MT
                    def gen_H1(tok0=tok0):
                        for j in range(MT // 128):
                            xt = xts[j % 2]
                            r0 = tok0 + j * 128
                            kb.dma("sp", xt[:], src_d[r0:r0 + 128, :])
                            ss = S("n_ss", 8)
                            junk = P_g
                            kb.act(junk[:, 0:1024], xt[:, :], AF.Square)
                            kb.red(ss[:, 0:1], junk[:, 0:1024])
                            rsqrt_small(ss[:, 1:2], ss[:, 0:1], 1.0 / D, EPS)
                            kb.ts(ht[:, :], xt[:, :], ss[:, 1:2], ALU.mult)
                            yield
                            for half in range(2):
                                pb = bank()
                                for q in range(4):
                                    kc = half * 4 + q
                                    kb.tr(pb[:, q * 128:(q + 1) * 128], ht[:, kc * 128:(kc + 1) * 128], ident)
                                outv = hT.v(hT.h[:, :].rearrange("p (k t) -> p k t", k=8)[:, half * 4:half * 4 + 4,
                                                                                          j * 128:(j + 1) * 128])
                                kb.tt(outv, r3(pb, 4), bc_t(ppar, half * 4, 4, 128), ALU.mult)
                                yield
                        if "sc" in mixers:
                            yield from project("sc", P_s, MT + 4, 4)
                        if "gla" in mixers:
                            yield from project("gla", P_a, MT + 4, 4)
                    def gen_sc():
                        W2 = MT + 4
                        W2 = MT + 4
                        for blk in range(2):
                            u = S("sc_u%d" % blk, MT + 2)
                            kb.copy("g", u[:, 0:2], hist["sc"][:, blk * 2:blk * 2 + 2])
                            kb.tt(u[:, 2:MT + 2], P_s[:, (2 + blk) * W2 + 4:(2 + blk) * W2 + 4 + MT],
                                  P_s[:, (4 + blk) * W2 + 4:(4 + blk) * W2 + 4 + MT], ALU.mult)
                            kb.copy("g", hist["sc"][:, blk * 2:blk * 2 + 2], u[:, MT:MT + 2])
                            acc_t = S("sc_acc")
                            acc = acc_t[:, 0:MT]
                            kb.ts(acc, u[:, 0:MT], ppar[:, 41 + blk * 3:42 + blk * 3], ALU.mult)
                            for tap in (1, 2):
                                kb.stt(acc, u[:, tap:tap + MT], ppar[:, 41 + blk * 3 + tap:42 + blk * 3 + tap],
                                       acc, ALU.mult, ALU.add)
                            zs_t = S("sc_zs")
                            zs = zs_t[:, 0:MT]
                            kb.act(zs, P_s[:, (6 + blk) * W2 + 4:(6 + blk) * W2 + 4 + MT], AF.Silu)
                            kb.tt(acc, acc, P_s[:, blk * W2 + 4: blk * W2 + 4 + MT], ALU.mult)
                            kb.tt(yT[:, (4 + blk) * MT:(5 + blk) * MT], acc, zs, ALU.mult)
                            yield

                    def gen_gla():
                        W2 = MT + 4
                        W2 = MT + 4
                        for blk in range(2):
                            kb.act(ZS_a[:, blk * MT:(blk + 1) * MT], P_a[:, (4 + blk) * W2 + 4:(4 + blk) * W2 + 4 + MT],
                                   AF.Silu)
                        for j in range(MT // 128):
                            tsl = slice(j * 128, (j + 1) * 128)

                            def Pg(blk, rows=slice(0, 128)):
                                return P_a[rows, blk * W2 + 4 + j * 128: blk * W2 + 4 + (j + 1) * 128]
                            pb = bank()
                            kb.mm(pb[:, 0:128], Pg(6, slice(0, 16)), gup[0:16, :])
                            sp_ = S("gl_sp", 128)
                            kb.tt(sp_[:, :], pb[:, 0:128], BCv("gla_bias"), ALU.add)
                            yield
                            kb.act(sp_[:, :], sp_[:, :], AF.Exp, scale=-1.0)
                            yield
                            kb.act(sp_[:, :], sp_[:, :], AF.Ln, bias=1.0)
                            yield
                            pc = bank()
                            kb.mm(pc[:, 0:128], sp_[:, :], C("mcum"))
                            kb.mm(pc[:, 128:256], C("mrest"), sp_[:, :])
                            kb.mm(pc[:, 256:258], sp_[:, :], C("chunksel", 2))
                            kb.tr(pc[:, 384:512], Pg(1), ident)
                            ex = S("gl_ex", 512)
                            kb.act(ex[:, 0:128], pc[:, 0:128], AF.Exp, scale=-1.0 / 16)
                            kb.act(ex[:, 128:256], pc[:, 0:128], AF.Exp, scale=1.0 / 16)
                            kb.act(ex[:, 256:384], pc[:, 128:256], AF.Exp, scale=-1.0 / 16)
                            dS = S("gl_dS", 8)
                            kb.act(dS[:, 0:2], pc[:, 256:258], AF.Exp, scale=-1.0 / 16)
                            qe = S("gl_qe", 128); ke = S("gl_ke", 128); kend = S("gl_kend", 128)
                            kb.stt(qe[:, :], ex[:, 0:128], 32.0 ** -0.5, Pg(0), ALU.mult, ALU.mult)
                            kb.tt(ke[:, :], ex[:, 128:256], Pg(1), ALU.mult)
                            kb.tt(kend[:, :], ex[:, 256:384], pc[:, 384:512], ALU.mult)
                            yield
                            pv = bank()
                            for blk in range(2):
                                kb.tr(pv[:, blk * 128:(blk + 1) * 128], Pg(2 + blk), ident)
                            vtm = S("gl_v", 256)
                            kb.evac(vtm[:, :], pv[:, 0:256])
                            qblk = S("gl_qblk")
                            kb.tt(r3(qblk, 4), bc_h(qe, 0, 128, 4),
                                  cst.v(cst.h[:, COFF["bq"]:COFF["bq"] + 512].rearrange("p (h t) -> p h t", h=4)),
                                  ALU.mult)
                            pa = bank()
                            kb.mm(pa[:, :], ke[:, :], qblk[:, :])
                            attnT = S("gl_attn")
                            kb.tt(r3(attnT, 4), r3(pa, 4), Cb4("m_incl"), ALU.mult)
                            yield
                            Sg = st["gla"]
                            po = banks[3]
                            for c in range(2):
                                cs = slice(64 * c, 64 * c + 64)
                                kb.mm(po[cs, 256:512], qe[:, cs], Sg[:, :], start=True, stop=False)
                                for h in range(4):
                                    kb.mm(po[cs, 256 + h * 64:256 + (h + 1) * 64], attnT[:, h * 128 + 64 * c:h * 128 + 64 * c + 64],
                                          vtm[:, h * 64:(h + 1) * 64], start=False, stop=(h == 3))
                                yield
                                pst = bank()
                                kb.mm(pst[:, 0:256], kend[cs, :], vtm[cs, :])
                                tmp = S("gl_tmp", 256)
                                kb.tt(tmp[:, :], pst[:, 0:256], C("bs", 256), ALU.mult)
                                kb.stt(Sg[:, :], Sg[:, :], dS[:, c:c + 1], tmp[:, :], ALU.mult, ALU.add)
                                yield
                            osb = S("gl_o", 256)
                            kb.evac(osb[:, :], po[:, 256:512])
                            yield from head_norm_gate(osb, "gla_norm_w", 6, tsl, ZS_a, "gl_")
                            yield

                    def gen_gdn():
                        W2 = MT + 4
                        W2 = MT + 4
                        for blk in range(6):
                            kb.copy("g", P_g[:, blk * W2 + 1: blk * W2 + 4], hist["gdn"][:, blk * 3:blk * 3 + 3])
                            kb.copy("g", hist["gdn"][:, blk * 3:blk * 3 + 3], P_g[:, blk * W2 + 1 + MT: blk * W2 + 4 + MT])
                            acc = S("cv_acc%d" % (blk % 2), MT)[:, 0:MT]
                            kb.ts(acc, P_g[:, blk * W2 + 1: blk * W2 + 1 + MT], ppar[:, 8 + blk * 4:9 + blk * 4], ALU.mult)
                            for tap in (1, 2, 3):
                                yield
                                kb.stt(acc, P_g[:, blk * W2 + 1 + tap: blk * W2 + 1 + tap + MT],
                                       ppar[:, 8 + blk * 4 + tap:9 + blk * 4 + tap], acc, ALU.mult, ALU.add)
                            yield
                            kb.act(Q_g[:, blk * MT:(blk + 1) * MT], acc, AF.Silu)
                            yield
                        for blk in range(2):
                            kb.act(ZS_g[:, blk * MT:(blk + 1) * MT], P_g[:, (6 + blk) * W2 + 4:(6 + blk) * W2 + 4 + MT],
                                   AF.Silu)
                        for j in range(MT // 128):
                            tsl = slice(j * 128, (j + 1) * 128)

                            def Qg(blk, rows=slice(0, 128)):
                                return Q_g[rows, blk * MT + j * 128: blk * MT + (j + 1) * 128]
                            pq = bank(); pv = bank()
                            for blk in range(4):
                                kb.tr(pq[:, blk * 128:(blk + 1) * 128], Qg(blk), ident)
                            for blk in range(2):
                                kb.tr(pv[:, blk * 128:(blk + 1) * 128], Qg(4 + blk), ident)
                            qk = S("gd_qk"); vtm = S("gd_v", 256, BF16)
                            kb.evac(qk[:, :], pq[:, :])
                            kb.evac(vtm[:, :], pv[:, 0:256])
                            yield
                            pab = bank()
                            for kc in range(8):
                                kb.mm(pab[:, 0:8], hT[:, kc * MT + j * 128: kc * MT + (j + 1) * 128],
                                      wab[:, kc * 8:(kc + 1) * 8], start=(kc == 0), stop=(kc == 7))
                            sc = S("gd_sc", 64)
                            kb.tt(sc[:, 0:4], pab[:, 0:4], BCv("dt_bias"), ALU.add)
                            kb.act(sc[:, 4:8], pab[:, 4:8], AF.Exp, scale=-1.0)
                            yield
                            kb.act(sc[:, 0:4], sc[:, 0:4], AF.Exp)
                            yield
                            kb.act(sc[:, 0:4], sc[:, 0:4], AF.Ln, bias=1.0)
                            yield
                            kb.tt(sc[:, 0:4], sc[:, 0:4], nal[:, :], ALU.mult)
                            kb.act(sc[:, 4:8], sc[:, 4:8], AF.Ln, bias=1.0)
                            yield
                            kb.act(sc[:, 8:12], sc[:, 4:8], AF.Exp, scale=-1.0)
                            sq = S("gd_sq")
                            kb.tt(sq[:, :], qk[:, :], qk[:, :], ALU.mult)
                            yield
                            kb.red(sc[:, 12:20], r3(sq, 8))
                            yield
                            kb.act(sc[:, 20:28], sc[:, 12:20], AF.Ln, bias=EPS)
                            yield
                            kb.ts(sc[:, 20:28], sc[:, 20:28], -0.5, ALU.mult)
                            kb.ts(sc[:, 20:24], sc[:, 20:24], math.log(1.0 / 8.0), ALU.add)
                            yield
                            pg = bank()
                            kb.mm(pg[:, 0:4], C("mcum"), sc[:, 0:4])
                            kb.mm(pg[:, 4:8], C("mrest"), sc[:, 0:4])
                            gb = S("gd_gb", 256)
                            kb.copy("v", r3(gb, 4), bc_t(sc, 0, 4, 64))
                            for cb in range(2):
                                kb.mm(pg[:, 8 + 2 * cb: 10 + 2 * cb], gb[:, cb * 128:(cb + 1) * 128], C("chunksel", 2))
                            dS = S("gd_dS", 8)
                            kb.act(dS[:, 0:4], pg[:, 8:12], AF.Exp)
                            kb.copy("v", sc[:, 28:32], pg[:, 0:4])
                            kb.tt(sc[:, 52:56], pg[:, 4:8], sc[:, 24:28], ALU.add)
                            yield
                            kb.tt(sc[:, 32:36], sc[:, 28:32], sc[:, 4:8], ALU.subtract)
                            kb.act(sc[:, 52:56], sc[:, 52:56], AF.Exp)
                            yield
                            kb.tt(sc[:, 48:52], sc[:, 32:36], sc[:, 24:28], ALU.add)
                            kb.tt(sc[:, 36:40], sc[:, 24:28], sc[:, 28:32], ALU.subtract)
                            yield
                            kb.copy("v", sc[:, 32:36], sc[:, 48:52])
                            kb.tt(sc[:, 40:44], sc[:, 28:32], sc[:, 20:24], ALU.add)
                            yield
                            kb.act(sc[:, 44:48], sc[:, 40:44], AF.Exp)
                            kb.act(sc[:, 48:52], sc[:, 48:52], AF.Exp)
                            yield
                            UW = S("gd_UW", 512, BF16)
                            kb.tt(Uv(UW), r3(vtm, 4), bc_t(sc, 8, 4, 64), ALU.mult)
                            yield
                            kb.tt(Wv(UW), r3(qk, 4, 256, 512), bc_t(sc, 48, 4, 64), ALU.mult)
                            yield
                            kend = S("gd_kend", 256, BF16)
                            kb.tt(r3(kend, 4), r3(qk, 4, 256, 512), bc_t(sc, 52, 4, 64), ALU.mult)
                            yield
                            dq = S("gd_dq")
                            kb.tt(r3(dq, 4), Cb4("ident"), bc_t(sc, 44, 4, 128), ALU.mult)
                            pqg = bank()
                            for h in range(4):
                                pb_, cb = 64 * (h % 2), h // 2
                                kb.mm(pqg[pb_:pb_ + 64, cb * 128:(cb + 1) * 128], qk[:, h * 64:(h + 1) * 64],
                                      dq[:, h * 128:(h + 1) * 128])
                            fm = S("gd_fm", 512, BF16)
                            kb.evac(fm[:, 0:256], pqg[:, 0:256])
                            yield
                            rd = S("gd_rd"); cbm = S("gd_cb")
                            kb.tt(r3(rd, 4), Cb4("ident"), bc_t(sc, 32, 4, 128), ALU.mult)
                            kb.tt(r3(cbm, 4), Cb4("nm_strict"), bc_t(sc, 36, 4, 128), ALU.add)
                            pe1 = bank()
                            kb.mm(pe1[:, :], C("ones"), rd[:, :], start=True, stop=False)
                            kb.mm(pe1[:, :], ident, cbm[:, :], start=False, stop=True)
                            DTs = S("gd_DTs")
                            kb.act(DTs[:, :], pe1[:, :], AF.Exp)
                            yield
                            rd2 = S("gd_rd"); cb2 = S("gd_cb")
                            kb.tt(r3(rd2, 4), Cb4("ident"), bc_t(sc, 40, 4, 128), ALU.mult)
                            kb.tt(r3(cb2, 4), Cb4("nm_incl"), bc_t(sc, 36, 4, 128), ALU.add)
                            pe2 = bank()
                            kb.mm(pe2[:, :], C("ones"), rd2[:, :], start=True, stop=False)
                            kb.mm(pe2[:, :], ident, cb2[:, :], start=False, stop=True)
                            DTi = S("gd_DTi")
                            kb.act(DTi[:, :], pe2[:, :], AF.Exp)
                            yield
                            Nm = S("gd_N", 512, BF16); attnT = S("gd_attn", 512, BF16)
                            pkk = bank()
                            for h in HO:
                                pb_, cb = 64 * (h % 2), h // 2
                                kT = Qg(2 + cb, slice(pb_, pb_ + 64))
                                kb.mm(pkk[:, h * 128:(h + 1) * 128], kT, kT)
                            kb.tt(Nm[:, :], pkk[:, :], DTs[:, :], ALU.mult)
                            yield
                            pqk = bank()
                            for h in HO:
                                pb_, cb = 64 * (h % 2), h // 2
                                kT = Qg(2 + cb, slice(pb_, pb_ + 64))
                                qT_ = Qg(cb, slice(pb_, pb_ + 64))
                                kb.mm(pqk[:, h * 128:(h + 1) * 128], kT, qT_)
                            kb.tt(attnT[:, :], pqk[:, :], DTi[:, :], ALU.mult)
                            yield
                            yield from solve(Nm, UW, "gd_")
                            pwt = bank()
                            for h in range(4):
                                pb_, cb = 64 * (h % 2), h // 2
                                kb.tr(pwt[pb_:pb_ + 64, cb * 128:(cb + 1) * 128], UW[:, h * 128 + 64:(h + 1) * 128], ident)
                            kb.evac(fm[:, 256:512], pwt[:, 0:256])
                            yield
                            po = banks[2]
                            yield from seq_core("gd_", po, st["gdn"], stb["gdn"], UW, -1, fm, 256, fm, 0, attnT, kend, dS)
                            osb = S("gd_o", 256)
                            kb.evac(osb[:, :], po[:, 0:256])
                            yield from head_norm_gate(osb, "gdn_norm_w", 0, tsl, ZS_g, "gd_")
                            yield

                    def gen_rwkv():
                        W2 = MT + 4
                        W2 = MT + 4
                        for blk in range(9):
                            kb.copy("g", P_r[:, blk * W2 + 3: blk * W2 + 4], hist["rwkv"][:, blk:blk + 1])
                            kb.copy("g", hist["rwkv"][:, blk:blk + 1], P_r[:, blk * W2 + 3 + MT: blk * W2 + 4 + MT])
                            dlt_t = S("rw_d")
                            dlt = dlt_t[:, 0:MT]
                            kb.tt(dlt, P_r[:, blk * W2 + 3: blk * W2 + 3 + MT], P_r[:, blk * W2 + 4: blk * W2 + 4 + MT],
                                  ALU.subtract)
                            yield
                            kb.stt(Q_r[:, blk * MT:(blk + 1) * MT], dlt, ppar[:, 32 + blk:33 + blk],
                                   P_r[:, blk * W2 + 4: blk * W2 + 4 + MT], ALU.mult, ALU.add)
                            yield
                        for blk in range(2):
                            kb.act(ZS_r[:, blk * MT:(blk + 1) * MT], Q_r[:, (6 + blk) * MT:(7 + blk) * MT], AF.Silu)
                        kb.act(Q_r[0:64, 8 * MT:9 * MT], Q_r[0:64, 8 * MT:9 * MT], AF.Tanh)
                        for j in range(MT // 128):
                            tsl = slice(j * 128, (j + 1) * 128)

                            def Qr(blk, rows=slice(0, 128)):
                                return Q_r[rows, blk * MT + j * 128: blk * MT + (j + 1) * 128]
                            prk = bank(); pv = bank()
                            for blk in range(4):
                                kb.tr(prk[:, blk * 128:(blk + 1) * 128], Qr(blk), ident)
                            for blk in range(2):
                                kb.tr(pv[:, blk * 128:(blk + 1) * 128], Qr(4 + blk), ident)
                            rk = S("rw_rk"); vtm = S("rw_v", 256, BF16)
                            kb.evac(rk[:, :], prk[:, :])
                            kb.evac(vtm[:, :], pv[:, 0:256])
                            yield
                            pwa = bank(); pwa2 = bank()
                            kb.mm(pwa[:, 0:256], Qr(8, slice(0, 64)), wup[0:64, :])
                            kb.mm(pwa2[:, 256:512], Qr(8, slice(64, 128)), wup[64:128, :])
                            lw = S("rw_lw", 256); a_ = S("rw_a", 256)
                            kb.tt(lw[:, :], pwa[:, 0:256], BCv("w0"), ALU.add)
                            kb.tt(a_[:, :], pwa2[:, 256:512], BCv("a0"), ALU.add)
                            yield
                            kb.act(lw[:, :], lw[:, :], AF.Sigmoid)
                            kb.act(a_[:, :], a_[:, :], AF.Sigmoid)
                            yield
                            kb.ts(lw[:, :], lw[:, :], -math.exp(-0.5), ALU.mult)
                            kk = S("rw_kk", 256); sq = S("rw_sq", 256); sc = S("rw_sc", 32)
                            yield
                            kb.tt(kk[:, :], rk[:, 256:512], BCv("k_k"), ALU.mult)
                            kb.tt(sq[:, :], kk[:, :], kk[:, :], ALU.mult)
                            yield
                            kb.red(sc[:, 0:4], r3(sq, 4))
                            yield
                            rsqrt_small(sc[:, 4:8], sc[:, 0:4], 1.0, EPS)
                            yield
                            kb.tt(r3(kk, 4), r3(kk, 4), bc_t(sc, 4, 4, 64), ALU.mult)
                            km = S("rw_km", 256)
                            yield
                            kb.stt(km[:, :], a_[:, :], -1.0, BCv("k_a"), ALU.add, ALU.mult)
                            yield
                            kb.stt(km[:, :], km[:, :], 1.0, rk[:, 256:512], ALU.add, ALU.mult)
                            yield
                            bb = S("rw_b", 256)
                            kb.tt(bb[:, :], kk[:, :], a_[:, :], ALU.mult)
                            yield
                            kb.tt(sq[:, :], rk[:, 0:256], km[:, :], ALU.mult)
                            yield
                            kb.tt(sq[:, :], sq[:, :], BCv("r_k"), ALU.mult)
                            kb.red(sc[:, 8:12], r3(sq, 4))
                            yield
                            pc1 = bank(); pc2 = bank()
                            kb.mm(pc1[:, 0:256], C("mcum"), lw[:, :])
                            kb.mm(pc1[:, 256:512], C("mcumx"), lw[:, :])
                            kb.mm(pc2[:, 0:256], C("mrest"), lw[:, :])
                            for h in range(4):
                                pb_, cb = 64 * (h % 2), h // 2
                                kb.mm(pc2[pb_:pb_ + 64, 256 + 2 * cb: 258 + 2 * cb], lw[:, h * 64:(h + 1) * 64],
                                      C("chunksel", 2))
                            dS = S("rw_dS", 8)
                            kb.act(dS[:, 0:4], pc2[:, 256:260], AF.Exp)
                            eg = S("rw_eg"); er = S("rw_er", 256); eng_ = S("rw_eng", 256)
                            kb.act(eg[:, :], pc1[:, :], AF.Exp)
                            kb.act(eng_[:, :], pc1[:, 0:256], AF.Exp, scale=-1.0)
                            kb.act(er[:, :], pc2[:, 0:256], AF.Exp)
                            TA = S("rw_TA", 512, BF16); TB = S("rw_TB", 512, BF16)
                            yield
                            kb.tt(TA[:, 0:256], rk[:, 0:256], eg[:, 0:256], ALU.mult)
                            kb.tt(TA[:, 256:512], km[:, :], eng_[:, :], ALU.mult)
                            yield
                            kb.tt(TB[:, 0:256], bb[:, :], eng_[:, :], ALU.mult)
                            kb.tt(TB[:, 256:512], kk[:, :], eg[:, 256:512], ALU.mult)
                            yield
                            kend2 = S("rw_kend2", 256, BF16); kend = S("rw_kend", 256, BF16)
                            yield
                            kb.tt(kend2[:, :], km[:, :], er[:, :], ALU.mult)
                            kb.stt(kend[:, :], bb[:, :], -1.0, er[:, :], ALU.mult, ALU.mult)
                            yield
                            FA = S("rw_FA", 512, BF16); FB = S("rw_FB", 512, BF16)
                            pfa = bank()
                            for q in range(4):
                                kb.tr(pfa[:, q * 128:(q + 1) * 128], TA[:, q * 128:(q + 1) * 128], ident)
                            kb.evac(FA[:, :], pfa[:, :])
                            yield
                            pfb = bank()
                            for q in range(4):
                                kb.tr(pfb[:, q * 128:(q + 1) * 128], TB[:, q * 128:(q + 1) * 128], ident)
                            kb.evac(FB[:, :], pfb[:, :])
                            yield
                            Nm = S("rw_N", 512, BF16); AbkT = S("rw_Abk", 512, BF16); attn2T = S("rw_attn2", 512, BF16); attnT = S("rw_attn", 512, BF16)
                            pn = bank(); pbk = bank()
                            for h in HO:
                                hs = slice(h * 128, (h + 1) * 128)
                                KT = fmh(FA, h, 0, 128, 256)
                                BT = fmh(FB, h, 0, 128, 0); KKT = fmh(FB, h, 0, 128, 256)
                                kb.mm(pn[:, hs], BT, KKT)
                                kb.mm(pbk[:, hs], KT, KKT)
                            kb.tt(r3(Nm, 4), r3(pn, 4), Cb4("m_strict"), ALU.mult)
                            kb.tt(r3(AbkT, 4), r3(pbk, 4), Cb4("m_strict"), ALU.mult)
                            yield
                            prk2 = bank(); prb = bank()
                            for h in HO:
                                hs = slice(h * 128, (h + 1) * 128)
                                RT = fmh(FA, h, 0, 128, 0); KT = fmh(FA, h, 0, 128, 256)
                                BT = fmh(FB, h, 0, 128, 0)
                                kb.mm(prk2[:, hs], KT, RT)
                                kb.mm(prb[:, hs], BT, RT)
                            kb.tt(r3(attn2T, 4), r3(prk2, 4), Cb4("m_incl"), ALU.mult)
                            kb.tt(r3(attnT, 4), r3(prb, 4), Cb4("m_incl_neg"), ALU.mult)
                            yield
                            UW = S("rw_UW", 512, BF16)
                            pu = bank()
                            for h in range(4):
                                kb.mm(pu[:, h * 64:(h + 1) * 64], AbkT[:, h * 128:(h + 1) * 128], vtm[:, h * 64:(h + 1) * 64])
                            kb.evac(Uv(UW), r3(pu, 4, 0, 256))
                            kb.copy("v", Wv(UW), r3(TB, 4, 256, 512))
                            yield
                            yield from solve(Nm, UW, "rw_")
                            pwt = bank()
                            for h in range(4):
                                pb_, cb = 64 * (h % 2), h // 2
                                kb.tr(pwt[pb_:pb_ + 64, cb * 128:(cb + 1) * 128], UW[:, h * 128 + 64:(h + 1) * 128], ident)
                            WT = S("rw_WT", 256, BF16)
                            kb.evac(WT[:, :], pwt[:, 0:256])
                            yield
                            po = banks[3]
                            yield from seq_core("rw_", po, st["rwkv"], stb["rwkv"], UW, +1, WT, 0, FA, 0, attnT, kend, dS, attn2T, kend2, vtm)
                            osb = S("rw_o", 256); cen = S("rw_cen", 256)
                            kb.evac(osb[:, :], po[:, 0:256])
                            yield
                            kb.red(sc[:, 12:16], r3(osb, 4))
                            yield
                            kb.ts(sc[:, 12:16], sc[:, 12:16], 1.0 / 64, ALU.mult)
                            yield
                            kb.tt(r3(cen, 4), r3(osb, 4), bc_t(sc, 12, 4, 64), ALU.subtract)
                            yield
                            kb.tt(sq[:, :], cen[:, :], cen[:, :], ALU.mult)
                            yield
                            kb.red(sc[:, 16:20], r3(sq, 4))
                            yield
                            rsqrt_small(sc[:, 20:24], sc[:, 16:20], 1.0 / 64, 64e-5)
                            yield
                            kb.tt(r3(cen, 4), r3(cen, 4), bc_t(sc, 20, 4, 64), ALU.mult)
                            yield
                            kb.tt(cen[:, :], cen[:, :], BCv("ln_w"), ALU.mult)
                            yield
                            kb.tt(cen[:, :], cen[:, :], BCv("ln_b"), ALU.add)
                            yield
                            kb.tt(r3(sq, 4), r3(vtm, 4), bc_t(sc, 8, 4, 64), ALU.mult)
                            yield
                            ob = S("rw_hn_ob", 256, BF16)
                            kb.tt(ob[:, :], cen[:, :], sq[:, :], ALU.add)
                            to_fm_gate(ob, 2, tsl, ZS_r)
                            yield


                    proj_done = {}

                    def gen_P():
                        if "gdn" in mixers:
                            yield from project("gdn", P_g, MT + 4, 4)
                        else:
                            kb.memset(yT[:, 0:2 * MT], 0.0)
                        proj_done["gdn"] = True
                        if "rwkv" in mixers:
                            yield from project("rwkv", P_r, MT + 4, 4)
                        else:
                            kb.memset(yT[:, 2 * MT:4 * MT], 0.0)
                        proj_done["rwkv"] = True
                    run_streams([gen_H1()] + ([pendingT[0]] if pendingT[0] is not None else []))
                    pendingT[0] = None
                    active = [gen_P()]
                    if "sc" in mixers:
                        active.append(gen_sc())
                    if "gla" in mixers:
                        active.append(gen_gla())
                    pending = {}
                    if "gdn" in mixers:
                        pending["gdn"] = gen_gdn
                    if "rwkv" in mixers:
                        pending["rwkv"] = gen_rwkv
                    clock = {g_: 0.0 for g_ in active}
                    while active or pending:
                        g_ = min(active, key=lambda x: clock[x])
                        if not step_stream(clock, g_):
                            active.remove(g_)
                            del clock[g_]
                        for m_ in list(pending):
                            if proj_done.get(m_):
                                gn_ = pending.pop(m_)()
                                active.append(gn_)
                                clock[gn_] = (min(clock.values()) if clock else 0.0) + (GOFF if m_ == "gdn" else 0.0)
                    for m_, (y0, y1) in (("sc", (4, 6)), ("gla", (6, 8))):
                        if m_ not in mixers:
                            kb.memset(yT[:, y0 * MT:y1 * MT], 0.0)
                    def gen_T(tok0=tok0, src_d=src_d, dst_d=dst_d):
                        for j in range(MT // 128):
                            r0 = tok0 + j * 128
                            kb.dma("sp", xres[:], src_d[r0:r0 + 128, :])
                            osb = S("op_o", 1024)
                            for half in range(2):
                                pb = pbank()
                                for kc in range(8):
                                    kb.mm(pb[:, :], yT[:, kc * MT + j * 128: kc * MT + (j + 1) * 128],
                                          wout[:, kc * 1024 + half * 512: kc * 1024 + (half + 1) * 512],
                                          start=(kc == 0), stop=(kc == 7))
                                kb.evac(osb[:, half * 512:(half + 1) * 512], pb[:, :])
                                yield
                            ss = S("t_ss", 8)
                            junk = P_r
                            kb.act(junk[:, 0:1024], osb[:, :], AF.Square)
                            kb.red(ss[:, 2:3], junk[:, 0:1024])
                            rsqrt_small(ss[:, 3:4], ss[:, 2:3], 1.0 / D, EPS)
                            yield
                            kb.stt(osb[:, :], osb[:, :], ss[:, 3:4], BCv("post_w"), ALU.mult, ALU.mult)
                            kb.tt(osb[:, :], osb[:, :], xres[:, :], ALU.add)
                            kb.dma("sp", dst_d[r0:r0 + 128, :], osb[:])
                            yield
                    pendingT[0] = gen_T()
            if pendingT[0] is not None:
                run_streams([pendingT[0]])
                pendingT[0] = None
            if l < L - 1:
                kb.wait_all("g")
                kb.op("g", lambda o: o.memset(P_r.h[:, 0:1], 0.0), [], [P_r[:, 0:1]])
                e = kb.engs["g"]
                kb.dq_eng["sp"].obj.wait_ge(e.sem, e.cnt)
                kb.dq_eng["sp"].waited["g"] = e.cnt
        kb.wait_all("g")
    return nc


_CACHE = {}


def run(inputs, NB, T, L, ncores, mixers=("gdn", "rwkv", "sc", "gla")):
    key = (NB, T, L, tuple(mixers))
    if key not in _CACHE:
        _CACHE[key] = build(NB, T, L, mixers)
    nc = _CACHE[key]
    pl = prep_layer_inputs(inputs, L)
    x = np.ascontiguousarray(inputs["x"], dtype=np.float32)
    in_maps = []
    for c in range(ncores):
        m = {"x": x[c * NB:(c + 1) * NB].reshape(NB * T, D), "consts": CONSTS}
        m.update(pl)
        in_maps.append(m)
    res = run_bass_kernel_spmd(nc, in_maps, core_ids=list(range(ncores)))
    outs = [r["out"].reshape(NB, T, D) for r in res.results]
    return np.concatenate(outs, axis=0)


def kernel(**inputs):
    inputs = {k: np.asarray(v) for k, v in inputs.items()}
    B, T, _ = inputs["x"].shape
    L = inputs["w_in"].shape[0]
    return run(inputs, B // NCORES, T, L, NCORES).astype(np.float32)
```
